# Optimizing a Trainium2 kernel written in Bass

```python
import math
import jax, jax.numpy as jnp
from jax import lax
import numpy as np

D_MODEL = 1024
BATCH = 4
SEQ = 8192
DEPTH = 1

ROPE_THETA = 10000.0
RMS_EPS = 1e-6
NEG_INF = -1e30

MLA_HEADS = 8
MLA_Q_RANK = 256
MLA_KV_RANK = 128
MLA_NOPE_DIM = 64
MLA_ROPE_DIM = 32
MLA_V_DIM = 64
MLA_QK_DIM = MLA_NOPE_DIM + MLA_ROPE_DIM
MLA_WIDTH = MLA_HEADS * MLA_V_DIM
Q_BLOCK = 128

DIL_CONFIGS = ((128, 1), (512, 4), (2048, 16))
DIL_GROUPS = len(DIL_CONFIGS)
DIL_HEADS_PER_GROUP = 8
DIL_HEAD_DIM = 64
DIL_HEADS = DIL_GROUPS * DIL_HEADS_PER_GROUP
DIL_QKV_WIDTH = DIL_HEADS * DIL_HEAD_DIM
DIL_WIDTH = DIL_HEADS_PER_GROUP * DIL_HEAD_DIM

IN_SPLITS = (MLA_Q_RANK, MLA_KV_RANK, MLA_ROPE_DIM, MLA_WIDTH,
             DIL_QKV_WIDTH, DIL_QKV_WIDTH, DIL_QKV_WIDTH, DIL_WIDTH,
             D_MODEL, D_MODEL)
IN_WIDTH = sum(IN_SPLITS)

kernel_name = "hybrid_mla_dilated_gated_encoder"


def rmsnorm(x, g):
    xf = x.astype(jnp.float32)
    xf = xf * lax.rsqrt(jnp.mean(xf * xf, axis=-1, keepdims=True) + RMS_EPS)
    return xf.astype(x.dtype) * g


def rope(t, positions):
    dim = t.shape[-1]
    inv_freq = ROPE_THETA ** (-jnp.arange(0, dim, 2, dtype=jnp.float32) / dim)
    ang = positions.astype(jnp.float32)[..., None] * inv_freq
    ang = jnp.concatenate([ang, ang], axis=-1)[:, :, None, :]
    cos, sin = jnp.cos(ang).astype(t.dtype), jnp.sin(ang).astype(t.dtype)
    t1, t2 = t[..., : dim // 2], t[..., dim // 2:]
    rot = jnp.concatenate([-t2, t1], axis=-1)
    return t * cos + rot * sin


def mla_attention(c_q, c_kv, k_rope, positions, q_norm_g, kv_norm_g, w_uq, w_ukv):
    B, S, _ = c_q.shape
    q = (rmsnorm(c_q, q_norm_g) @ w_uq).reshape(B, S, MLA_HEADS, MLA_QK_DIM)
    q = jnp.concatenate([q[..., :MLA_NOPE_DIM], rope(q[..., MLA_NOPE_DIM:], positions)], axis=-1)
    q = q * (MLA_QK_DIM ** -0.5)
    kv = (rmsnorm(c_kv, kv_norm_g) @ w_ukv).reshape(B, S, MLA_HEADS, MLA_NOPE_DIM + MLA_V_DIM)
    k_nope, v = kv[..., :MLA_NOPE_DIM], kv[..., MLA_NOPE_DIM:]
    k_pe = rope(k_rope[:, :, None, :], positions)
    k = jnp.concatenate([k_nope, jnp.broadcast_to(k_pe, (B, S, MLA_HEADS, MLA_ROPE_DIM))], axis=-1)
    n_blk = S // Q_BLOCK
    qb = q.reshape(B, n_blk, Q_BLOCK, MLA_HEADS, MLA_QK_DIM).transpose(1, 0, 2, 3, 4)

    def one_block(q_blk):
        s = jnp.einsum('bqhd,bkhd->bhqk', q_blk, k).astype(jnp.float32)
        p = jax.nn.softmax(s, axis=-1).astype(v.dtype)
        return jnp.einsum('bhqk,bkhd->bqhd', p, v)

    out = lax.map(one_block, qb)
    return out.transpose(1, 0, 2, 3, 4).reshape(B, S, MLA_WIDTH)


def dilated_group(q, k, v, dilation, side):
    B, S, H, D = q.shape
    L = S // dilation
    nb = -(-L // side)
    Lp = nb * side

    def to_sub(t):
        t = t.reshape(B, L, dilation, H, D).transpose(0, 2, 3, 1, 4)
        return jnp.pad(t, ((0, 0), (0, 0), (0, 0), (0, Lp - L), (0, 0)))

    def neighbourhood(t):
        tp = jnp.pad(t, ((0, 0), (0, 0), (0, 0), (side, side), (0, 0)))
        tb = tp.reshape(B, dilation, H, nb + 2, side, D)
        return jnp.concatenate([tb[:, :, :, :-2], tb[:, :, :, 1:-1], tb[:, :, :, 2:]], axis=4)

    qb = to_sub(q).reshape(B, dilation, H, nb, side, D)
    kb = neighbourhood(to_sub(k))
    vb = neighbourhood(to_sub(v))
    a = jnp.arange(side)[:, None]
    c = jnp.arange(3 * side)[None, :]
    key_pos = (jnp.arange(nb)[:, None, None] - 1) * side + c[None]
    valid = (c >= a)[None] & (c <= a + 2 * side)[None] & (key_pos >= 0) & (key_pos < L)
    s = jnp.einsum('brhnqd,brhnkd->brhnqk', qb, kb).astype(jnp.float32) * (D ** -0.5)
    s = jnp.where(valid, s, NEG_INF)
    m = jnp.max(s, axis=-1, keepdims=True)
    p = jnp.exp(s - m)
    den = jnp.sum(p, axis=-1)
    o = jnp.einsum('brhnqk,brhnkd->brhnqd', p.astype(v.dtype), vb).astype(jnp.float32) / den[..., None]
    lse = m[..., 0] + jnp.log(den)
    o = o.reshape(B, dilation, H, Lp, D)[:, :, :, :L].transpose(0, 3, 1, 2, 4).reshape(B, S, H, D)
    lse = lse.reshape(B, dilation, H, Lp)[:, :, :, :L].transpose(0, 3, 1, 2).reshape(B, S, H)
    return o, lse


def dilated_mixture(q, k, v, positions):
    B, S, _ = q.shape
    q = rope(q.reshape(B, S, DIL_HEADS, DIL_HEAD_DIM), positions)
    k = rope(k.reshape(B, S, DIL_HEADS, DIL_HEAD_DIM), positions)
    v = v.reshape(B, S, DIL_HEADS, DIL_HEAD_DIM)
    outs, lses = [], []
    for g, (window, dilation) in enumerate(DIL_CONFIGS):
        sl = slice(g * DIL_HEADS_PER_GROUP, (g + 1) * DIL_HEADS_PER_GROUP)
        o, lse = dilated_group(q[:, :, sl], k[:, :, sl], v[:, :, sl], dilation, window // (2 * dilation))
        outs.append(o)
        lses.append(lse)
    outs = jnp.stack(outs, axis=0)
    alpha = jax.nn.softmax(jnp.stack(lses, axis=0), axis=0)
    out = jnp.sum(alpha[..., None] * outs, axis=0)
    return out.reshape(B, S, DIL_WIDTH).astype(q.dtype)


def hybrid_layer(x, positions, attn_norm_g, w_in, b_gate, mla_q_norm_g, mla_kv_norm_g,
                 w_uq, w_ukv, w_o_mla, w_o_dil, w_out):
    h = rmsnorm(x, attn_norm_g)
    proj = h @ w_in
    split_points = [int(p) for p in np.cumsum(IN_SPLITS)[:-1]]
    (c_q, c_kv, k_rope, z_mla, q_dil, k_dil, v_dil, z_dil,
     g_mla, g_dil) = jnp.split(proj, split_points, axis=-1)
    a_out = mla_attention(c_q, c_kv, k_rope, positions, mla_q_norm_g, mla_kv_norm_g, w_uq, w_ukv)
    y_mla = (a_out * jax.nn.silu(z_mla)) @ w_o_mla
    d_out = dilated_mixture(q_dil, k_dil, v_dil, positions)
    y_dil = (d_out * jax.nn.silu(z_dil)) @ w_o_dil
    gate_mla = jax.nn.sigmoid(g_mla + b_gate[:D_MODEL])
    gate_dil = jax.nn.sigmoid(g_dil + b_gate[D_MODEL:])
    merged = gate_mla * y_mla + gate_dil * y_dil
    return x + merged @ w_out


def setup_inputs(seed: int = 0) -> dict:
    key = jax.random.key(seed)
    ks = jax.random.split(key, 14)
    f32 = jnp.float32

    def normal(k, shape, fan_in):
        return jax.random.normal(k, shape, f32) * fan_in ** -0.5

    def gain(k, shape):
        return 1.0 + 0.02 * jax.random.normal(k, shape, f32)

    x = jax.random.normal(ks[0], (BATCH, SEQ, D_MODEL), f32)
    positions = jnp.broadcast_to(jnp.arange(SEQ, dtype=jnp.int32)[None, :], (BATCH, SEQ))
    return {
        "x": x,
        "positions": positions,
        "attn_norm_g": gain(ks[1], (DEPTH, D_MODEL)),
        "w_in": normal(ks[2], (DEPTH, D_MODEL, IN_WIDTH), D_MODEL),
        "b_gate": 0.02 * jax.random.normal(ks[3], (DEPTH, 2 * D_MODEL), f32),
        "mla_q_norm_g": gain(ks[4], (DEPTH, MLA_Q_RANK)),
        "mla_kv_norm_g": gain(ks[5], (DEPTH, MLA_KV_RANK)),
        "w_uq": normal(ks[6], (DEPTH, MLA_Q_RANK, MLA_HEADS * MLA_QK_DIM), MLA_Q_RANK),
        "w_ukv": normal(ks[7], (DEPTH, MLA_KV_RANK, MLA_HEADS * (MLA_NOPE_DIM + MLA_V_DIM)), MLA_KV_RANK),
        "w_o_mla": normal(ks[8], (DEPTH, MLA_WIDTH, D_MODEL), MLA_WIDTH),
        "w_o_dil": normal(ks[9], (DEPTH, DIL_WIDTH, D_MODEL), DIL_WIDTH),
        "w_out": normal(ks[10], (DEPTH, D_MODEL, D_MODEL), D_MODEL),
        "final_norm_g": gain(ks[11], (D_MODEL,)),
    }


def reference(x, positions, attn_norm_g, w_in, b_gate, mla_q_norm_g, mla_kv_norm_g,
              w_uq, w_ukv, w_o_mla, w_o_dil, w_out, final_norm_g):
    h = x
    for layer in range(DEPTH):
        h = hybrid_layer(h, positions, attn_norm_g[layer], w_in[layer], b_gate[layer],
                         mla_q_norm_g[layer], mla_kv_norm_g[layer], w_uq[layer], w_ukv[layer],
                         w_o_mla[layer], w_o_dil[layer], w_out[layer])
    return rmsnorm(h, final_norm_g)
```

```python
import math
from contextlib import ExitStack

import numpy as np
import ml_dtypes
import concourse.bass as bass
import concourse.mybir as mybir
from concourse.bass_utils import run_bass_kernel_spmd

F32 = mybir.dt.float32
BF16 = mybir.dt.bfloat16
I32 = mybir.dt.int32
AF = mybir.ActivationFunctionType
ALU = mybir.AluOpType

S = 8192
D = 1024
OWN0 = 1024
NOWN = 4096
NLOC = 6144
EPS = 1e-6
TWO_PI_S = 6.28318
DIL = (1, 4, 16)
NDSEM = 48

C_CQ, C_CKV, C_KR, C_ZM, C_QD, C_KD, C_VD, C_ZD, C_GM, C_GD = 0, 256, 384, 416, 928, 2464, 4000, 5536, 6048, 7072
C_KR_ROT = 8096
WIN_EXT = 8128


def ssl(start, n, step):
    return slice(start, start + (n - 1) * step + 1, step)


class Reg:
    __slots__ = ("w", "r", "free_w")

    def __init__(self, free_w=False):
        self.w = {}
        self.r = {}
        self.free_w = free_w


class K:
    ENG = ("pe", "act", "dve", "pool", "sp")

    def __init__(self, nc, es):
        self.nc = nc
        self.streams = {e: [] for e in self.ENG}
        self.esem = {e: es.enter_context(nc.semaphore("sem_" + e)) for e in self.ENG}
        self.ecnt = {e: 0 for e in self.ENG}
        self.waited = {e: {} for e in self.ENG}
        self.dsem = [es.enter_context(nc.semaphore("dsem%d" % i)) for i in range(NDSEM)]
        self.dcnt = [0] * NDSEM
        self.dnext = {"sp": 0, "pool": NDSEM // 2}
        self.semobj = {}
        for e in self.ENG:
            self.semobj[id(self.esem[e])] = self.esem[e]
        for s in self.dsem:
            self.semobj[id(s)] = s

    def _deps(self, eng, reads, writes, extra=()):
        need = {}
        for r in reads:
            for k, v in r.w.items():
                need[k] = max(need.get(k, 0), v)
        for r in writes:
            if r.free_w:
                continue
            for k, v in r.w.items():
                need[k] = max(need.get(k, 0), v)
            for k, v in r.r.items():
                need[k] = max(need.get(k, 0), v)
        for k, v in extra:
            need[k] = max(need.get(k, 0), v)
        out = []
        wd = self.waited[eng]
        own = id(self.esem[eng])
        for k, v in need.items():
            if eng == "pe" and k == own:
                continue
            if wd.get(k, 0) >= v:
                continue
            wd[k] = v
            out.append((self.semobj[k], v))
        return out

    def _mark(self, key, val, reads, writes):
        for r in reads:
            r.r[key] = max(r.r.get(key, 0), val)
        for r in writes:
            r.w[key] = max(r.w.get(key, 0), val)

    def op(self, eng, fn, reads=(), writes=()):
        waits = self._deps(eng, reads, writes)
        sem = self.esem[eng]
        self.ecnt[eng] += 1
        val = self.ecnt[eng]

        def emit(e, fn=fn, waits=waits, sem=sem):
            for s, v in waits:
                e.wait_ge(s, v)
            fn(e).then_inc(sem, 1)

        self.streams[eng].append(emit)
        self._mark(id(sem), val, reads, writes)

    def dma(self, q, out, in_, reads=(), writes=()):
        i = self.dnext[q]
        half = NDSEM // 2
        base = 0 if q == "sp" else half
        self.dnext[q] = base + (i - base + 1) % half
        sem = self.dsem[i]
        prev = self.dcnt[i]
        self.dcnt[i] += 16
        val = self.dcnt[i]
        extra = [(id(sem), prev)] if prev else []
        waits = self._deps(q, reads, writes, extra)

        def emit(e, waits=waits, sem=sem, out=out, in_=in_):
            for s, v in waits:
                e.wait_ge(s, v)
            e.dma_start(out=out, in_=in_).then_inc(sem, 16)

        self.streams[q].append(emit)
        self._mark(id(sem), val, reads, writes)

    def barrier(self):
        tgt = [(id(self.esem[e]), self.ecnt[e]) for e in self.ENG if self.ecnt[e]]
        tgt += [(id(self.dsem[i]), self.dcnt[i]) for i in range(NDSEM) if self.dcnt[i]]
        for eng in self.ENG:
            waits = self._deps(eng, (), (), tgt)

            def emit(e, waits=waits):
                for s, v in waits:
                    e.wait_ge(s, v)

            self.streams[eng].append(emit)

    def flush(self):
        self.barrier()
        streams = self.streams
        self.streams = {e: [] for e in self.ENG}
        with self.nc.Block() as blk:
            def run(name):
                def body(e):
                    for f in streams[name]:
                        f(e)
                return body
            blk.tensor(run("pe"))
            blk.scalar(run("act"))
            blk.vector(run("dve"))
            blk.gpsimd(run("pool"))
            blk.sync(run("sp"))

    def copy(self, eng, out, in_, R=(), W=()):
        self.op(eng, lambda e: e.tensor_copy(out=out, in_=in_), R, W)

    def memset(self, eng, ap, val, W=()):
        self.op(eng, lambda e: e.memset(ap, val), (), W)

    def ts(self, eng, out, in0, s1, s2, op0, op1, R=(), W=()):
        if op1 is None:
            self.op(eng, lambda e: e.tensor_scalar(out=out, in0=in0, scalar1=s1, scalar2=None, op0=op0), R, W)
        else:
            self.op(eng, lambda e: e.tensor_scalar(out=out, in0=in0, scalar1=s1, scalar2=s2, op0=op0, op1=op1), R, W)

    def tt(self, eng, out, in0, in1, op, R=(), W=()):
        self.op(eng, lambda e: e.tensor_tensor(out=out, in0=in0, in1=in1, op=op), R, W)

    def stt(self, eng, out, in0, scalar, in1, op0, op1, R=(), W=()):
        self.op(eng, lambda e: e.scalar_tensor_tensor(out=out, in0=in0, scalar=scalar, in1=in1, op0=op0, op1=op1), R, W)

    def recip(self, out, in_, R=(), W=()):
        self.op("dve", lambda e: e.reciprocal(out=out, in_=in_), R, W)

    def act(self, out, in_, func, R=(), W=(), bias=None, scale=None, accum=None):
        kw = {}
        if bias is not None:
            kw["bias"] = bias
        if scale is not None:
            kw["scale"] = scale
        if accum is not None:
            kw["accum_out"] = accum
        self.op("act", lambda e: e.activation(out=out, in_=in_, func=func, **kw), R, W)

    def mm(self, ps, pairs, R=(), W=()):
        def fn(e):
            n = len(pairs)
            ins = None
            for i, (l, r) in enumerate(pairs):
                ins = e.matmul(ps, lhsT=l, rhs=r, start=(i == 0), stop=(i == n - 1))
            return ins
        self.op("pe", fn, R, W)

    def mms(self, groups, R=(), W=()):
        def fn(e):
            ins = None
            for ps, pairs in groups:
                n = len(pairs)
                for i, (l, r) in enumerate(pairs):
                    ins = e.matmul(ps, lhsT=l, rhs=r, start=(i == 0), stop=(i == n - 1))
            return ins
        self.op("pe", fn, R, W)

    def trs(self, items, ident, R=(), W=()):
        def fn(e):
            ins = None
            for o, i in items:
                ins = e.transpose(o, i, ident)
            return ins
        self.op("pe", fn, R, W)


def build(stop_after=None, debug=False):
    nc = bass.Bass("TRN2", target_bir_lowering=False)
    kind_dbg = "ExternalOutput" if debug else "Internal"

    def din(name, shape, dt):
        return nc.dram_tensor(name, shape, dt, kind="ExternalInput").ap()

    def dscr(name, shape, dt=BF16):
        return nc.dram_tensor(name, shape, dt, kind=kind_dbg).ap()

    x_d = din("x", [S, D], F32)
    pos_d = din("pos", [1, S], I32)
    win_d = din("w_in", [D, WIN_EXT], F32)
    wuq_d = din("w_uq", [256, 2 * 768], F32)
    wukv_d = din("w_ukv", [128, 1024], F32)
    wom_d = din("w_o_mla", [512, D], F32)
    wod_d = din("w_o_dil", [512, D], F32)
    wout_d = din("w_out", [D, D], F32)
    vec_d = din("vecs", [128, 40], F32)
    fg_d = din("fg", [128, D], F32)
    ident_d = din("ident", [128, 128], F32)
    mask_d = din("masks", [128, 4 * 1024], BF16)
    perm_d = din("permd", [128, 128], BF16)
    out_d = nc.dram_tensor("out", [NOWN, D], F32, kind="ExternalOutput").ap()

    TAB = dscr("TAB", [4, 128, S], F32)
    KVN_d = dscr("KVN", [128, S])
    KPE_d = dscr("KPE", [32, S])
    CQN_d = dscr("CQN", [256, NOWN])
    SZM_d = dscr("SZM", [512, NOWN])
    SZD_d = dscr("SZD", [512, NOWN])
    QD_d = dscr("QD", [1536, NOWN])
    KD_d = dscr("KD", [1536, NLOC])
    VD_d = dscr("VD", [24, 128, 48 * 64])
    GM_d = dscr("GM", [1024, NOWN])
    GD_d = dscr("GD", [1024, NOWN])
    KM_d = dscr("KM", [512, S])
    VM_d = dscr("VM", [S, 512])
    QM_d = dscr("QM", [768, NOWN])
    GMT_d = dscr("GMT", [512, NOWN])
    GDT_d = dscr("GDT", [512, NOWN])
    RR_d = nc.dram_tensor("RRS", [2, 512], F32).ap()
    RD1_d = nc.dram_tensor("RD1", [1, NOWN], F32).ap()
    RD2_d = nc.dram_tensor("RD2", [1, NOWN], F32).ap()
    R_d = {n: Reg(free_w=True) for n in ("TAB", "KVN", "KPE", "CQN", "SZM", "SZD", "QD", "KD", "VD", "GM", "GD", "KM", "VM", "QM", "GMT", "GDT", "out")}

    phases = ["T", "H", "A", "M0", "MLA", "DIL", "F"]
    nph = len(phases) if stop_after is None else phases.index(stop_after) + 1
    run = set(phases[:nph])

    with ExitStack() as es:
        k = K(nc, es)

        def sb(stack, name, shape, dt):
            return stack.enter_context(nc.sbuf_tensor("s_" + name, shape, dt))

        vec = sb(es, "vec", [128, 40], F32)
        ident = sb(es, "ident", [128, 128], F32)
        ones = sb(es, "ones", [128, 128], F32)
        R_c = Reg()
        k.dma("sp", vec[:], vec_d[:, :], (), [R_c])
        k.dma("sp", ident[:], ident_d[:, :], (), [R_c])
        k.memset("dve", ones[:], 1.0, [R_c])
        P2 = [es.enter_context(nc.psum_tensor("p2_%d" % i, [128, 1024], F32)) for i in range(3)]
        P1 = [es.enter_context(nc.psum_tensor("p1_%d" % i, [128, 512], F32)) for i in range(2)]
        RP2 = [[Reg(), Reg()] for _ in range(3)]
        RP1 = [Reg(), Reg()]

        if "T" in run:
            with ExitStack() as ph:
                posi = sb(ph, "posi", [128, 2, 2048], I32)
                posf = sb(ph, "posf", [128, 2048], F32)
                tt_ = sb(ph, "tt", [128, 2048], F32)
                ti_ = sb(ph, "ti", [128, 2048], I32)
                tf_ = sb(ph, "tf", [128, 2048], F32)
                fr_ = sb(ph, "fr", [128, 2048], F32)
                so_ = sb(ph, "so", [128, 2, 2048], F32)
                Rposi = [Reg(), Reg()]
                Rposf, Rtt, Rti, Rtf, Rfr = Reg(), Reg(), Reg(), Reg(), Reg()
                Rso = [Reg(), Reg()]
                n = 0
                fa_ = sb(ph, "fa", [128, 2048], F32)
                Rfa = Reg()
                for c in range(4):
                    sl = slice(c * 2048, (c + 1) * 2048)
                    k.dma("sp", posi[:, c % 2, :], pos_d[0:1, sl].partition_broadcast(128), (), [Rposi[c % 2]])
                    k.copy("dve", posf[:], posi[:, c % 2, :], [Rposi[c % 2]], [Rposf])
                    k.ts("dve", tt_[:], posf[:], vec[:, 34:35], None, ALU.mult, None, [Rposf, R_c], [Rtt])
                    k.copy("dve", ti_[:], tt_[:], [Rtt], [Rti])
                    k.copy("dve", tf_[:], ti_[:], [Rti], [Rtf])
                    k.tt("dve", fr_[:], tt_[:], tf_[:], ALU.subtract, [Rtt, Rtf], [Rfr])
                    k.stt("dve", fa_[:], fr_[:], -1.0, fr_[:], ALU.mult, ALU.max, [Rfr], [Rfa])
                    for kind_ in range(2):
                        s = n % 2
                        n += 1
                        if kind_ == 0:
                            k.act(so_[:, s, :], fa_[:], AF.Sin, [Rfa], [Rso[s]], scale=-TWO_PI_S, bias=vec[:, 32:33])
                        else:
                            k.act(so_[:, s, :], fr_[:], AF.Sin, [Rfr, R_c], [Rso[s]], scale=vec[:, 35:36])
                        k.dma("pool", TAB[kind_, 0:64, sl], so_[0:64, s, :], [Rso[s]], [R_d["TAB"]])
                        k.dma("pool", TAB[kind_, 64:128, sl], so_[0:64, s, :], [Rso[s]], [R_d["TAB"]])
                        k.dma("pool", TAB[2 + kind_, 0:32, sl], so_[64:96, s, :], [Rso[s]], [R_d["TAB"]])
                k.flush()

        if "H" in run:
            with ExitStack() as phH:
                hT = sb(phH, "hT", [128, 8, NLOC], BF16)
                RhT = [[Reg() for _ in range(4)] for _ in range(12)]
                with ExitStack() as ph:
                    xt = sb(ph, "xt", [128, 4, D], F32)
                    junk = sb(ph, "junk", [128, D], BF16)
                    xn = sb(ph, "xn", [128, 2, D], F32)
                    st = sb(ph, "st", [128, 3, 4], F32)
                    hTo = sb(ph, "hTo", [128, 2, 8, 512], BF16)
                    gfull = sb(ph, "gfull", [128, 8, 128], F32)
                    wkvs = sb(ph, "wkvs", [128, 8, 192], F32)
                    wkv = sb(ph, "wkv", [128, 8, 192], BF16)
                    sq = sb(ph, "sq", [128, 512], F32)
                    ms = sb(ph, "ms", [128, 512], F32)
                    rs = sb(ph, "rs", [128, 512], F32)
                    tabm = sb(ph, "tabm", [32, 2, 2, 512], F32)
                    t1 = sb(ph, "t1", [32, 512], F32)
                    t2 = sb(ph, "t2", [32, 512], F32)
                    t2p = sb(ph, "t2p", [32, 512], F32)
                    Rt2p = Reg()
                    kvst = sb(ph, "kvst", [128, 2, 512], BF16)
                    kpst = sb(ph, "kpst", [32, 2, 512], BF16)
                    Rxt = [Reg() for _ in range(4)]
                    Rjunk, Rsq, Rms, Rrs, Rt1, Rt2, Rw = Reg(), Reg(), Reg(), Reg(), Reg(), Reg(), Reg()
                    Rxn = [Reg(), Reg()]
                    Rst = [Reg(), Reg(), Reg()]
                    RhTo = [[Reg() for _ in range(4)] for _ in range(2)]
                    Rgf = Reg()
                    for kk in range(8):
                        k.ts("pool", gfull[:, kk, :], ones[:], vec[:, kk:kk + 1], None, ALU.mult, None, [R_c], [Rgf])
                    Rtabm = [Reg(), Reg()]
                    Rkvst = [Reg(), Reg()]
                    Rkpst = [Reg(), Reg()]
                    wv_ = win_d.rearrange("(k p) n -> p k n", p=128)
                    k.dma("sp", wkvs[:, :, 0:160], wv_[:, :, C_CKV:C_CKV + 160], (), [Rw])
                    k.dma("sp", wkvs[:, :, 160:192], wv_[:, :, C_KR_ROT:C_KR_ROT + 32], (), [Rw])
                    k.copy("pool", wkv[:], wkvs[:], [Rw], [Rw])
                    def stage_a(i):
                        s4, s3 = i % 4, i % 3
                        k.dma("sp", xt[:, s4, :], x_d[i * 128:(i + 1) * 128, :], (), [Rxt[s4]])
                        k.act(junk[:], xt[:, s4, :], AF.Square, [Rxt[s4]], [Rjunk, Rst[s3]], accum=st[:, s3, 0:1])
                        k.act(st[:, s3, 1:2], st[:, s3, 0:1], AF.Ln, [Rst[s3]], [Rst[s3]], scale=1.0 / D, bias=vec[:, 33:34])

                    def stage_b(i):
                        s3 = i % 3
                        k.act(st[:, s3, 3:4], st[:, s3, 1:2], AF.Exp, [Rst[s3]], [Rst[s3]], scale=-0.5)

                    stage_a(0)
                    stage_a(1)
                    stage_b(0)
                    for i in range(64):
                        c, j = divmod(i, 4)
                        s4, s3, s2 = i % 4, i % 3, i % 2
                        if i + 2 < 64:
                            stage_a(i + 2)
                        k.act(xn[:, s2, :], xt[:, s4, :], AF.Copy, [Rxt[s4], Rst[s3]], [Rxn[s2]], scale=st[:, s3, 3:4])
                        pt = P2[s2]
                        k.trs([(pt[:, kk * 128:(kk + 1) * 128], xn[:, s2, kk * 128:(kk + 1) * 128]) for kk in range(8)],
                              ident[:], [Rxn[s2], R_c], RP2[s2])
                        if i + 1 < 64:
                            stage_b(i + 1)
                        if c < 12:
                            dst3 = hT[:, :, i * 128:(i + 1) * 128]
                            Rdst = RhT[c][j]
                        else:
                            dst3 = hTo[:, c % 2, :, j * 128:(j + 1) * 128]
                            Rdst = RhTo[c % 2][j]
                        k.tt("dve", dst3, pt[:, :].rearrange("p (a b) -> p a b", a=8), gfull[:], ALU.mult, RP2[s2] + [Rgf], [Rdst])
                        if j != 3:
                            continue
                        sl = slice(c * 512, (c + 1) * 512)
                        s = c % 2
                        if c < 12:
                            src = lambda kk: hT[:, kk, sl]
                        else:
                            src = lambda kk, s=s: hTo[:, s, kk, :]
                        Rsrc = RhT[c] if c < 12 else RhTo[s]
                        pc, pk, pss = P2[2], P1[0], P1[1]
                        k.mm(pc[:, 0:512], [(wkv[:, kk, 0:128], src(kk)) for kk in range(8)], [Rw] + Rsrc, [RP2[2][0]])
                        k.mm(pk[0:32, :], [(wkv[:, kk, 128:160], src(kk)) for kk in range(8)], [Rw] + Rsrc, [RP1[0]])
                        k.act(sq[:], pc[:, 0:512], AF.Square, [RP2[2][0]], [Rsq])
                        k.mm(pss[:], [(ones[:], sq[:])], [Rsq, R_c], [RP1[1]])
                        k.act(ms[:], pss[:], AF.Ln, [RP1[1], R_c], [Rms], scale=1.0 / 128, bias=vec[:, 33:34])
                        k.act(rs[:], ms[:], AF.Exp, [Rms], [Rrs], scale=-0.5)
                        k.stt("dve", kvst[:, s, :], pc[:, 0:512], vec[:, 26:27], rs[:], ALU.mult, ALU.mult,
                              [RP2[2][0], Rrs, R_c], [Rkvst[s]])
                        k.dma("pool", KVN_d[:, sl], kvst[:, s, :], [Rkvst[s]], [R_d["KVN"]])
                        k.dma("sp", tabm[:, s, 0, :], TAB[2, 0:32, sl], [R_d["TAB"]], [Rtabm[s]])
                        k.dma("sp", tabm[:, s, 1, :], TAB[3, 0:32, sl], [R_d["TAB"]], [Rtabm[s]])
                        k.tt("dve", t1[:], pk[0:32, :], tabm[:, s, 0, :], ALU.mult, [RP1[0], Rtabm[s]], [Rt1])
                        k.tt("dve", t2[:], pk[0:32, :], tabm[:, s, 1, :], ALU.mult, [RP1[0], Rtabm[s]], [Rt2])
                        k.dma("pool", t2p[0:16, :], t2[16:32, :], [Rt2], [Rt2p])
                        k.dma("pool", t2p[16:32, :], t2[0:16, :], [Rt2], [Rt2p])
                        k.tt("dve", kpst[:, s, :], t1[:], t2p[:], ALU.subtract, [Rt1, Rt2p], [Rkpst[s]])
                        k.dma("pool", KPE_d[:, sl], kpst[:, s, :], [Rkpst[s]], [R_d["KPE"]])
                    k.flush()

                if "A" in run:
                    with ExitStack() as ph:
                        wst = sb(ph, "wst", [128, 3, 8, 128], F32)
                        wt = sb(ph, "wt", [128, 2, 8, 8, 128], BF16)
                        tabd = sb(ph, "tabd", [128, 2, 2, 512], F32)
                        sqa = sb(ph, "sqa", [128, 2, 2, 512], F32)
                        msa = sb(ph, "msa", [128, 2, 512], F32)
                        rsa = sb(ph, "rsa", [128, 2, 512], F32)
                        ta = sb(ph, "ta", [128, 2, 2, 512], F32)
                        ost = sb(ph, "ost", [128, 4, 512], BF16)
                        Rwst = [Reg() for _ in range(3)]
                        Rwt = [Reg(), Reg()]
                        Rtabd = [Reg(), Reg()]
                        Rsqa, Rmsa, Rrsa = [[Reg(), Reg()], [Reg(), Reg()]], [Reg(), Reg()], [Reg(), Reg()]
                        Rta = [[Reg(), Reg()], [Reg(), Reg()]]
                        Rost = [Reg() for _ in range(4)]
                        wv_ = win_d.rearrange("(k p) n -> p k n", p=128)
                        own_chunks = list(range(2, 10))
                        groups = [("cq", [C_CQ, C_CQ + 128], [], own_chunks, CQN_d, OWN0, None)]
                        groups.append(("silu", [C_ZM + 128 * t for t in range(4)], [], own_chunks, SZM_d, OWN0, None))
                        for g in range(3):
                            groups.append(("rope", [C_QD + 512 * g + 128 * t for t in range(4)], [], own_chunks, QD_d, OWN0, 4 * g))
                        for g in range(3):
                            chs = list(range(1, 11)) if g < 2 else list(range(0, 12))
                            groups.append(("rope", [C_KD + 512 * g + 128 * t for t in range(4)], [], chs, KD_d, 0, 4 * g))
                        groups.append(("silu", [C_ZD + 128 * t for t in range(4)], [], own_chunks, SZD_d, OWN0, None))
                        groups.append(("sig", [C_GM + 128 * t for t in range(8)], [], own_chunks, GM_d, OWN0, 8))
                        groups.append(("sig", [C_GD + 128 * t for t in range(8)], [], own_chunks, GD_d, OWN0, 16))
                        no = 0
                        npz = 0
                        for g in range(3):
                            groups.append(("vd", [C_VD + 512 * g + 128 * t for t in range(4)], [], None, VD_d, 0, g))
                        nwc = [0]
                        nvs = [0]
                        cnt = [0]
                        qb = sb(ph, "qb", [128, 3, 512], BF16)
                        Rqb = [Reg() for _ in range(3)]
                        permd = sb(ph, "permd", [128, 128], BF16)
                        Rpermd = Reg()
                        k.dma("sp", permd[:], perm_d[:, :], (), [Rpermd])
                        VTB = 6
                        vst = sb(ph, "vst", [128, 2, 8, VTB, 64], BF16)
                        Rvst = [[Reg() for _ in range(VTB)] for _ in range(2)]

                        def load_group(gi):
                            if gi >= len(groups):
                                return
                            ws = gi % 2
                            for ti, c0 in enumerate(groups[gi][1] + groups[gi][2]):
                                s3 = nwc[0] % 3
                                nwc[0] += 1
                                k.dma("sp", wst[:, s3], wv_[:, :, c0:c0 + 128], (), [Rwst[s3]])
                                if ti % 2 == 0:
                                    k.act(wt[:, ws, ti], wst[:, s3], AF.Copy, [Rwst[s3]], [Rwt[ws]])
                                else:
                                    k.copy("dve", wt[:, ws, ti], wst[:, s3], [Rwst[s3]], [Rwt[ws]])

                        load_group(0)
                        for gi, (kind, cols, rcols, chs, dst, tok0, extra) in enumerate(groups):
                            ws = gi % 2
                            load_group(gi + 1)
                            nt = len(cols)
                            if kind == "vd":
                                g = extra
                                r_ = DIL[g]
                                ntr_ = NOWN // r_ // 128 + 1
                                li0_ = OWN0 // r_ - 64
                                ntile_ = r_ * ntr_
                                allh = [x for cc in RhT for x in cc]
                                for t0 in range(0, ntile_, VTB):
                                    nb = min(VTB, ntile_ - t0)
                                    sv = nvs[0] % 2
                                    nvs[0] += 1
                                    for tb in range(nb):
                                        res_, m_ = divmod(t0 + tb, ntr_)
                                        tok = ssl((li0_ + 128 * m_) * r_ + res_, 128, r_)
                                        pi = npz % 3
                                        npz += 1
                                        pa = P2[pi]
                                        k.mm(pa[:, 0:512].rearrange("p (a b) -> p a b", a=4),
                                             [(hT[:, kk, tok], wt[:, ws, 0:4, kk, :]) for kk in range(8)],
                                             [Rwt[ws]] + allh, [RP2[pi][0]])
                                        src3 = pa[:, 0:512].rearrange("p (h d) -> p h d", h=8)
                                        if tb % 2 == 0:
                                            k.act(vst[:, sv, :, tb, :], src3, AF.Copy, [RP2[pi][0]], [Rvst[sv][tb]])
                                        else:
                                            k.copy("dve", vst[:, sv, :, tb, :], src3, [RP2[pi][0]], [Rvst[sv][tb]])
                                    for h_ in range(8):
                                        dstv = VD_d[g * 8 + h_].rearrange("p (t d) -> p t d", d=64)[:, t0:t0 + nb, :]
                                        k.dma("pool" if h_ % 2 else "sp", dstv, vst[:, sv, h_, 0:nb, :], Rvst[sv][0:nb], [R_d["VD"]])
                                continue
                            if kind == "cq":
                                def cq_main(j, ws=ws, chs=chs):
                                    c = chs[j]
                                    sl = slice(c * 512, (c + 1) * 512)
                                    pi = j % 3
                                    pa = P2[pi]
                                    b_ = j % 2
                                    k.mms([(pa[:, 0:512], [(wt[:, ws, 0, kk, :], hT[:, kk, sl]) for kk in range(8)]),
                                           (pa[:, 512:1024], [(wt[:, ws, 1, kk, :], hT[:, kk, sl]) for kk in range(8)])],
                                          [Rwt[ws]] + RhT[c], RP2[pi])
                                    k.act(sqa[:, b_, 0, :], pa[:, 0:512], AF.Square, [RP2[pi][0]], [Rsqa[b_][0]])
                                    k.act(sqa[:, b_, 1, :], pa[:, 512:1024], AF.Square, [RP2[pi][1]], [Rsqa[b_][1]])

                                def cq_fin(j, chs=chs, dst=dst, tok0=tok0):
                                    c = chs[j]
                                    dsl = slice(c * 512 - tok0, (c + 1) * 512 - tok0)
                                    pi = j % 3
                                    pa = P2[pi]
                                    b_ = j % 2
                                    k.mm(P1[b_][:], [(ones[:], sqa[:, b_, 0, :]), (ones[:], sqa[:, b_, 1, :])], Rsqa[b_] + [R_c], [RP1[b_]])
                                    k.ts("dve", msa[:, b_, :], P1[b_][:], 1.0 / 256, EPS, ALU.mult, ALU.add, [RP1[b_]], [Rmsa[b_]])
                                    k.act(msa[:, b_, :], msa[:, b_, :], AF.Sqrt, [Rmsa[b_]], [Rmsa[b_]])
                                    k.recip(rsa[:, b_, :], msa[:, b_, :], [Rmsa[b_]], [Rrsa[b_]])
                                    for t in range(2):
                                        so = cnt[0] % 4
                                        cnt[0] += 1
                                        k.stt("dve", ost[:, so, :], pa[:, t * 512:(t + 1) * 512], vec[:, 24 + t:25 + t], rsa[:, b_, :],
                                              ALU.mult, ALU.mult, [RP2[pi][t], Rrsa[b_], R_c], [Rost[so]])
                                        k.dma("sp", dst[t * 128:(t + 1) * 128, dsl], ost[:, so, :], [Rost[so]], [R_d["CQN"]])

                                cq_main(0)
                                for j in range(len(chs)):
                                    if j + 1 < len(chs):
                                        cq_main(j + 1)
                                    cq_fin(j)
                                continue
                            if kind == "rope":
                                items = [(c, t) for c in chs for t in range(nt)]
                                info = {}
                                dname = "QD" if dst is QD_d else "KD"

                                def rope_main(j, items=items, info=info, ws=ws, nt=nt, chs_all=chs):
                                    c, t = items[j]
                                    sl = slice(c * 512, (c + 1) * 512)
                                    s = c % 2
                                    nxt = []
                                    if j == 0:
                                        nxt = [c]
                                    elif t == 1:
                                        nxt = [cc for cc in chs_all if cc > c][:1]
                                    if nxt:
                                        for cc in nxt:
                                            sl2 = slice(cc * 512, (cc + 1) * 512)
                                            k.dma("sp", tabd[:, cc % 2, 0, :], TAB[0, :, sl2], [R_d["TAB"]], [Rtabd[cc % 2]])
                                            k.dma("sp", tabd[:, cc % 2, 1, :], TAB[1, :, sl2], [R_d["TAB"]], [Rtabd[cc % 2]])
                                    pi = cnt[0] % 3
                                    sq_ = cnt[0] % 3
                                    so = cnt[0] % 4
                                    cnt[0] += 1
                                    info[j] = (pi, sq_, so)
                                    pa = P2[pi]
                                    k.mm(pa[:, 0:512], [(wt[:, ws, t, kk, :], hT[:, kk, sl]) for kk in range(8)],
                                         [Rwt[ws]] + RhT[c], [RP2[pi][0]])
                                    k.act(qb[:, sq_, :], pa[:, 0:512], AF.Copy, [RP2[pi][0]], [Rqb[sq_]])

                                def rope_fin(j, items=items, info=info, dst=dst, tok0=tok0, extra=extra, dname=dname):
                                    c, t = items[j]
                                    pi, sq_, so = info[j]
                                    s = c % 2
                                    pa = P2[pi]
                                    dsl = slice(c * 512 - tok0, (c + 1) * 512 - tok0)
                                    row0 = 512 * (extra // 4) + 128 * t
                                    k.mm(pa[:, 512:1024], [(permd[:], qb[:, sq_, :])], [Rqb[sq_], Rpermd], [RP2[pi][1]])
                                    b_ = j % 2
                                    k.tt("dve", ta[:, b_, 0, :], pa[:, 0:512], tabd[:, s, 0, :], ALU.mult, [RP2[pi][0], Rtabd[s]], [Rta[b_][0]])
                                    k.tt("dve", ta[:, b_, 1, :], pa[:, 512:1024], tabd[:, s, 1, :], ALU.mult, [RP2[pi][1], Rtabd[s]], [Rta[b_][1]])
                                    k.tt("pool", ost[:, so, :], ta[:, b_, 0, :], ta[:, b_, 1, :], ALU.add, Rta[b_], [Rost[so]])
                                    k.dma("sp", dst[row0:row0 + 128, dsl], ost[:, so, :], [Rost[so]], [R_d[dname]])

                                rope_main(0)
                                for j in range(len(items)):
                                    if j + 1 < len(items):
                                        rope_main(j + 1)
                                    rope_fin(j)
                                no += len(items)
                                npz += len(items)
                                continue
                            for c in chs:
                                sl = slice(c * 512, (c + 1) * 512)
                                dsl = slice(c * 512 - tok0, (c + 1) * 512 - tok0)
                                rhs = [hT[:, kk, sl] for kk in range(8)]
                                if kind == "rope":
                                    s = c % 2
                                    k.dma("sp", tabd[:, s, 0, :], TAB[0, :, sl], [R_d["TAB"]], [Rtabd[s]])
                                    k.dma("sp", tabd[:, s, 1, :], TAB[1, :, sl], [R_d["TAB"]], [Rtabd[s]])
                                if kind == "cq":
                                    raise AssertionError("cq handled by the pipelined branch")
                                for t in range(nt):
                                    so = no % 4
                                    no += 1
                                    pi = npz % 3
                                    npz += 1
                                    pa = P2[pi]
                                    row0 = (cols[t] - {"silu": cols[0], "sig": cols[0], "rope": cols[0]}[kind])
                                    if kind == "rope":
                                        row0 += 512 * (extra // 4)
                                    drow = slice(row0, row0 + 128)
                                    Rdst = [R_d[{id(SZM_d): "SZM", id(SZD_d): "SZD", id(QD_d): "QD", id(KD_d): "KD",
                                                 id(GM_d): "GM", id(GD_d): "GD"}[id(dst)]]]
                                    if kind == "rope":
                                        k.mms([(pa[:, 0:512], [(wt[:, ws, t, kk, :], rhs[kk]) for kk in range(8)]),
                                               (pa[:, 512:1024], [(wt[:, ws, nt + t, kk, :], rhs[kk]) for kk in range(8)])],
                                              [Rwt[ws]] + RhT[c], RP2[pi])
                                        s = c % 2
                                        raise AssertionError("rope handled by the pipelined branch")
                                    else:
                                        k.mm(pa[:, 0:512], [(wt[:, ws, t, kk, :], rhs[kk]) for kk in range(8)], [Rwt[ws]] + RhT[c], [RP2[pi][0]])
                                        if kind == "silu":
                                            k.act(ost[:, so, :], pa[:, 0:512], AF.Silu, [RP2[pi][0]], [Rost[so]])
                                        else:
                                            k.act(ost[:, so, :], pa[:, 0:512], AF.Sigmoid, [RP2[pi][0], R_c], [Rost[so]],
                                                  bias=vec[:, extra + t:extra + t + 1])
                                    k.dma("pool", dst[drow, dsl], ost[:, so, :], [Rost[so]], Rdst)
                        k.flush()

        if "M0" in run:
            with ExitStack() as ph:
                kvn = sb(ph, "kvn", [128, S], BF16)
                cqn = sb(ph, "cqn", [128, 2, NOWN], BF16)
                wks = sb(ph, "wks", [128, 1024], F32)
                wkb = sb(ph, "wkb", [128, 1024], BF16)
                wqs = sb(ph, "wqs", [128, 2, 1536], F32)
                wqb = sb(ph, "wqb", [128, 2, 1536], BF16)
                tabq = sb(ph, "tabq", [96, 2, 2, 512], F32)
                t1 = sb(ph, "t1q", [96, 2, 2, 512], F32)
                ost = sb(ph, "ostm", [128, 4, 512], BF16)
                Rkvn, Rcqn, Rwk, Rwq = Reg(), Reg(), Reg(), Reg()
                Rtabq = [Reg(), Reg()]
                Rt1 = [[Reg(), Reg()], [Reg(), Reg()]]
                Rost = [Reg() for _ in range(4)]
                k.dma("sp", kvn[:], KVN_d[:, :], [R_d["KVN"]], [Rkvn])
                k.dma("sp", cqn[:, 0, :], CQN_d[0:128, :], [R_d["CQN"]], [Rcqn])
                k.dma("sp", cqn[:, 1, :], CQN_d[128:256, :], [R_d["CQN"]], [Rcqn])
                k.dma("sp", wks[:], wukv_d[:, :], (), [Rwk])
                k.copy("pool", wkb[:], wks[:], [Rwk], [Rwk])
                k.dma("sp", wqs[:], wuq_d.rearrange("(k p) n -> p k n", p=128), (), [Rwq])
                k.copy("pool", wqb[:], wqs[:], [Rwq], [Rwq])
                for s in range(2):
                    k.memset("pool", tabq[0:64, s, 0, :], 1.0, [Rtabq[s]])
                    k.memset("pool", tabq[0:64, s, 1, :], 0.0, [Rtabq[s]])
                no = 0
                npz = 0
                for hp in range(4):
                    for c in range(16):
                        sl = slice(c * 512, (c + 1) * 512)
                        so = no % 4
                        no += 1
                        pi = npz % 3
                        npz += 1
                        k.mm(P2[pi][:, 0:512], [(wkb[:, hp * 128:(hp + 1) * 128], kvn[:, sl])], [Rwk, Rkvn], [RP2[pi][0]])
                        if c % 2 == 0:
                            k.act(ost[:, so, :], P2[pi][:, 0:512], AF.Copy, [RP2[pi][0]], [Rost[so]])
                        else:
                            k.copy("dve", ost[:, so, :], P2[pi][:, 0:512], [RP2[pi][0]], [Rost[so]])
                        k.dma("sp", KM_d[hp * 128:(hp + 1) * 128, sl], ost[:, so, :], [Rost[so]], [R_d["KM"]])
                for t in range(64):
                    so = no % 4
                    no += 1
                    pi = npz % 3
                    npz += 1
                    k.mm(P2[pi][:, 0:512], [(kvn[:, t * 128:(t + 1) * 128], wkb[:, 512:1024])], [Rwk, Rkvn], [RP2[pi][0]])
                    if t % 2 == 0:
                        k.act(ost[:, so, :], P2[pi][:, 0:512], AF.Copy, [RP2[pi][0]], [Rost[so]])
                    else:
                        k.copy("dve", ost[:, so, :], P2[pi][:, 0:512], [RP2[pi][0]], [Rost[so]])
                    k.dma("sp", VM_d[t * 128:(t + 1) * 128, :], ost[:, so, :], [Rost[so]], [R_d["VM"]])
                for c in range(8):
                    sl = slice(c * 512, (c + 1) * 512)
                    gsl = slice(OWN0 + c * 512, OWN0 + (c + 1) * 512)
                    s = c % 2
                    k.dma("sp", tabq[64:96, s, 0, :], TAB[2, 0:32, gsl], [R_d["TAB"]], [Rtabq[s]])
                    k.dma("sp", tabq[64:96, s, 1, :], TAB[3, 0:32, gsl], [R_d["TAB"]], [Rtabq[s]])
                    for h in range(8):
                        so = no % 4
                        no += 1
                        pi = npz % 3
                        npz += 1
                        pa = P2[pi]
                        k.mms([(pa[0:96, 0:512], [(wqb[:, kk, h * 96:(h + 1) * 96], cqn[:, kk, sl]) for kk in range(2)]),
                               (pa[0:96, 512:1024], [(wqb[:, kk, 768 + h * 96:768 + (h + 1) * 96], cqn[:, kk, sl]) for kk in range(2)])],
                              [Rwq, Rcqn], RP2[pi])
                        b_ = h % 2
                        k.tt("dve", t1[:, b_, 0, :], pa[0:96, 0:512], tabq[:, s, 0, :], ALU.mult, [RP2[pi][0], Rtabq[s]], [Rt1[b_][0]])
                        k.tt("dve", t1[:, b_, 1, :], pa[0:96, 512:1024], tabq[:, s, 1, :], ALU.mult, [RP2[pi][1], Rtabq[s]], [Rt1[b_][1]])
                        k.tt("pool", ost[0:96, so, :], t1[:, b_, 0, :], t1[:, b_, 1, :], ALU.add, Rt1[b_], [Rost[so]])
                        k.dma("sp", QM_d[h * 96:(h + 1) * 96, sl], ost[0:96, so, :], [Rost[so]], [R_d["QM"]])
                k.flush()

        if "MLA" in run:
            with ExitStack() as ph:
                Kt = sb(ph, "Kt", [96, 2, S], BF16)
                Va = sb(ph, "Va", [128, 2, 64, 65], BF16)
                qt = sb(ph, "qt", [96, 2, NOWN], BF16)
                szt = sb(ph, "szt", [64, 2, NOWN], BF16)
                Pt = sb(ph, "Pt", [128, 3, 1024], BF16)
                rrow = sb(ph, "rrow", [65, 2, 512], F32)
                bcs = sb(ph, "bcs", [64, 2, 512], F32)
                tn = sb(ph, "tn", [64, 512], F32)
                ost = sb(ph, "osta", [64, 2, 512], BF16)
                RK = [Reg(), Reg()]
                RV = [Reg(), Reg()]
                Rq = [Reg(), Reg()]
                Rsz = [Reg(), Reg()]
                RP = [Reg() for _ in range(3)]
                Rrrow, Rbcs, Rtn = [Reg(), Reg()], [Reg(), Reg()], Reg()
                Rrrd = [Reg(), Reg()]
                Rost = [Reg(), Reg()]
                for s in range(2):
                    k.memset("pool", Va[:, s, :, 64:65], 1.0, [RV[s]])
                VMv = VM_d.rearrange("(t p) c -> p t c", p=128)
                sc = 96.0 ** -0.5

                def mla_loads(h):
                    s = h % 2
                    k.dma("sp", Kt[0:64, s, :], KM_d[h * 64:(h + 1) * 64, :], [R_d["KM"]], [RK[s]])
                    k.dma("sp", Kt[64:96, s, :], KPE_d[:, :], [R_d["KPE"]], [RK[s]])
                    k.dma("sp", qt[:, s, :], QM_d[h * 96:(h + 1) * 96, :], [R_d["QM"]], [Rq[s]])
                    k.dma("sp", Va[:, s, :, 0:64], VMv[:, :, h * 64:(h + 1) * 64], [R_d["VM"]], [RV[s]])
                    k.dma("sp", szt[:, s, :], SZM_d[h * 64:(h + 1) * 64, :], [R_d["SZM"]], [Rsz[s]])

                steps = [(h, qc, kp) for h in range(8) for qc in range(8) for kp in range(32)]

                def mla_qk(i):
                    h, qc, kp = steps[i]
                    s, ss = h % 2, i % 3
                    qsl = slice(qc * 512, (qc + 1) * 512)
                    ps = P2[ss]
                    k.mms([(ps[:, 0:512], [(Kt[:, s, (2 * kp) * 128:(2 * kp + 1) * 128], qt[:, s, qsl])]),
                           (ps[:, 512:1024], [(Kt[:, s, (2 * kp + 1) * 128:(2 * kp + 2) * 128], qt[:, s, qsl])])],
                          [RK[s], Rq[s]], RP2[ss])

                def mla_exp_pv(i):
                    h, qc, kp = steps[i]
                    s, ss, sp_ = h % 2, i % 3, i % 3
                    acc = P1[qc % 2]
                    k.act(Pt[:, sp_, :], P2[ss][:, :], AF.Exp, RP2[ss], [RP[sp_]], scale=sc)

                    def pv(e):
                        e.matmul(acc[0:65, :], lhsT=Va[:, s, 2 * kp, :], rhs=Pt[:, sp_, 0:512], start=(kp == 0), stop=False)
                        return e.matmul(acc[0:65, :], lhsT=Va[:, s, 2 * kp + 1, :], rhs=Pt[:, sp_, 512:1024], start=False, stop=(kp == 31))
                    k.op("pe", pv, [RV[s], RP[sp_]], [RP1[qc % 2]])

                def mla_norm(h, qc):
                    s = h % 2
                    qsl = slice(qc * 512, (qc + 1) * 512)
                    acc = P1[qc % 2]
                    Racc = RP1[qc % 2]
                    so = (h * 8 + qc) % 2
                    k.recip(rrow[64:65, so, :], acc[64:65, :], [Racc], [Rrrow[so]])
                    k.dma("pool", RR_d[so:so + 1, :], rrow[64:65, so, :], [Rrrow[so]], [Rrrd[so]])
                    k.dma("pool", bcs[:, so, :], RR_d[so:so + 1, :].partition_broadcast(64), [Rrrd[so]], [Rbcs[so]])
                    k.tt("dve", tn[:], acc[0:64, :], bcs[:, so, :], ALU.mult, [Racc, Rbcs[so]], [Rtn])
                    k.tt("pool", ost[:, so, :], tn[:], szt[:, s, qsl], ALU.mult, [Rtn, Rsz[s]], [Rost[so]])
                    k.dma("pool", GMT_d[h * 64:(h + 1) * 64, qsl], ost[:, so, :], [Rost[so]], [R_d["GMT"]])

                mla_loads(0)
                pending = None
                for i, (h, qc, kp) in enumerate(steps):
                    if i == 0:
                        mla_qk(0)
                        mla_qk(1)
                    if i + 2 < len(steps):
                        mla_qk(i + 2)
                    mla_exp_pv(i)
                    if kp == 31:
                        pending = (h, qc)
                    elif kp == 1 and pending is not None:
                        mla_norm(*pending)
                        pending = None
                    if qc == 0 and kp == 2 and h + 1 < 8:
                        mla_loads(h + 1)
                mla_norm(*pending)
                k.flush()

        if "DIL" in run:
            with ExitStack() as ph:
                qd = sb(ph, "qd", [64, 2, NOWN], BF16)
                kd = sb(ph, "kd", [64, 2, NLOC], BF16)
                Vd = sb(ph, "Vd", [128, 2, 48, 65], BF16)
                Vdd = sb(ph, "Vdd", [128, 2, 48, 64], BF16)
                RVdd = [Reg(), Reg()]
                mask = sb(ph, "mask", [128, 4, 1024], BF16)
                Pd = sb(ph, "Pd", [128, 2, 1024], BF16)
                Pm = sb(ph, "Pm", [128, 2, 1024], BF16)
                nd = sb(ph, "nd", [65, 2, NOWN], F32)
                szd = sb(ph, "szd", [64, 2, NOWN], BF16)
                dsp = sb(ph, "dsp", [64, 64], F32)
                bcf = sb(ph, "bcf", [64, NOWN], F32)
                tn = sb(ph, "tnd", [64, 2, 512], F32)
                ost = sb(ph, "ostd", [64, 2, 512], BF16)
                Rqd, Rkd, RVd = [Reg(), Reg()], [Reg(), Reg()], [Reg(), Reg()]
                Rmask = Reg()
                RPd, RPm = [Reg(), Reg()], [Reg(), Reg()]
                Rnd = [Reg(), Reg()]
                Rszd = [Reg(), Reg()]
                Rdsp, Rbcf, Rtn = Reg(), Reg(), [Reg(), Reg()]
                Rrd1, Rrd2 = Reg(), Reg()
                Rost = [Reg(), Reg()]
                RPmh = [[Reg(), Reg()], [Reg(), Reg()]]
                k.dma("sp", mask[:], mask_d.rearrange("p (m c) -> p m c", m=4), (), [Rmask])
                for s in range(2):
                    k.memset("pool", Vd[:, s, :, 64:65], 1.0, [RVd[s]])
                units = [(hg, g) for hg in range(8) for g in range(3)]
                dsteps = [(u, ci) for u in range(len(units)) for ci in range(8)]

                def geom(g):
                    r = DIL[g]
                    nbr = NOWN // r // 128
                    return r, nbr, nbr + 1, OWN0 // r - 64

                def dil_loads(u):
                    hg, g = units[u]
                    s = u % 2
                    r, nbr, ntr, li0 = geom(g)
                    hd = g * 8 + hg
                    if g == 0:
                        k.dma("sp", szd[:, hg % 2, :], SZD_d[hg * 64:(hg + 1) * 64, :], [R_d["SZD"]], [Rszd[hg % 2]])
                    k.dma("sp", qd[:, s, :], QD_d[hd * 64:(hd + 1) * 64, :], [R_d["QD"]], [Rqd[s]])
                    if g < 2:
                        k.dma("sp", kd[:, s, 512:5632], KD_d[hd * 64:(hd + 1) * 64, 512:5632], [R_d["KD"]], [Rkd[s]])
                    else:
                        k.dma("sp", kd[:, s, :], KD_d[hd * 64:(hd + 1) * 64, :], [R_d["KD"]], [Rkd[s]])
                    nt_ = r * ntr
                    k.dma("sp", Vdd[:, s, 0:nt_, :], VD_d[hd].rearrange("p (t d) -> p t d", d=64)[:, 0:nt_, :], [R_d["VD"]], [RVdd[s]])

                def dil_pad(u):
                    hg, g = units[u]
                    s = u % 2
                    r, nbr, ntr, li0 = geom(g)
                    nt_ = r * ntr
                    k.act(Vd[:, s, 0:nt_, 0:64], Vdd[:, s, 0:nt_, :], AF.Copy, [RVdd[s]], [RVd[s]])

                def blocks_of(g, ci):
                    r, nbr, ntr, li0 = geom(g)
                    return [divmod(ci * 4 + bi, nbr) for bi in range(4)]

                def dil_qk(i):
                    u, ci = dsteps[i]
                    hg, g = units[u]
                    s, ss = u % 2, i % 3
                    r, nbr, ntr, li0 = geom(g)
                    ps = P2[ss]
                    grp = []
                    blks = blocks_of(g, ci)
                    merged = set()
                    for bi in (0, 2):
                        (r0, n0), (r1, n1) = blks[bi], blks[bi + 1]
                        if r0 == r1 and n1 == n0 + 1:
                            kl0 = (li0 + 128 * (n0 + 1)) * r + r0
                            ks = kd[:, s, ssl(kl0, 128, r)]
                            qs2 = qd[:, s, ssl((128 * n0) * r + r0, 256, r)]
                            col = (bi * 2 + 1) * 128
                            grp.append((ps[:, col:col + 256], [(ks, qs2)]))
                            merged.add((bi, 1))
                            merged.add((bi + 1, 0))
                    for bi, (res, n_) in enumerate(blks):
                        qo = (128 * n_) * r + res
                        qs = qd[:, s, ssl(qo, 128, r)]
                        for side in range(2):
                            if (bi, side) in merged:
                                continue
                            kl0 = (li0 + 128 * (n_ + side)) * r + res
                            ks = kd[:, s, ssl(kl0, 128, r)]
                            col = (bi * 2 + side) * 128
                            grp.append((ps[:, col:col + 128], [(ks, qs)]))
                    k.mms(grp, [Rqd[s], Rkd[s]], RP2[ss])

                def dil_e(i):
                    u, ci = dsteps[i]
                    hg, g = units[u]
                    ss = i % 2
                    s3_ = i % 3
                    k.act(Pd[:, ss, :], P2[s3_][:, :], AF.Exp, RP2[s3_], [RPd[ss]], scale=0.125)
                    if g == 0:
                        mk = 1 if ci == 0 else (2 if ci == 7 else 0)
                    elif g == 1:
                        mk = 1 if ci % 2 == 0 else 2
                    else:
                        mk = 3
                    k.tt("dve", Pm[:, ss, 0:512], Pd[:, ss, 0:512], mask[:, mk, 0:512], ALU.mult, [RPd[ss], Rmask], [RPmh[ss][0]])
                    k.tt("dve", Pm[:, ss, 512:1024], Pd[:, ss, 512:1024], mask[:, mk, 512:1024], ALU.mult, [RPd[ss], Rmask], [RPmh[ss][1]])

                def dil_p(i):
                    u, ci = dsteps[i]
                    hg, g = units[u]
                    s, ss, sn = u % 2, i % 2, hg % 2
                    r, nbr, ntr, li0 = geom(g)
                    blocks = blocks_of(g, ci)
                    acc = P1[ss]
                    grp = []
                    for bi, (res, n_) in enumerate(blocks):
                        pairs = []
                        for side in range(2):
                            col = (bi * 2 + side) * 128
                            pairs.append((Vd[:, s, res * ntr + n_ + side, :], Pm[:, ss, col:col + 128]))
                        grp.append((acc[0:65, bi * 128:(bi + 1) * 128], pairs))
                    k.mms(grp, [RVd[s]] + RPmh[ss], [RP1[ss]])
                    runs = []
                    if g < 2:
                        res, n0 = blocks[0]
                        o0 = (128 * n0) * r + res
                        runs.append((ssl(o0, 512, r), slice(0, 512)))
                    else:
                        for j in range(2):
                            res, n0 = blocks[2 * j]
                            runs.append((ssl(res, 256, r), slice(j * 256, (j + 1) * 256)))
                    for osl, asl in runs:
                        if g == 0:
                            k.copy("dve", nd[:, sn, osl], acc[0:65, asl], [RP1[ss]], [Rnd[sn]])
                        else:
                            k.tt("dve", nd[:, sn, osl], nd[:, sn, osl], acc[0:65, asl], ALU.add, [RP1[ss], Rnd[sn]], [Rnd[sn]])

                def dil_final_a(hg):
                    sn = hg % 2
                    k.dma("sp", RD1_d[0:1, :], nd[64:65, sn, :], [Rnd[sn]], [Rrd1])
                    k.dma("sp", dsp[:], RD1_d.rearrange("o (p f) -> (o p) f", p=64), [Rrd1], [Rdsp])
                    k.recip(dsp[:], dsp[:], [Rdsp], [Rdsp])
                    k.dma("sp", RD2_d.rearrange("o (p f) -> (o p) f", p=64), dsp[:], [Rdsp], [Rrd2])
                    k.dma("sp", bcf[:], RD2_d[0:1, :].partition_broadcast(64), [Rrd2], [Rbcf])

                def dil_final_b(hg, qc):
                    sn = hg % 2
                    qsl = slice(qc * 512, (qc + 1) * 512)
                    so = qc % 2
                    k.tt("dve", tn[:, so, :], nd[0:64, sn, qsl], bcf[:, qsl], ALU.mult, [Rnd[sn], Rbcf], [Rtn[so]])
                    k.tt("pool", ost[:, so, :], tn[:, so, :], szd[:, sn, qsl], ALU.mult, [Rtn[so], Rszd[sn]], [Rost[so]])
                    k.dma("pool", GDT_d[hg * 64:(hg + 1) * 64, qsl], ost[:, so, :], [Rost[so]], [R_d["GDT"]])

                dil_loads(0)
                dil_pad(0)
                pending = None
                fin_slots = {(0, 6): 0, (0, 7): 1, (1, 1): 2, (1, 2): 3, (1, 3): 4, (1, 5): 5, (1, 6): 6, (1, 7): 7}
                for i, (u, ci) in enumerate(dsteps):
                    hg, g = units[u]
                    if ci == 0 and u + 1 < len(units):
                        dil_loads(u + 1)
                    if ci == 4 and u + 1 < len(units):
                        dil_pad(u + 1)
                    if i == 0:
                        dil_qk(0)
                        dil_qk(1)
                        dil_e(0)
                    if i + 2 < len(dsteps):
                        dil_qk(i + 2)
                    if i + 1 < len(dsteps):
                        dil_e(i + 1)
                    dil_p(i)
                    if g == 2 and ci == 7:
                        pending = hg
                    elif g == 0 and ci == 1 and pending is not None:
                        dil_final_a(pending)
                    elif pending is not None and (g, ci) in fin_slots:
                        dil_final_b(pending, fin_slots[(g, ci)])
                        if fin_slots[(g, ci)] == 7:
                            pending = None
                dil_final_a(pending)
                for qc in range(8):
                    dil_final_b(pending, qc)
                k.flush()

        if "F" in run:
            with ExitStack() as ph:
                wstg = sb(ph, "wstg", [128, 8, D], F32)
                wom = sb(ph, "wom", [128, 4, D], BF16)
                wod = sb(ph, "wod", [128, 4, D], BF16)
                wout = sb(ph, "wout", [128, 8, D], BF16)
                fg = sb(ph, "fg", [128, D], F32)
                gmt = sb(ph, "gmt", [128, 2, 4, 512], BF16)
                gdt = sb(ph, "gdt", [128, 2, 4, 512], BF16)
                gm = sb(ph, "gm", [128, 2, 8, 512], BF16)
                gd = sb(ph, "gd", [128, 2, 8, 512], BF16)
                m1 = sb(ph, "m1", [128, 2, 512], F32)
                m2 = sb(ph, "m2", [128, 2, 512], F32)
                mg = sb(ph, "mg", [128, 2, 8, 512], BF16)
                xt = sb(ph, "xtf", [128, 2, D], F32)
                rr = sb(ph, "rr", [128, 2, D], F32)
                junk = sb(ph, "junkf", [128, D], BF16)
                st = sb(ph, "stf", [128, 2, 4], F32)
                ot = sb(ph, "ot", [128, 2, D], F32)
                Rwstg, Rwom, Rwod, Rwout, Rfg = Reg(), Reg(), Reg(), Reg(), Reg()
                Rwout2 = Reg()
                Rgmt, Rgdt, Rgm, Rgd = [Reg(), Reg()], [Reg(), Reg()], [Reg(), Reg()], [Reg(), Reg()]
                Rmgt = [[Reg() for _ in range(8)] for _ in range(2)]
                Rm1, Rm2, Rjunk = [Reg(), Reg()], [Reg(), Reg()], Reg()
                Rxt, Rrr, Rst, Rot = [Reg(), Reg()], [Reg(), Reg()], [Reg(), Reg()], [Reg(), Reg()]
                wstg2 = sb(ph, "wstg2", [128, 8, D], F32)
                Rwa, Rwb, Rw2 = Reg(), Reg(), Reg()
                k.dma("sp", wstg[:, 0:4, :], wom_d.rearrange("(h p) n -> p h n", p=128), (), [Rwa])
                k.dma("sp", wstg[:, 4:8, :], wod_d.rearrange("(h p) n -> p h n", p=128), (), [Rwb])
                k.dma("sp", wstg2[:], wout_d.rearrange("(k p) n -> p k n", p=128), (), [Rw2])
                k.act(wom[:], wstg[:, 0:4, :], AF.Copy, [Rwa], [Rwom])
                k.copy("dve", wod[:], wstg[:, 4:8, :], [Rwb], [Rwod])
                k.act(wout[:, 0:4, :], wstg2[:, 0:4, :], AF.Copy, [Rw2], [Rwout])
                k.copy("dve", wout[:, 4:8, :], wstg2[:, 4:8, :], [Rw2], [Rwout2])
                k.dma("sp", fg[:], fg_d[:, :], (), [Rfg])
                npz = 0
                ntile = 0
                def f_loads(qc):
                    s = qc % 2
                    qsl = slice(qc * 512, (qc + 1) * 512)
                    k.dma("sp", gmt[:, s], GMT_d[:, qsl].rearrange("(h p) t -> p h t", p=128), [R_d["GMT"]], [Rgmt[s]])
                    k.dma("sp", gdt[:, s], GDT_d[:, qsl].rearrange("(h p) t -> p h t", p=128), [R_d["GDT"]], [Rgdt[s]])
                    k.dma("sp", gm[:, s], GM_d[:, qsl].rearrange("(h p) t -> p h t", p=128), [R_d["GM"]], [Rgm[s]])
                    k.dma("sp", gd[:, s], GD_d[:, qsl].rearrange("(h p) t -> p h t", p=128), [R_d["GD"]], [Rgd[s]])

                def f_first(qc):
                    s = qc % 2
                    for dt_ in range(8):
                        pi = fcnt[0] % 2
                        fcnt[0] += 1
                        pa = P2[pi]
                        dsl = slice(dt_ * 128, (dt_ + 1) * 128)
                        k.mms([(pa[:, 0:512], [(wom[:, h, dsl], gmt[:, s, h, :]) for h in range(4)]),
                               (pa[:, 512:1024], [(wod[:, h, dsl], gdt[:, s, h, :]) for h in range(4)])],
                              [Rwom, Rwod, Rgmt[s], Rgdt[s]], RP2[pi])
                        b_ = dt_ % 2
                        k.tt("dve", m1[:, b_, :], pa[:, 0:512], gm[:, s, dt_, :], ALU.mult, [RP2[pi][0], Rgm[s]], [Rm1[b_]])
                        k.tt("dve", m2[:, b_, :], pa[:, 512:1024], gd[:, s, dt_, :], ALU.mult, [RP2[pi][1], Rgd[s]], [Rm2[b_]])
                        k.tt("pool", mg[:, s, dt_, :], m1[:, b_, :], m2[:, b_, :], ALU.add, [Rm1[b_], Rm2[b_]], [Rmgt[s][dt_]])

                def f_out(qc):
                    s = qc % 2
                    for tt4 in range(4):
                        i = qc * 4 + tt4
                        s2 = i % 2
                        if s2 == 0:
                            halves = [(P2[2][:, 0:512], RP2[2][0]), (P2[2][:, 512:1024], RP2[2][1])]
                        else:
                            halves = [(P1[0][:], RP1[0]), (P1[1][:], RP1[1])]
                        tsl = slice(tt4 * 128, (tt4 + 1) * 128)
                        k.mms([(halves[0][0], [(mg[:, s, kk, tsl], wout[:, kk, 0:512]) for kk in range(8)]),
                               (halves[1][0], [(mg[:, s, kk, tsl], wout[:, kk, 512:1024]) for kk in range(8)])],
                              Rmgt[s] + [Rwout, Rwout2], [halves[0][1], halves[1][1]])
                        k.dma("sp", xt[:, s2, :], x_d[OWN0 + i * 128:OWN0 + (i + 1) * 128, :], (), [Rxt[s2]])
                        for hh in range(2):
                            k.tt("dve", rr[:, s2, hh * 512:(hh + 1) * 512], halves[hh][0], xt[:, s2, hh * 512:(hh + 1) * 512], ALU.add,
                                 [halves[hh][1], Rxt[s2]], [Rrr[s2]])
                        k.act(junk[:], rr[:, s2, :], AF.Square, [Rrr[s2]], [Rjunk, Rst[s2]], accum=st[:, s2, 0:1])
                        k.act(st[:, s2, 1:2], st[:, s2, 0:1], AF.Ln, [Rst[s2], R_c], [Rst[s2]], scale=1.0 / D, bias=vec[:, 33:34])
                        k.act(st[:, s2, 3:4], st[:, s2, 1:2], AF.Exp, [Rst[s2]], [Rst[s2]], scale=-0.5)
                        k.act(rr[:, s2, :], rr[:, s2, :], AF.Copy, [Rrr[s2], Rst[s2]], [Rrr[s2]], scale=st[:, s2, 3:4])
                        k.tt("pool", ot[:, s2, :], rr[:, s2, :], fg[:], ALU.mult, [Rrr[s2], Rfg], [Rot[s2]])
                        k.dma("pool", out_d[i * 128:(i + 1) * 128, :], ot[:, s2, :], [Rot[s2]], [R_d["out"]])

                fcnt = [0]
                f_loads(0)
                f_loads(1)
                f_first(0)
                for qc in range(8):
                    if qc + 1 < 8:
                        f_first(qc + 1)
                    f_out(qc)
                    if qc + 2 < 8:
                        f_loads(qc + 2)
                k.flush()
    return nc


def _rot_cols(w, head_dim):
    n = w.shape[1]
    half = head_dim // 2
    idx = np.arange(n)
    d = idx % head_dim
    src = np.where(d < half, idx + half, idx - half)
    return w[:, src]


def _masks(half):
    kk = np.arange(128)[:, None]
    qq = np.arange(128)[None, :]
    lo = (kk >= qq)
    hi = (kk <= qq)
    lo_first = lo & (kk >= 64) if half == 0 else lo
    hi_last = hi & (kk < 64) if half == 1 else hi

    def tile(pattern):
        return np.concatenate(pattern, axis=1)
    plain = tile([lo, hi] * 4)
    first = tile([lo_first, hi] + [lo, hi] * 3)
    last = tile([lo, hi] * 3 + [lo, hi_last])
    g3 = tile([lo_first, hi, lo, hi_last] * 2)
    m = np.concatenate([plain, first, last, g3], axis=1).astype(np.float32)
    return m.astype(ml_dtypes.bfloat16)


def prepare_inputs(x, positions, attn_norm_g, w_in, b_gate, mla_q_norm_g, mla_kv_norm_g,
                   w_uq, w_ukv, w_o_mla, w_o_dil, w_out, final_norm_g):
    f32 = np.float32
    x = np.asarray(x, f32)
    positions = np.asarray(positions, np.int32)
    w_in0 = np.asarray(w_in, f32)[0]
    w_in_ext = np.concatenate([w_in0, _rot_cols(w_in0[:, C_KR:C_KR + 32], 32)], axis=1)
    assert w_in_ext.shape[1] == WIN_EXT
    wuq0 = np.asarray(w_uq, f32)[0].reshape(256, 8, 96)
    rot = np.zeros_like(wuq0)
    rot[:, :, 64:96] = _rot_cols(wuq0[:, :, 64:96].reshape(256, 8 * 32), 32).reshape(256, 8, 32)
    wuq_ext = np.concatenate([wuq0.reshape(256, 768), rot.reshape(256, 768)], axis=1)
    wukv0 = np.asarray(w_ukv, f32)[0].reshape(128, 8, 128)
    wukv_ext = np.concatenate([wukv0[:, :, :64].reshape(128, 512), wukv0[:, :, 64:].reshape(128, 512)], axis=1)
    p = np.arange(128)
    vec = np.zeros((128, 40), f32)
    vec[:, 0:8] = np.asarray(attn_norm_g, f32)[0].reshape(8, 128).T
    vec[:, 8:24] = np.asarray(b_gate, f32)[0].reshape(16, 128).T
    vec[:, 24:26] = np.asarray(mla_q_norm_g, f32)[0].reshape(2, 128).T
    vec[:, 26] = np.asarray(mla_kv_norm_g, f32)[0]
    invD = (10000.0 ** (-(2.0 * (p % 32)) / 64.0)).astype(f32)
    invM = (10000.0 ** (-(2.0 * (p % 16)) / 32.0)).astype(f32)
    vec[:, 27] = (invD.astype(np.float64) / (2 * math.pi)).astype(f32)
    vec[:, 28] = (invM.astype(np.float64) / (2 * math.pi)).astype(f32)
    vec[:, 29] = TWO_PI_S
    vec[:, 30] = np.where((p % 64) < 32, -TWO_PI_S, TWO_PI_S)
    vec[:, 31] = np.where((p % 32) < 16, -TWO_PI_S, TWO_PI_S)
    vec[:, 33] = EPS
    pm = np.clip(p - 64, 0, 31)
    invC = np.where(p < 64, invD, np.where(p < 96, (10000.0 ** (-(2.0 * (pm % 16)) / 32.0)), 0.0))
    vec[:, 34] = (invC.astype(np.float64) / (2 * math.pi)).astype(f32)
    vec[:, 35] = np.where(p < 64, np.where(p < 32, -TWO_PI_S, TWO_PI_S), np.where((p < 96) & (pm < 16), -TWO_PI_S, TWO_PI_S))
    vec[:, 32] = TWO_PI_S / 4.0
    fg = np.ascontiguousarray(np.broadcast_to(np.asarray(final_norm_g, f32)[None, :], (128, D)))
    ident = np.eye(128, dtype=f32)
    dd = np.arange(128)
    srcp = np.where((dd % 64) < 32, dd + 32, dd - 32)
    permd = np.zeros((128, 128), f32)
    permd[srcp, dd] = 1.0
    permd = permd.astype(ml_dtypes.bfloat16)
    shared = {"w_in": np.ascontiguousarray(w_in_ext), "w_uq": np.ascontiguousarray(wuq_ext),
              "w_ukv": np.ascontiguousarray(wukv_ext), "w_o_mla": np.asarray(w_o_mla, f32)[0],
              "w_o_dil": np.asarray(w_o_dil, f32)[0], "w_out": np.asarray(w_out, f32)[0],
              "vecs": vec, "fg": fg, "ident": ident, "permd": permd}
    in_maps = []
    for c in range(8):
        b, half = divmod(c, 2)
        shift = (half * NOWN - OWN0) % S
        m = dict(shared)
        m["x"] = np.ascontiguousarray(np.roll(x[b], -shift, axis=0))
        m["pos"] = np.ascontiguousarray(np.roll(positions[b], -shift)[None, :])
        m["masks"] = _masks(half)
        in_maps.append(m)
    return in_maps


_NC_CACHE = {}


def kernel(x, positions, attn_norm_g, w_in, b_gate, mla_q_norm_g, mla_kv_norm_g,
           w_uq, w_ukv, w_o_mla, w_o_dil, w_out, final_norm_g):
    in_maps = prepare_inputs(x, positions, attn_norm_g, w_in, b_gate, mla_q_norm_g, mla_kv_norm_g,
                             w_uq, w_ukv, w_o_mla, w_o_dil, w_out, final_norm_g)
    nc = build()
    res = run_bass_kernel_spmd(nc, in_maps, core_ids=list(range(8)))
    out = np.zeros((4, S, D), np.float32)
    for c in range(8):
        b, half = divmod(c, 2)
        out[b, half * NOWN:(half + 1) * NOWN] = np.asarray(res.results[c]["out"], np.float32)
    return out
```

```python
import math
from contextlib import ExitStack

import numpy as np
import ml_dtypes
import concourse.bass as bass
import concourse.mybir as mybir
from concourse.bass_utils import run_bass_kernel_spmd

F32 = mybir.dt.float32
BF16 = mybir.dt.bfloat16
I32 = mybir.dt.int32
AF = mybir.ActivationFunctionType
ALU = mybir.AluOpType

S = 8192
D = 1024
OWN0 = 1024
NOWN = 4096
NLOC = 6144
EPS = 1e-6
TWO_PI_S = 6.28318
DIL = (1, 4, 16)
NDSEM = 48

C_CQ, C_CKV, C_KR, C_ZM, C_QD, C_KD, C_VD, C_ZD, C_GM, C_GD = 0, 256, 384, 416, 928, 2464, 4000, 5536, 6048, 7072
C_KR_ROT = 8096
WIN_EXT = 8128


def ssl(start, n, step):
    return slice(start, start + (n - 1) * step + 1, step)


class Reg:
    __slots__ = ("w", "r", "free_w")

    def __init__(self, free_w=False):
        self.w = {}
        self.r = {}
        self.free_w = free_w


class K:
    ENG = ("pe", "act", "dve", "pool", "sp")

    def __init__(self, nc, es):
        self.nc = nc
        self.streams = {e: [] for e in self.ENG}
        self.esem = {e: es.enter_context(nc.semaphore("sem_" + e)) for e in self.ENG}
        self.ecnt = {e: 0 for e in self.ENG}
        self.waited = {e: {} for e in self.ENG}
        self.dsem = [es.enter_context(nc.semaphore("dsem%d" % i)) for i in range(NDSEM)]
        self.dcnt = [0] * NDSEM
        self.dnext = {"sp": 0, "pool": NDSEM // 2}
        self.semobj = {}
        for e in self.ENG:
            self.semobj[id(self.esem[e])] = self.esem[e]
        for s in self.dsem:
            self.semobj[id(s)] = s

    def _deps(self, eng, reads, writes, extra=()):
        need = {}
        for r in reads:
            for k, v in r.w.items():
                need[k] = max(need.get(k, 0), v)
        for r in writes:
            if r.free_w:
                continue
            for k, v in r.w.items():
                need[k] = max(need.get(k, 0), v)
            for k, v in r.r.items():
                need[k] = max(need.get(k, 0), v)
        for k, v in extra:
            need[k] = max(need.get(k, 0), v)
        out = []
        wd = self.waited[eng]
        own = id(self.esem[eng])
        for k, v in need.items():
            if eng == "pe" and k == own:
                continue
            if wd.get(k, 0) >= v:
                continue
            wd[k] = v
            out.append((self.semobj[k], v))
        return out

    def _mark(self, key, val, reads, writes):
        for r in reads:
            r.r[key] = max(r.r.get(key, 0), val)
        for r in writes:
            r.w[key] = max(r.w.get(key, 0), val)

    def op(self, eng, fn, reads=(), writes=()):
        waits = self._deps(eng, reads, writes)
        sem = self.esem[eng]
        self.ecnt[eng] += 1
        val = self.ecnt[eng]

        def emit(e, fn=fn, waits=waits, sem=sem):
            for s, v in waits:
                e.wait_ge(s, v)
            fn(e).then_inc(sem, 1)

        self.streams[eng].append(emit)
        self._mark(id(sem), val, reads, writes)

    def dma(self, q, out, in_, reads=(), writes=()):
        i = self.dnext[q]
        half = NDSEM // 2
        base = 0 if q == "sp" else half
        self.dnext[q] = base + (i - base + 1) % half
        sem = self.dsem[i]
        prev = self.dcnt[i]
        self.dcnt[i] += 16
        val = self.dcnt[i]
        extra = [(id(sem), prev)] if prev else []
        waits = self._deps(q, reads, writes, extra)

        def emit(e, waits=waits, sem=sem, out=out, in_=in_):
            for s, v in waits:
                e.wait_ge(s, v)
            e.dma_start(out=out, in_=in_).then_inc(sem, 16)

        self.streams[q].append(emit)
        self._mark(id(sem), val, reads, writes)

    def barrier(self):
        tgt = [(id(self.esem[e]), self.ecnt[e]) for e in self.ENG if self.ecnt[e]]
        tgt += [(id(self.dsem[i]), self.dcnt[i]) for i in range(NDSEM) if self.dcnt[i]]
        for eng in self.ENG:
            waits = self._deps(eng, (), (), tgt)

            def emit(e, waits=waits):
                for s, v in waits:
                    e.wait_ge(s, v)

            self.streams[eng].append(emit)

    def flush(self):
        self.barrier()
        streams = self.streams
        self.streams = {e: [] for e in self.ENG}
        with self.nc.Block() as blk:
            def run(name):
                def body(e):
                    for f in streams[name]:
                        f(e)
                return body
            blk.tensor(run("pe"))
            blk.scalar(run("act"))
            blk.vector(run("dve"))
            blk.gpsimd(run("pool"))
            blk.sync(run("sp"))

    def copy(self, eng, out, in_, R=(), W=()):
        self.op(eng, lambda e: e.tensor_copy(out=out, in_=in_), R, W)

    def memset(self, eng, ap, val, W=()):
        self.op(eng, lambda e: e.memset(ap, val), (), W)

    def ts(self, eng, out, in0, s1, s2, op0, op1, R=(), W=()):
        if op1 is None:
            self.op(eng, lambda e: e.tensor_scalar(out=out, in0=in0, scalar1=s1, scalar2=None, op0=op0), R, W)
        else:
            self.op(eng, lambda e: e.tensor_scalar(out=out, in0=in0, scalar1=s1, scalar2=s2, op0=op0, op1=op1), R, W)

    def tt(self, eng, out, in0, in1, op, R=(), W=()):
        self.op(eng, lambda e: e.tensor_tensor(out=out, in0=in0, in1=in1, op=op), R, W)

    def stt(self, eng, out, in0, scalar, in1, op0, op1, R=(), W=()):
        self.op(eng, lambda e: e.scalar_tensor_tensor(out=out, in0=in0, scalar=scalar, in1=in1, op0=op0, op1=op1), R, W)

    def recip(self, out, in_, R=(), W=()):
        self.op("dve", lambda e: e.reciprocal(out=out, in_=in_), R, W)

    def act(self, out, in_, func, R=(), W=(), bias=None, scale=None, accum=None):
        kw = {}
        if bias is not None:
            kw["bias"] = bias
        if scale is not None:
            kw["scale"] = scale
        if accum is not None:
            kw["accum_out"] = accum
        self.op("act", lambda e: e.activation(out=out, in_=in_, func=func, **kw), R, W)

    def mm(self, ps, pairs, R=(), W=()):
        def fn(e):
            n = len(pairs)
            ins = None
            for i, (l, r) in enumerate(pairs):
                ins = e.matmul(ps, lhsT=l, rhs=r, start=(i == 0), stop=(i == n - 1))
            return ins
        self.op("pe", fn, R, W)

    def mms(self, groups, R=(), W=()):
        def fn(e):
            ins = None
            for ps, pairs in groups:
                n = len(pairs)
                for i, (l, r) in enumerate(pairs):
                    ins = e.matmul(ps, lhsT=l, rhs=r, start=(i == 0), stop=(i == n - 1))
            return ins
        self.op("pe", fn, R, W)

    def trs(self, items, ident, R=(), W=()):
        def fn(e):
            ins = None
            for o, i in items:
                ins = e.transpose(o, i, ident)
            return ins
        self.op("pe", fn, R, W)


def build(stop_after=None, debug=False):
    nc = bass.Bass("TRN2", target_bir_lowering=False)
    kind_dbg = "ExternalOutput" if debug else "Internal"

    def din(name, shape, dt):
        return nc.dram_tensor(name, shape, dt, kind="ExternalInput").ap()

    def dscr(name, shape, dt=BF16):
        return nc.dram_tensor(name, shape, dt, kind=kind_dbg).ap()

    x_d = din("x", [S, D], F32)
    pos_d = din("pos", [1, S], I32)
    win_d = din("w_in", [D, WIN_EXT], F32)
    wuq_d = din("w_uq", [256, 2 * 768], F32)
    wukv_d = din("w_ukv", [128, 1024], F32)
    wom_d = din("w_o_mla", [512, D], F32)
    wod_d = din("w_o_dil", [512, D], F32)
    wout_d = din("w_out", [D, D], F32)
    vec_d = din("vecs", [128, 40], F32)
    fg_d = din("fg", [128, D], F32)
    ident_d = din("ident", [128, 128], F32)
    mask_d = din("masks", [128, 4 * 1024], BF16)
    perm_d = din("permd", [128, 128], BF16)
    out_d = nc.dram_tensor("out", [NOWN, D], F32, kind="ExternalOutput").ap()

    TAB = dscr("TAB", [4, 128, S], F32)
    KVN_d = dscr("KVN", [128, S])
    KPE_d = dscr("KPE", [32, S])
    CQN_d = dscr("CQN", [256, NOWN])
    SZM_d = dscr("SZM", [512, NOWN])
    SZD_d = dscr("SZD", [512, NOWN])
    QD_d = dscr("QD", [1536, NOWN])
    KD_d = dscr("KD", [1536, NLOC])
    VD_d = dscr("VD", [24, 128, 48 * 64])
    GM_d = dscr("GM", [1024, NOWN])
    GD_d = dscr("GD", [1024, NOWN])
    KM_d = dscr("KM", [512, S])
    VM_d = dscr("VM", [S, 512])
    QM_d = dscr("QM", [768, NOWN])
    GMT_d = dscr("GMT", [512, NOWN])
    GDT_d = dscr("GDT", [512, NOWN])
    RR_d = nc.dram_tensor("RRS", [2, 512], F32).ap()
    RD1_d = nc.dram_tensor("RD1", [1, NOWN], F32).ap()
    RD2_d = nc.dram_tensor("RD2", [1, NOWN], F32).ap()
    R_d = {n: Reg(free_w=True) for n in ("TAB", "KVN", "KPE", "CQN", "SZM", "SZD", "QD", "KD", "VD", "GM", "GD", "KM", "VM", "QM", "GMT", "GDT", "out")}

    phases = ["T", "H", "A", "M0", "MLA", "DIL", "F"]
    nph = len(phases) if stop_after is None else phases.index(stop_after) + 1
    run = set(phases[:nph])

    with ExitStack() as es:
        k = K(nc, es)

        def sb(stack, name, shape, dt):
            return stack.enter_context(nc.sbuf_tensor("s_" + name, shape, dt))

        vec = sb(es, "vec", [128, 40], F32)
        ident = sb(es, "ident", [128, 128], F32)
        ones = sb(es, "ones", [128, 128], F32)
        R_c = Reg()
        k.dma("sp", vec[:], vec_d[:, :], (), [R_c])
        k.dma("sp", ident[:], ident_d[:, :], (), [R_c])
        k.memset("dve", ones[:], 1.0, [R_c])
        P2 = [es.enter_context(nc.psum_tensor("p2_%d" % i, [128, 1024], F32)) for i in range(3)]
        P1 = [es.enter_context(nc.psum_tensor("p1_%d" % i, [128, 512], F32)) for i in range(2)]
        RP2 = [[Reg(), Reg()] for _ in range(3)]
        RP1 = [Reg(), Reg()]

        if "T" in run:
            with ExitStack() as ph:
                posi = sb(ph, "posi", [128, 2, 2048], I32)
                posf = sb(ph, "posf", [128, 2048], F32)
                tt_ = sb(ph, "tt", [128, 2048], F32)
                ti_ = sb(ph, "ti", [128, 2048], I32)
                tf_ = sb(ph, "tf", [128, 2048], F32)
                fr_ = sb(ph, "fr", [128, 2048], F32)
                so_ = sb(ph, "so", [128, 2, 2048], F32)
                Rposi = [Reg(), Reg()]
                Rposf, Rtt, Rti, Rtf, Rfr = Reg(), Reg(), Reg(), Reg(), Reg()
                Rso = [Reg(), Reg()]
                n = 0
                fa_ = sb(ph, "fa", [128, 2048], F32)
                Rfa = Reg()
                for c in range(4):
                    sl = slice(c * 2048, (c + 1) * 2048)
                    k.dma("sp", posi[:, c % 2, :], pos_d[0:1, sl].partition_broadcast(128), (), [Rposi[c % 2]])
                    k.copy("dve", posf[:], posi[:, c % 2, :], [Rposi[c % 2]], [Rposf])
                    k.ts("dve", tt_[:], posf[:], vec[:, 34:35], None, ALU.mult, None, [Rposf, R_c], [Rtt])
                    k.copy("dve", ti_[:], tt_[:], [Rtt], [Rti])
                    k.copy("dve", tf_[:], ti_[:], [Rti], [Rtf])
                    k.tt("dve", fr_[:], tt_[:], tf_[:], ALU.subtract, [Rtt, Rtf], [Rfr])
                    k.stt("dve", fa_[:], fr_[:], -1.0, fr_[:], ALU.mult, ALU.max, [Rfr], [Rfa])
                    for kind_ in range(2):
                        s = n % 2
                        n += 1
                        if kind_ == 0:
                            k.act(so_[:, s, :], fa_[:], AF.Sin, [Rfa], [Rso[s]], scale=-TWO_PI_S, bias=vec[:, 32:33])
                        else:
                            k.act(so_[:, s, :], fr_[:], AF.Sin, [Rfr, R_c], [Rso[s]], scale=vec[:, 35:36])
                        k.dma("pool", TAB[kind_, 0:64, sl], so_[0:64, s, :], [Rso[s]], [R_d["TAB"]])
                        k.dma("pool", TAB[kind_, 64:128, sl], so_[0:64, s, :], [Rso[s]], [R_d["TAB"]])
                        k.dma("pool", TAB[2 + kind_, 0:32, sl], so_[64:96, s, :], [Rso[s]], [R_d["TAB"]])
                k.flush()

        if "H" in run:
            with ExitStack() as phH:
                hT = sb(phH, "hT", [128, 8, NLOC], BF16)
                RhT = [[Reg() for _ in range(4)] for _ in range(12)]
                with ExitStack() as ph:
                    xt = sb(ph, "xt", [128, 4, D], F32)
                    junk = sb(ph, "junk", [128, D], BF16)
                    xn = sb(ph, "xn", [128, 2, D], F32)
                    st = sb(ph, "st", [128, 3, 4], F32)
                    hTo = sb(ph, "hTo", [128, 2, 8, 512], BF16)
                    gfull = sb(ph, "gfull", [128, 8, 128], F32)
                    wkvs = sb(ph, "wkvs", [128, 8, 192], F32)
                    wkv = sb(ph, "wkv", [128, 8, 192], BF16)
                    sq = sb(ph, "sq", [128, 512], F32)
                    ms = sb(ph, "ms", [128, 512], F32)
                    rs = sb(ph, "rs", [128, 512], F32)
                    tabm = sb(ph, "tabm", [32, 2, 2, 512], F32)
                    t1 = sb(ph, "t1", [32, 512], F32)
                    t2 = sb(ph, "t2", [32, 512], F32)
                    t2p = sb(ph, "t2p", [32, 512], F32)
                    Rt2p = Reg()
                    kvst = sb(ph, "kvst", [128, 2, 512], BF16)
                    kpst = sb(ph, "kpst", [32, 2, 512], BF16)
                    Rxt = [Reg() for _ in range(4)]
                    Rjunk, Rsq, Rms, Rrs, Rt1, Rt2, Rw = Reg(), Reg(), Reg(), Reg(), Reg(), Reg(), Reg()
                    Rxn = [Reg(), Reg()]
                    Rst = [Reg(), Reg(), Reg()]
                    RhTo = [[Reg() for _ in range(4)] for _ in range(2)]
                    Rgf = Reg()
                    for kk in range(8):
                        k.ts("pool", gfull[:, kk, :], ones[:], vec[:, kk:kk + 1], None, ALU.mult, None, [R_c], [Rgf])
                    Rtabm = [Reg(), Reg()]
                    Rkvst = [Reg(), Reg()]
                    Rkpst = [Reg(), Reg()]
                    wv_ = win_d.rearrange("(k p) n -> p k n", p=128)
                    k.dma("sp", wkvs[:, :, 0:160], wv_[:, :, C_CKV:C_CKV + 160], (), [Rw])
                    k.dma("sp", wkvs[:, :, 160:192], wv_[:, :, C_KR_ROT:C_KR_ROT + 32], (), [Rw])
                    k.copy("dve", wkv[:], wkvs[:], [Rw], [Rw])
                    def stage_a(i):
                        s4, s3 = i % 4, i % 3
                        k.dma("sp", xt[:, s4, :], x_d[i * 128:(i + 1) * 128, :], (), [Rxt[s4]])
                        k.act(junk[:], xt[:, s4, :], AF.Square, [Rxt[s4]], [Rjunk, Rst[s3]], accum=st[:, s3, 0:1])
                        k.act(st[:, s3, 1:2], st[:, s3, 0:1], AF.Ln, [Rst[s3]], [Rst[s3]], scale=1.0 / D, bias=vec[:, 33:34])

                    def stage_b(i):
                        s3 = i % 3
                        k.act(st[:, s3, 3:4], st[:, s3, 1:2], AF.Exp, [Rst[s3]], [Rst[s3]], scale=-0.5)

                    stage_a(0)
                    stage_a(1)
                    stage_b(0)
                    for i in range(64):
                        c, j = divmod(i, 4)
                        s4, s3, s2 = i % 4, i % 3, i % 2
                        if i + 2 < 64:
                            stage_a(i + 2)
                        k.act(xn[:, s2, :], xt[:, s4, :], AF.Copy, [Rxt[s4], Rst[s3]], [Rxn[s2]], scale=st[:, s3, 3:4])
                        pt = P2[s2]
                        k.trs([(pt[:, kk * 128:(kk + 1) * 128], xn[:, s2, kk * 128:(kk + 1) * 128]) for kk in range(8)],
                              ident[:], [Rxn[s2], R_c], RP2[s2])
                        if i + 1 < 64:
                            stage_b(i + 1)
                        if c < 12:
                            dst3 = hT[:, :, i * 128:(i + 1) * 128]
                            Rdst = RhT[c][j]
                        else:
                            dst3 = hTo[:, c % 2, :, j * 128:(j + 1) * 128]
                            Rdst = RhTo[c % 2][j]
                        k.tt("dve", dst3, pt[:, :].rearrange("p (a b) -> p a b", a=8), gfull[:], ALU.mult, RP2[s2] + [Rgf], [Rdst])
                        if j != 3:
                            continue
                        sl = slice(c * 512, (c + 1) * 512)
                        s = c % 2
                        if c < 12:
                            src = lambda kk: hT[:, kk, sl]
                        else:
                            src = lambda kk, s=s: hTo[:, s, kk, :]
                        Rsrc = RhT[c] if c < 12 else RhTo[s]
                        pc, pk, pss = P2[2], P1[0], P1[1]
                        k.mm(pc[:, 0:512], [(wkv[:, kk, 0:128], src(kk)) for kk in range(8)], [Rw] + Rsrc, [RP2[2][0]])
                        k.mm(pk[0:32, :], [(wkv[:, kk, 128:160], src(kk)) for kk in range(8)], [Rw] + Rsrc, [RP1[0]])
                        k.act(sq[:], pc[:, 0:512], AF.Square, [RP2[2][0]], [Rsq])
                        k.mm(pss[:], [(ones[:], sq[:])], [Rsq, R_c], [RP1[1]])
                        k.act(ms[:], pss[:], AF.Ln, [RP1[1], R_c], [Rms], scale=1.0 / 128, bias=vec[:, 33:34])
                        k.act(rs[:], ms[:], AF.Exp, [Rms], [Rrs], scale=-0.5)
                        k.stt("dve", kvst[:, s, :], pc[:, 0:512], vec[:, 26:27], rs[:], ALU.mult, ALU.mult,
                              [RP2[2][0], Rrs, R_c], [Rkvst[s]])
                        k.dma("pool", KVN_d[:, sl], kvst[:, s, :], [Rkvst[s]], [R_d["KVN"]])
                        k.dma("sp", tabm[:, s, 0, :], TAB[2, 0:32, sl], [R_d["TAB"]], [Rtabm[s]])
                        k.dma("sp", tabm[:, s, 1, :], TAB[3, 0:32, sl], [R_d["TAB"]], [Rtabm[s]])
                        k.tt("dve", t1[:], pk[0:32, :], tabm[:, s, 0, :], ALU.mult, [RP1[0], Rtabm[s]], [Rt1])
                        k.tt("dve", t2[:], pk[0:32, :], tabm[:, s, 1, :], ALU.mult, [RP1[0], Rtabm[s]], [Rt2])
                        k.dma("pool", t2p[0:16, :], t2[16:32, :], [Rt2], [Rt2p])
                        k.dma("pool", t2p[16:32, :], t2[0:16, :], [Rt2], [Rt2p])
                        k.tt("dve", kpst[:, s, :], t1[:], t2p[:], ALU.subtract, [Rt1, Rt2p], [Rkpst[s]])
                        k.dma("pool", KPE_d[:, sl], kpst[:, s, :], [Rkpst[s]], [R_d["KPE"]])
                    k.flush()

                if "A" in run:
                    with ExitStack() as ph:
                        wst = sb(ph, "wst", [128, 3, 8, 128], F32)
                        wt = sb(ph, "wt", [128, 2, 8, 8, 128], BF16)
                        tabd = sb(ph, "tabd", [128, 2, 2, 512], F32)
                        sqa = sb(ph, "sqa", [128, 2, 2, 512], F32)
                        msa = sb(ph, "msa", [128, 2, 512], F32)
                        rsa = sb(ph, "rsa", [128, 2, 512], F32)
                        ta = sb(ph, "ta", [128, 2, 2, 512], F32)
                        ost = sb(ph, "ost", [128, 4, 512], BF16)
                        Rwst = [Reg() for _ in range(3)]
                        Rwt = [Reg(), Reg()]
                        Rtabd = [Reg(), Reg()]
                        Rsqa, Rmsa, Rrsa = [[Reg(), Reg()], [Reg(), Reg()]], [Reg(), Reg()], [Reg(), Reg()]
                        Rta = [[Reg(), Reg()], [Reg(), Reg()]]
                        Rost = [Reg() for _ in range(4)]
                        wv_ = win_d.rearrange("(k p) n -> p k n", p=128)
                        own_chunks = list(range(2, 10))
                        groups = [("cq", [C_CQ, C_CQ + 128], [], own_chunks, CQN_d, OWN0, None)]
                        groups.append(("silu", [C_ZM + 128 * t for t in range(4)], [], own_chunks, SZM_d, OWN0, None))
                        for g in range(3):
                            groups.append(("rope", [C_QD + 512 * g + 128 * t for t in range(4)], [], own_chunks, QD_d, OWN0, 4 * g))
                        for g in range(3):
                            chs = list(range(1, 11)) if g < 2 else list(range(0, 12))
                            groups.append(("rope", [C_KD + 512 * g + 128 * t for t in range(4)], [], chs, KD_d, 0, 4 * g))
                        groups.append(("silu", [C_ZD + 128 * t for t in range(4)], [], own_chunks, SZD_d, OWN0, None))
                        groups.append(("sig", [C_GM + 128 * t for t in range(8)], [], own_chunks, GM_d, OWN0, 8))
                        groups.append(("sig", [C_GD + 128 * t for t in range(8)], [], own_chunks, GD_d, OWN0, 16))
                        no = 0
                        npz = 0
                        for g in range(3):
                            groups.append(("vd", [C_VD + 512 * g + 128 * t for t in range(4)], [], None, VD_d, 0, g))
                        nwc = [0]
                        nvs = [0]
                        cnt = [0]
                        qb = sb(ph, "qb", [128, 3, 512], BF16)
                        Rqb = [Reg() for _ in range(3)]
                        permd = sb(ph, "permd", [128, 128], BF16)
                        Rpermd = Reg()
                        k.dma("sp", permd[:], perm_d[:, :], (), [Rpermd])
                        VTB = 6
                        vst = sb(ph, "vst", [128, 2, 8, VTB, 64], BF16)
                        Rvst = [[Reg() for _ in range(VTB)] for _ in range(2)]

                        def load_group(gi):
                            if gi >= len(groups):
                                return
                            ws = gi % 2
                            for ti, c0 in enumerate(groups[gi][1] + groups[gi][2]):
                                s3 = nwc[0] % 3
                                nwc[0] += 1
                                k.dma("sp", wst[:, s3], wv_[:, :, c0:c0 + 128], (), [Rwst[s3]])
                                if ti % 2 == 0:
                                    k.act(wt[:, ws, ti], wst[:, s3], AF.Copy, [Rwst[s3]], [Rwt[ws]])
                                else:
                                    k.copy("dve", wt[:, ws, ti], wst[:, s3], [Rwst[s3]], [Rwt[ws]])

                        load_group(0)
                        for gi, (kind, cols, rcols, chs, dst, tok0, extra) in enumerate(groups):
                            ws = gi % 2
                            load_group(gi + 1)
                            nt = len(cols)
                            if kind == "vd":
                                g = extra
                                r_ = DIL[g]
                                ntr_ = NOWN // r_ // 128 + 1
                                li0_ = OWN0 // r_ - 64
                                ntile_ = r_ * ntr_
                                allh = [x for cc in RhT for x in cc]
                                for t0 in range(0, ntile_, VTB):
                                    nb = min(VTB, ntile_ - t0)
                                    sv = nvs[0] % 2
                                    nvs[0] += 1
                                    for tb in range(nb):
                                        res_, m_ = divmod(t0 + tb, ntr_)
                                        tok = ssl((li0_ + 128 * m_) * r_ + res_, 128, r_)
                                        pi = npz % 3
                                        npz += 1
                                        pa = P2[pi]
                                        k.mm(pa[:, 0:512].rearrange("p (a b) -> p a b", a=4),
                                             [(hT[:, kk, tok], wt[:, ws, 0:4, kk, :]) for kk in range(8)],
                                             [Rwt[ws]] + allh, [RP2[pi][0]])
                                        src3 = pa[:, 0:512].rearrange("p (h d) -> p h d", h=8)
                                        if tb % 2 == 0:
                                            k.act(vst[:, sv, :, tb, :], src3, AF.Copy, [RP2[pi][0]], [Rvst[sv][tb]])
                                        else:
                                            k.copy("dve", vst[:, sv, :, tb, :], src3, [RP2[pi][0]], [Rvst[sv][tb]])
                                    for h_ in range(8):
                                        dstv = VD_d[g * 8 + h_].rearrange("p (t d) -> p t d", d=64)[:, t0:t0 + nb, :]
                                        k.dma("pool" if h_ % 2 else "sp", dstv, vst[:, sv, h_, 0:nb, :], Rvst[sv][0:nb], [R_d["VD"]])
                                continue
                            if kind == "cq":
                                def cq_main(j, ws=ws, chs=chs):
                                    c = chs[j]
                                    sl = slice(c * 512, (c + 1) * 512)
                                    pi = j % 3
                                    pa = P2[pi]
                                    b_ = j % 2
                                    k.mms([(pa[:, 0:512], [(wt[:, ws, 0, kk, :], hT[:, kk, sl]) for kk in range(8)]),
                                           (pa[:, 512:1024], [(wt[:, ws, 1, kk, :], hT[:, kk, sl]) for kk in range(8)])],
                                          [Rwt[ws]] + RhT[c], RP2[pi])
                                    k.act(sqa[:, b_, 0, :], pa[:, 0:512], AF.Square, [RP2[pi][0]], [Rsqa[b_][0]])
                                    k.act(sqa[:, b_, 1, :], pa[:, 512:1024], AF.Square, [RP2[pi][1]], [Rsqa[b_][1]])

                                def cq_fin(j, chs=chs, dst=dst, tok0=tok0):
                                    c = chs[j]
                                    dsl = slice(c * 512 - tok0, (c + 1) * 512 - tok0)
                                    pi = j % 3
                                    pa = P2[pi]
                                    b_ = j % 2
                                    k.mm(P1[b_][:], [(ones[:], sqa[:, b_, 0, :]), (ones[:], sqa[:, b_, 1, :])], Rsqa[b_] + [R_c], [RP1[b_]])
                                    k.ts("dve", msa[:, b_, :], P1[b_][:], 1.0 / 256, EPS, ALU.mult, ALU.add, [RP1[b_]], [Rmsa[b_]])
                                    k.act(msa[:, b_, :], msa[:, b_, :], AF.Sqrt, [Rmsa[b_]], [Rmsa[b_]])
                                    k.recip(rsa[:, b_, :], msa[:, b_, :], [Rmsa[b_]], [Rrsa[b_]])
                                    for t in range(2):
                                        so = cnt[0] % 4
                                        cnt[0] += 1
                                        k.stt("dve", ost[:, so, :], pa[:, t * 512:(t + 1) * 512], vec[:, 24 + t:25 + t], rsa[:, b_, :],
                                              ALU.mult, ALU.mult, [RP2[pi][t], Rrsa[b_], R_c], [Rost[so]])
                                        k.dma("sp", dst[t * 128:(t + 1) * 128, dsl], ost[:, so, :], [Rost[so]], [R_d["CQN"]])

                                cq_main(0)
                                for j in range(len(chs)):
                                    if j + 1 < len(chs):
                                        cq_main(j + 1)
                                    cq_fin(j)
                                continue
                            if kind == "rope":
                                items = [(c, t) for c in chs for t in range(nt)]
                                info = {}
                                dname = "QD" if dst is QD_d else "KD"

                                def rope_main(j, items=items, info=info, ws=ws, nt=nt, chs_all=chs):
                                    c, t = items[j]
                                    sl = slice(c * 512, (c + 1) * 512)
                                    s = c % 2
                                    nxt = []
                                    if j == 0:
                                        nxt = [c]
                                    elif t == 1:
                                        nxt = [cc for cc in chs_all if cc > c][:1]
                                    if nxt:
                                        for cc in nxt:
                                            sl2 = slice(cc * 512, (cc + 1) * 512)
                                            k.dma("sp", tabd[:, cc % 2, 0, :], TAB[0, :, sl2], [R_d["TAB"]], [Rtabd[cc % 2]])
                                            k.dma("sp", tabd[:, cc % 2, 1, :], TAB[1, :, sl2], [R_d["TAB"]], [Rtabd[cc % 2]])
                                    pi = cnt[0] % 3
                                    sq_ = cnt[0] % 3
                                    so = cnt[0] % 4
                                    cnt[0] += 1
                                    info[j] = (pi, sq_, so)
                                    pa = P2[pi]
                                    k.mm(pa[:, 0:512], [(wt[:, ws, t, kk, :], hT[:, kk, sl]) for kk in range(8)],
                                         [Rwt[ws]] + RhT[c], [RP2[pi][0]])
                                    k.act(qb[:, sq_, :], pa[:, 0:512], AF.Copy, [RP2[pi][0]], [Rqb[sq_]])

                                def rope_fin(j, items=items, info=info, dst=dst, tok0=tok0, extra=extra, dname=dname):
                                    c, t = items[j]
                                    pi, sq_, so = info[j]
                                    s = c % 2
                                    pa = P2[pi]
                                    dsl = slice(c * 512 - tok0, (c + 1) * 512 - tok0)
                                    row0 = 512 * (extra // 4) + 128 * t
                                    k.mm(pa[:, 512:1024], [(permd[:], qb[:, sq_, :])], [Rqb[sq_], Rpermd], [RP2[pi][1]])
                                    b_ = j % 2
                                    k.tt("dve", ta[:, b_, 0, :], pa[:, 0:512], tabd[:, s, 0, :], ALU.mult, [RP2[pi][0], Rtabd[s]], [Rta[b_][0]])
                                    k.tt("dve", ta[:, b_, 1, :], pa[:, 512:1024], tabd[:, s, 1, :], ALU.mult, [RP2[pi][1], Rtabd[s]], [Rta[b_][1]])
                                    k.tt("pool", ost[:, so, :], ta[:, b_, 0, :], ta[:, b_, 1, :], ALU.add, Rta[b_], [Rost[so]])
                                    k.dma("sp", dst[row0:row0 + 128, dsl], ost[:, so, :], [Rost[so]], [R_d[dname]])

                                rope_main(0)
                                for j in range(len(items)):
                                    if j + 1 < len(items):
                                        rope_main(j + 1)
                                    rope_fin(j)
                                no += len(items)
                                npz += len(items)
                                continue
                            for c in chs:
                                sl = slice(c * 512, (c + 1) * 512)
                                dsl = slice(c * 512 - tok0, (c + 1) * 512 - tok0)
                                rhs = [hT[:, kk, sl] for kk in range(8)]
                                if kind == "rope":
                                    s = c % 2
                                    k.dma("sp", tabd[:, s, 0, :], TAB[0, :, sl], [R_d["TAB"]], [Rtabd[s]])
                                    k.dma("sp", tabd[:, s, 1, :], TAB[1, :, sl], [R_d["TAB"]], [Rtabd[s]])
                                if kind == "cq":
                                    raise AssertionError("cq handled by the pipelined branch")
                                for t in range(nt):
                                    so = no % 4
                                    no += 1
                                    pi = npz % 3
                                    npz += 1
                                    pa = P2[pi]
                                    row0 = (cols[t] - {"silu": cols[0], "sig": cols[0], "rope": cols[0]}[kind])
                                    if kind == "rope":
                                        row0 += 512 * (extra // 4)
                                    drow = slice(row0, row0 + 128)
                                    Rdst = [R_d[{id(SZM_d): "SZM", id(SZD_d): "SZD", id(QD_d): "QD", id(KD_d): "KD",
                                                 id(GM_d): "GM", id(GD_d): "GD"}[id(dst)]]]
                                    if kind == "rope":
                                        k.mms([(pa[:, 0:512], [(wt[:, ws, t, kk, :], rhs[kk]) for kk in range(8)]),
                                               (pa[:, 512:1024], [(wt[:, ws, nt + t, kk, :], rhs[kk]) for kk in range(8)])],
                                              [Rwt[ws]] + RhT[c], RP2[pi])
                                        s = c % 2
                                        raise AssertionError("rope handled by the pipelined branch")
                                    else:
                                        k.mm(pa[:, 0:512], [(wt[:, ws, t, kk, :], rhs[kk]) for kk in range(8)], [Rwt[ws]] + RhT[c], [RP2[pi][0]])
                                        if kind == "silu":
                                            k.act(ost[:, so, :], pa[:, 0:512], AF.Silu, [RP2[pi][0]], [Rost[so]])
                                        else:
                                            k.act(ost[:, so, :], pa[:, 0:512], AF.Sigmoid, [RP2[pi][0], R_c], [Rost[so]],
                                                  bias=vec[:, extra + t:extra + t + 1])
                                    k.dma("pool", dst[drow, dsl], ost[:, so, :], [Rost[so]], Rdst)
                        k.flush()

        if "M0" in run:
            with ExitStack() as ph:
                kvn = sb(ph, "kvn", [128, S], BF16)
                cqn = sb(ph, "cqn", [128, 2, NOWN], BF16)
                wks = sb(ph, "wks", [128, 1024], F32)
                wkb = sb(ph, "wkb", [128, 1024], BF16)
                wqs = sb(ph, "wqs", [128, 2, 1536], F32)
                wqb = sb(ph, "wqb", [128, 2, 1536], BF16)
                tabq = sb(ph, "tabq", [96, 2, 2, 512], F32)
                t1 = sb(ph, "t1q", [96, 2, 2, 512], F32)
                ost = sb(ph, "ostm", [128, 4, 512], BF16)
                Rkvn, Rcqn, Rwk, Rwq = Reg(), Reg(), Reg(), Reg()
                Rtabq = [Reg(), Reg()]
                Rt1 = [[Reg(), Reg()], [Reg(), Reg()]]
                Rost = [Reg() for _ in range(4)]
                k.dma("sp", kvn[:], KVN_d[:, :], [R_d["KVN"]], [Rkvn])
                k.dma("sp", cqn[:, 0, :], CQN_d[0:128, :], [R_d["CQN"]], [Rcqn])
                k.dma("sp", cqn[:, 1, :], CQN_d[128:256, :], [R_d["CQN"]], [Rcqn])
                k.dma("sp", wks[:], wukv_d[:, :], (), [Rwk])
                k.act(wkb[:], wks[:], AF.Copy, [Rwk], [Rwk])
                k.dma("sp", wqs[:], wuq_d.rearrange("(k p) n -> p k n", p=128), (), [Rwq])
                k.copy("dve", wqb[:], wqs[:], [Rwq], [Rwq])
                for s in range(2):
                    k.memset("pool", tabq[0:64, s, 0, :], 1.0, [Rtabq[s]])
                    k.memset("pool", tabq[0:64, s, 1, :], 0.0, [Rtabq[s]])
                no = 0
                npz = 0
                for hp in range(4):
                    for c in range(16):
                        sl = slice(c * 512, (c + 1) * 512)
                        so = no % 4
                        no += 1
                        pi = npz % 3
                        npz += 1
                        k.mm(P2[pi][:, 0:512], [(wkb[:, hp * 128:(hp + 1) * 128], kvn[:, sl])], [Rwk, Rkvn], [RP2[pi][0]])
                        if c % 2 == 0:
                            k.act(ost[:, so, :], P2[pi][:, 0:512], AF.Copy, [RP2[pi][0]], [Rost[so]])
                        else:
                            k.copy("dve", ost[:, so, :], P2[pi][:, 0:512], [RP2[pi][0]], [Rost[so]])
                        k.dma("sp", KM_d[hp * 128:(hp + 1) * 128, sl], ost[:, so, :], [Rost[so]], [R_d["KM"]])
                for t in range(64):
                    so = no % 4
                    no += 1
                    pi = npz % 3
                    npz += 1
                    k.mm(P2[pi][:, 0:512], [(kvn[:, t * 128:(t + 1) * 128], wkb[:, 512:1024])], [Rwk, Rkvn], [RP2[pi][0]])
                    if t % 2 == 0:
                        k.act(ost[:, so, :], P2[pi][:, 0:512], AF.Copy, [RP2[pi][0]], [Rost[so]])
                    else:
                        k.copy("dve", ost[:, so, :], P2[pi][:, 0:512], [RP2[pi][0]], [Rost[so]])
                    k.dma("sp", VM_d[t * 128:(t + 1) * 128, :], ost[:, so, :], [Rost[so]], [R_d["VM"]])
                for c in range(8):
                    sl = slice(c * 512, (c + 1) * 512)
                    gsl = slice(OWN0 + c * 512, OWN0 + (c + 1) * 512)
                    s = c % 2
                    k.dma("sp", tabq[64:96, s, 0, :], TAB[2, 0:32, gsl], [R_d["TAB"]], [Rtabq[s]])
                    k.dma("sp", tabq[64:96, s, 1, :], TAB[3, 0:32, gsl], [R_d["TAB"]], [Rtabq[s]])
                    for h in range(8):
                        so = no % 4
                        no += 1
                        pi = npz % 3
                        npz += 1
                        pa = P2[pi]
                        k.mms([(pa[0:96, 0:512], [(wqb[:, kk, h * 96:(h + 1) * 96], cqn[:, kk, sl]) for kk in range(2)]),
                               (pa[0:96, 512:1024], [(wqb[:, kk, 768 + h * 96:768 + (h + 1) * 96], cqn[:, kk, sl]) for kk in range(2)])],
                              [Rwq, Rcqn], RP2[pi])
                        b_ = h % 2
                        k.tt("dve", t1[:, b_, 0, :], pa[0:96, 0:512], tabq[:, s, 0, :], ALU.mult, [RP2[pi][0], Rtabq[s]], [Rt1[b_][0]])
                        k.tt("dve", t1[:, b_, 1, :], pa[0:96, 512:1024], tabq[:, s, 1, :], ALU.mult, [RP2[pi][1], Rtabq[s]], [Rt1[b_][1]])
                        k.tt("pool", ost[0:96, so, :], t1[:, b_, 0, :], t1[:, b_, 1, :], ALU.add, Rt1[b_], [Rost[so]])
                        k.dma("sp", QM_d[h * 96:(h + 1) * 96, sl], ost[0:96, so, :], [Rost[so]], [R_d["QM"]])
                k.flush()

        if "MLA" in run:
            with ExitStack() as ph:
                Kt = sb(ph, "Kt", [96, 2, S], BF16)
                Va = sb(ph, "Va", [128, 2, 64, 65], BF16)
                qt = sb(ph, "qt", [96, 2, NOWN], BF16)
                szt = sb(ph, "szt", [64, 2, NOWN], BF16)
                Pt = sb(ph, "Pt", [128, 3, 1024], BF16)
                rrow = sb(ph, "rrow", [65, 2, 512], F32)
                bcs = sb(ph, "bcs", [64, 2, 512], F32)
                tn = sb(ph, "tn", [64, 512], F32)
                ost = sb(ph, "osta", [64, 2, 512], BF16)
                RK = [Reg(), Reg()]
                RV = [Reg(), Reg()]
                Rq = [Reg(), Reg()]
                Rsz = [Reg(), Reg()]
                RP = [Reg() for _ in range(3)]
                Rrrow, Rbcs, Rtn = [Reg(), Reg()], [Reg(), Reg()], Reg()
                Rrrd = [Reg(), Reg()]
                Rost = [Reg(), Reg()]
                for s in range(2):
                    k.memset("pool", Va[:, s, :, 64:65], 1.0, [RV[s]])
                VMv = VM_d.rearrange("(t p) c -> p t c", p=128)
                sc = 96.0 ** -0.5

                def mla_loads(h):
                    s = h % 2
                    k.dma("sp", Kt[0:64, s, :], KM_d[h * 64:(h + 1) * 64, :], [R_d["KM"]], [RK[s]])
                    k.dma("sp", Kt[64:96, s, :], KPE_d[:, :], [R_d["KPE"]], [RK[s]])
                    k.dma("sp", qt[:, s, :], QM_d[h * 96:(h + 1) * 96, :], [R_d["QM"]], [Rq[s]])
                    k.dma("sp", Va[:, s, :, 0:64], VMv[:, :, h * 64:(h + 1) * 64], [R_d["VM"]], [RV[s]])
                    k.dma("sp", szt[:, s, :], SZM_d[h * 64:(h + 1) * 64, :], [R_d["SZM"]], [Rsz[s]])

                steps = [(h, qc, kp) for h in range(8) for qc in range(8) for kp in range(32)]

                def mla_qk(i):
                    h, qc, kp = steps[i]
                    s, ss = h % 2, i % 3
                    qsl = slice(qc * 512, (qc + 1) * 512)
                    ps = P2[ss]
                    k.mms([(ps[:, 0:512], [(Kt[:, s, (2 * kp) * 128:(2 * kp + 1) * 128], qt[:, s, qsl])]),
                           (ps[:, 512:1024], [(Kt[:, s, (2 * kp + 1) * 128:(2 * kp + 2) * 128], qt[:, s, qsl])])],
                          [RK[s], Rq[s]], RP2[ss])

                def mla_exp_pv(i):
                    h, qc, kp = steps[i]
                    s, ss, sp_ = h % 2, i % 3, i % 3
                    acc = P1[qc % 2]
                    k.act(Pt[:, sp_, :], P2[ss][:, :], AF.Exp, RP2[ss], [RP[sp_]], scale=sc)

                    def pv(e):
                        e.matmul(acc[0:65, :], lhsT=Va[:, s, 2 * kp, :], rhs=Pt[:, sp_, 0:512], start=(kp == 0), stop=False)
                        return e.matmul(acc[0:65, :], lhsT=Va[:, s, 2 * kp + 1, :], rhs=Pt[:, sp_, 512:1024], start=False, stop=(kp == 31))
                    k.op("pe", pv, [RV[s], RP[sp_]], [RP1[qc % 2]])

                def mla_norm(h, qc):
                    s = h % 2
                    qsl = slice(qc * 512, (qc + 1) * 512)
                    acc = P1[qc % 2]
                    Racc = RP1[qc % 2]
                    so = (h * 8 + qc) % 2
                    k.recip(rrow[64:65, so, :], acc[64:65, :], [Racc], [Rrrow[so]])
                    k.dma("pool", RR_d[so:so + 1, :], rrow[64:65, so, :], [Rrrow[so]], [Rrrd[so]])
                    k.dma("pool", bcs[:, so, :], RR_d[so:so + 1, :].partition_broadcast(64), [Rrrd[so]], [Rbcs[so]])
                    k.tt("dve", tn[:], acc[0:64, :], bcs[:, so, :], ALU.mult, [Racc, Rbcs[so]], [Rtn])
                    k.tt("pool", ost[:, so, :], tn[:], szt[:, s, qsl], ALU.mult, [Rtn, Rsz[s]], [Rost[so]])
                    k.dma("pool", GMT_d[h * 64:(h + 1) * 64, qsl], ost[:, so, :], [Rost[so]], [R_d["GMT"]])

                mla_loads(0)
                pending = None
                for i, (h, qc, kp) in enumerate(steps):
                    if i == 0:
                        mla_qk(0)
                        mla_qk(1)
                    if i + 2 < len(steps):
                        mla_qk(i + 2)
                    mla_exp_pv(i)
                    if kp == 31:
                        pending = (h, qc)
                    elif kp == 1 and pending is not None:
                        mla_norm(*pending)
                        pending = None
                    if qc == 0 and kp == 2 and h + 1 < 8:
                        mla_loads(h + 1)
                mla_norm(*pending)
                k.flush()

        if "DIL" in run:
            with ExitStack() as ph:
                qd = sb(ph, "qd", [64, 2, NOWN], BF16)
                kd = sb(ph, "kd", [64, 2, NLOC], BF16)
                Vd = sb(ph, "Vd", [128, 2, 48, 65], BF16)
                Vdd = sb(ph, "Vdd", [128, 2, 48, 64], BF16)
                RVdd = [Reg(), Reg()]
                mask = sb(ph, "mask", [128, 4, 1024], BF16)
                Pd = sb(ph, "Pd", [128, 2, 1024], BF16)
                Pm = sb(ph, "Pm", [128, 2, 1024], BF16)
                nd = sb(ph, "nd", [65, 2, NOWN], F32)
                szd = sb(ph, "szd", [64, 2, NOWN], BF16)
                dsp = sb(ph, "dsp", [64, 64], F32)
                bcf = sb(ph, "bcf", [64, NOWN], F32)
                tn = sb(ph, "tnd", [64, 2, 512], F32)
                ost = sb(ph, "ostd", [64, 2, 512], BF16)
                Rqd, Rkd, RVd = [Reg(), Reg()], [Reg(), Reg()], [Reg(), Reg()]
                Rmask = Reg()
                RPd, RPm = [Reg(), Reg()], [Reg(), Reg()]
                Rnd = [Reg(), Reg()]
                Rszd = [Reg(), Reg()]
                Rdsp, Rbcf, Rtn = Reg(), Reg(), [Reg(), Reg()]
                Rrd1, Rrd2 = Reg(), Reg()
                Rost = [Reg(), Reg()]
                RPmh = [[Reg(), Reg()], [Reg(), Reg()]]
                k.dma("sp", mask[:], mask_d.rearrange("p (m c) -> p m c", m=4), (), [Rmask])
                for s in range(2):
                    k.memset("pool", Vd[:, s, :, 64:65], 1.0, [RVd[s]])
                units = [(hg, g) for hg in range(8) for g in range(3)]
                dsteps = [(u, ci) for u in range(len(units)) for ci in range(8)]

                def geom(g):
                    r = DIL[g]
                    nbr = NOWN // r // 128
                    return r, nbr, nbr + 1, OWN0 // r - 64

                def dil_loads(u):
                    hg, g = units[u]
                    s = u % 2
                    r, nbr, ntr, li0 = geom(g)
                    hd = g * 8 + hg
                    if g == 0:
                        k.dma("sp", szd[:, hg % 2, :], SZD_d[hg * 64:(hg + 1) * 64, :], [R_d["SZD"]], [Rszd[hg % 2]])
                    k.dma("sp", qd[:, s, :], QD_d[hd * 64:(hd + 1) * 64, :], [R_d["QD"]], [Rqd[s]])
                    if g < 2:
                        k.dma("sp", kd[:, s, 512:5632], KD_d[hd * 64:(hd + 1) * 64, 512:5632], [R_d["KD"]], [Rkd[s]])
                    else:
                        k.dma("sp", kd[:, s, :], KD_d[hd * 64:(hd + 1) * 64, :], [R_d["KD"]], [Rkd[s]])
                    nt_ = r * ntr
                    k.dma("sp", Vdd[:, s, 0:nt_, :], VD_d[hd].rearrange("p (t d) -> p t d", d=64)[:, 0:nt_, :], [R_d["VD"]], [RVdd[s]])

                def dil_pad(u):
                    hg, g = units[u]
                    s = u % 2
                    r, nbr, ntr, li0 = geom(g)
                    nt_ = r * ntr
                    k.act(Vd[:, s, 0:nt_, 0:64], Vdd[:, s, 0:nt_, :], AF.Copy, [RVdd[s]], [RVd[s]])

                def blocks_of(g, ci):
                    r, nbr, ntr, li0 = geom(g)
                    return [divmod(ci * 4 + bi, nbr) for bi in range(4)]

                def dil_qk(i):
                    u, ci = dsteps[i]
                    hg, g = units[u]
                    s, ss = u % 2, i % 3
                    r, nbr, ntr, li0 = geom(g)
                    ps = P2[ss]
                    grp = []
                    blks = blocks_of(g, ci)
                    merged = set()
                    for bi in (0, 2):
                        (r0, n0), (r1, n1) = blks[bi], blks[bi + 1]
                        if r0 == r1 and n1 == n0 + 1:
                            kl0 = (li0 + 128 * (n0 + 1)) * r + r0
                            ks = kd[:, s, ssl(kl0, 128, r)]
                            qs2 = qd[:, s, ssl((128 * n0) * r + r0, 256, r)]
                            col = (bi * 2 + 1) * 128
                            grp.append((ps[:, col:col + 256], [(ks, qs2)]))
                            merged.add((bi, 1))
                            merged.add((bi + 1, 0))
                    for bi, (res, n_) in enumerate(blks):
                        qo = (128 * n_) * r + res
                        qs = qd[:, s, ssl(qo, 128, r)]
                        for side in range(2):
                            if (bi, side) in merged:
                                continue
                            kl0 = (li0 + 128 * (n_ + side)) * r + res
                            ks = kd[:, s, ssl(kl0, 128, r)]
                            col = (bi * 2 + side) * 128
                            grp.append((ps[:, col:col + 128], [(ks, qs)]))
                    k.mms(grp, [Rqd[s], Rkd[s]], RP2[ss])

                def dil_e(i):
                    u, ci = dsteps[i]
                    hg, g = units[u]
                    ss = i % 2
                    s3_ = i % 3
                    k.act(Pd[:, ss, :], P2[s3_][:, :], AF.Exp, RP2[s3_], [RPd[ss]], scale=0.125)
                    if g == 0:
                        mk = 1 if ci == 0 else (2 if ci == 7 else 0)
                    elif g == 1:
                        mk = 1 if ci % 2 == 0 else 2
                    else:
                        mk = 3
                    k.tt("dve", Pm[:, ss, 0:512], Pd[:, ss, 0:512], mask[:, mk, 0:512], ALU.mult, [RPd[ss], Rmask], [RPmh[ss][0]])
                    k.tt("dve", Pm[:, ss, 512:1024], Pd[:, ss, 512:1024], mask[:, mk, 512:1024], ALU.mult, [RPd[ss], Rmask], [RPmh[ss][1]])

                def dil_p(i):
                    u, ci = dsteps[i]
                    hg, g = units[u]
                    s, ss, sn = u % 2, i % 2, hg % 2
                    r, nbr, ntr, li0 = geom(g)
                    blocks = blocks_of(g, ci)
                    acc = P1[ss]
                    grp = []
                    for bi, (res, n_) in enumerate(blocks):
                        pairs = []
                        for side in range(2):
                            col = (bi * 2 + side) * 128
                            pairs.append((Vd[:, s, res * ntr + n_ + side, :], Pm[:, ss, col:col + 128]))
                        grp.append((acc[0:65, bi * 128:(bi + 1) * 128], pairs))
                    k.mms(grp, [RVd[s]] + RPmh[ss], [RP1[ss]])
                    runs = []
                    if g < 2:
                        res, n0 = blocks[0]
                        o0 = (128 * n0) * r + res
                        runs.append((ssl(o0, 512, r), slice(0, 512)))
                    else:
                        for j in range(2):
                            res, n0 = blocks[2 * j]
                            runs.append((ssl(res, 256, r), slice(j * 256, (j + 1) * 256)))
                    for osl, asl in runs:
                        if g == 0:
                            k.copy("dve", nd[:, sn, osl], acc[0:65, asl], [RP1[ss]], [Rnd[sn]])
                        else:
                            k.tt("dve", nd[:, sn, osl], nd[:, sn, osl], acc[0:65, asl], ALU.add, [RP1[ss], Rnd[sn]], [Rnd[sn]])

                def dil_final_a(hg):
                    sn = hg % 2
                    k.dma("sp", RD1_d[0:1, :], nd[64:65, sn, :], [Rnd[sn]], [Rrd1])
                    k.dma("sp", dsp[:], RD1_d.rearrange("o (p f) -> (o p) f", p=64), [Rrd1], [Rdsp])
                    k.recip(dsp[:], dsp[:], [Rdsp], [Rdsp])
                    k.dma("sp", RD2_d.rearrange("o (p f) -> (o p) f", p=64), dsp[:], [Rdsp], [Rrd2])
                    k.dma("sp", bcf[:], RD2_d[0:1, :].partition_broadcast(64), [Rrd2], [Rbcf])

                def dil_final_b(hg, qc):
                    sn = hg % 2
                    qsl = slice(qc * 512, (qc + 1) * 512)
                    so = qc % 2
                    k.tt("dve", tn[:, so, :], nd[0:64, sn, qsl], bcf[:, qsl], ALU.mult, [Rnd[sn], Rbcf], [Rtn[so]])
                    k.tt("pool", ost[:, so, :], tn[:, so, :], szd[:, sn, qsl], ALU.mult, [Rtn[so], Rszd[sn]], [Rost[so]])
                    k.dma("pool", GDT_d[hg * 64:(hg + 1) * 64, qsl], ost[:, so, :], [Rost[so]], [R_d["GDT"]])

                dil_loads(0)
                dil_pad(0)
                pending = None
                fin_slots = {(0, 6): 0, (0, 7): 1, (1, 1): 2, (1, 2): 3, (1, 3): 4, (1, 5): 5, (1, 6): 6, (1, 7): 7}
                for i, (u, ci) in enumerate(dsteps):
                    hg, g = units[u]
                    if ci == 0 and u + 1 < len(units):
                        dil_loads(u + 1)
                    if ci == 4 and u + 1 < len(units):
                        dil_pad(u + 1)
                    if i == 0:
                        dil_qk(0)
                        dil_qk(1)
                        dil_e(0)
                    if i + 2 < len(dsteps):
                        dil_qk(i + 2)
                    if i + 1 < len(dsteps):
                        dil_e(i + 1)
                    dil_p(i)
                    if g == 2 and ci == 7:
                        pending = hg
                    elif g == 0 and ci == 1 and pending is not None:
                        dil_final_a(pending)
                    elif pending is not None and (g, ci) in fin_slots:
                        dil_final_b(pending, fin_slots[(g, ci)])
                        if fin_slots[(g, ci)] == 7:
                            pending = None
                dil_final_a(pending)
                for qc in range(8):
                    dil_final_b(pending, qc)
                k.flush()

        if "F" in run:
            with ExitStack() as ph:
                wstg = sb(ph, "wstg", [128, 8, D], F32)
                wom = sb(ph, "wom", [128, 4, D], BF16)
                wod = sb(ph, "wod", [128, 4, D], BF16)
                wout = sb(ph, "wout", [128, 8, D], BF16)
                fg = sb(ph, "fg", [128, D], F32)
                gmt = sb(ph, "gmt", [128, 2, 4, 512], BF16)
                gdt = sb(ph, "gdt", [128, 2, 4, 512], BF16)
                gm = sb(ph, "gm", [128, 2, 8, 512], BF16)
                gd = sb(ph, "gd", [128, 2, 8, 512], BF16)
                m1 = sb(ph, "m1", [128, 2, 512], F32)
                m2 = sb(ph, "m2", [128, 2, 512], F32)
                mg = sb(ph, "mg", [128, 2, 8, 512], BF16)
                xt = sb(ph, "xtf", [128, 2, D], F32)
                rr = sb(ph, "rr", [128, 2, D], F32)
                junk = sb(ph, "junkf", [128, D], BF16)
                st = sb(ph, "stf", [128, 2, 4], F32)
                ot = sb(ph, "ot", [128, 2, D], F32)
                Rwstg, Rwom, Rwod, Rwout, Rfg = Reg(), Reg(), Reg(), Reg(), Reg()
                Rwout2 = Reg()
                Rgmt, Rgdt, Rgm, Rgd = [Reg(), Reg()], [Reg(), Reg()], [Reg(), Reg()], [Reg(), Reg()]
                Rmgt = [[Reg() for _ in range(8)] for _ in range(2)]
                Rm1, Rm2, Rjunk = [Reg(), Reg()], [Reg(), Reg()], Reg()
                Rxt, Rrr, Rst, Rot = [Reg(), Reg()], [Reg(), Reg()], [Reg(), Reg()], [Reg(), Reg()]
                wstg2 = sb(ph, "wstg2", [128, 8, D], F32)
                Rwa, Rwb, Rw2 = Reg(), Reg(), Reg()
                k.dma("sp", wstg[:, 0:4, :], wom_d.rearrange("(h p) n -> p h n", p=128), (), [Rwa])
                k.dma("sp", wstg[:, 4:8, :], wod_d.rearrange("(h p) n -> p h n", p=128), (), [Rwb])
                k.dma("sp", wstg2[:], wout_d.rearrange("(k p) n -> p k n", p=128), (), [Rw2])
                k.act(wom[:], wstg[:, 0:4, :], AF.Copy, [Rwa], [Rwom])
                k.copy("dve", wod[:], wstg[:, 4:8, :], [Rwb], [Rwod])
                k.act(wout[:, 0:4, :], wstg2[:, 0:4, :], AF.Copy, [Rw2], [Rwout])
                k.copy("dve", wout[:, 4:8, :], wstg2[:, 4:8, :], [Rw2], [Rwout2])
                k.dma("sp", fg[:], fg_d[:, :], (), [Rfg])
                npz = 0
                ntile = 0
                def f_loads(qc):
                    s = qc % 2
                    qsl = slice(qc * 512, (qc + 1) * 512)
                    k.dma("sp", gmt[:, s], GMT_d[:, qsl].rearrange("(h p) t -> p h t", p=128), [R_d["GMT"]], [Rgmt[s]])
                    k.dma("sp", gdt[:, s], GDT_d[:, qsl].rearrange("(h p) t -> p h t", p=128), [R_d["GDT"]], [Rgdt[s]])
                    k.dma("sp", gm[:, s], GM_d[:, qsl].rearrange("(h p) t -> p h t", p=128), [R_d["GM"]], [Rgm[s]])
                    k.dma("sp", gd[:, s], GD_d[:, qsl].rearrange("(h p) t -> p h t", p=128), [R_d["GD"]], [Rgd[s]])

                def f_first(qc):
                    s = qc % 2
                    for dt_ in range(8):
                        pi = fcnt[0] % 2
                        fcnt[0] += 1
                        pa = P2[pi]
                        dsl = slice(dt_ * 128, (dt_ + 1) * 128)
                        k.mms([(pa[:, 0:512], [(wom[:, h, dsl], gmt[:, s, h, :]) for h in range(4)]),
                               (pa[:, 512:1024], [(wod[:, h, dsl], gdt[:, s, h, :]) for h in range(4)])],
                              [Rwom, Rwod, Rgmt[s], Rgdt[s]], RP2[pi])
                        b_ = dt_ % 2
                        k.tt("dve", m1[:, b_, :], pa[:, 0:512], gm[:, s, dt_, :], ALU.mult, [RP2[pi][0], Rgm[s]], [Rm1[b_]])
                        k.tt("dve", m2[:, b_, :], pa[:, 512:1024], gd[:, s, dt_, :], ALU.mult, [RP2[pi][1], Rgd[s]], [Rm2[b_]])
                        k.tt("pool", mg[:, s, dt_, :], m1[:, b_, :], m2[:, b_, :], ALU.add, [Rm1[b_], Rm2[b_]], [Rmgt[s][dt_]])

                def f_out(qc):
                    s = qc % 2
                    for tt4 in range(4):
                        i = qc * 4 + tt4
                        s2 = i % 2
                        if s2 == 0:
                            halves = [(P2[2][:, 0:512], RP2[2][0]), (P2[2][:, 512:1024], RP2[2][1])]
                        else:
                            halves = [(P1[0][:], RP1[0]), (P1[1][:], RP1[1])]
                        tsl = slice(tt4 * 128, (tt4 + 1) * 128)
                        k.mms([(halves[0][0], [(mg[:, s, kk, tsl], wout[:, kk, 0:512]) for kk in range(8)]),
                               (halves[1][0], [(mg[:, s, kk, tsl], wout[:, kk, 512:1024]) for kk in range(8)])],
                              Rmgt[s] + [Rwout, Rwout2], [halves[0][1], halves[1][1]])
                        k.dma("sp", xt[:, s2, :], x_d[OWN0 + i * 128:OWN0 + (i + 1) * 128, :], (), [Rxt[s2]])
                        for hh in range(2):
                            k.tt("dve", rr[:, s2, hh * 512:(hh + 1) * 512], halves[hh][0], xt[:, s2, hh * 512:(hh + 1) * 512], ALU.add,
                                 [halves[hh][1], Rxt[s2]], [Rrr[s2]])
                        k.act(junk[:], rr[:, s2, :], AF.Square, [Rrr[s2]], [Rjunk, Rst[s2]], accum=st[:, s2, 0:1])
                        k.act(st[:, s2, 1:2], st[:, s2, 0:1], AF.Ln, [Rst[s2], R_c], [Rst[s2]], scale=1.0 / D, bias=vec[:, 33:34])
                        k.act(st[:, s2, 3:4], st[:, s2, 1:2], AF.Exp, [Rst[s2]], [Rst[s2]], scale=-0.5)
                        k.act(rr[:, s2, :], rr[:, s2, :], AF.Copy, [Rrr[s2], Rst[s2]], [Rrr[s2]], scale=st[:, s2, 3:4])
                        k.tt("pool", ot[:, s2, :], rr[:, s2, :], fg[:], ALU.mult, [Rrr[s2], Rfg], [Rot[s2]])
                        k.dma("pool", out_d[i * 128:(i + 1) * 128, :], ot[:, s2, :], [Rot[s2]], [R_d["out"]])

                fcnt = [0]
                f_loads(0)
                f_loads(1)
                f_first(0)
                for qc in range(8):
                    if qc + 1 < 8:
                        f_first(qc + 1)
                    f_out(qc)
                    if qc + 2 < 8:
                        f_loads(qc + 2)
                k.flush()
    return nc


def _rot_cols(w, head_dim):
    n = w.shape[1]
    half = head_dim // 2
    idx = np.arange(n)
    d = idx % head_dim
    src = np.where(d < half, idx + half, idx - half)
    return w[:, src]


def _masks(half):
    kk = np.arange(128)[:, None]
    qq = np.arange(128)[None, :]
    lo = (kk >= qq)
    hi = (kk <= qq)
    lo_first = lo & (kk >= 64) if half == 0 else lo
    hi_last = hi & (kk < 64) if half == 1 else hi

    def tile(pattern):
        return np.concatenate(pattern, axis=1)
    plain = tile([lo, hi] * 4)
    first = tile([lo_first, hi] + [lo, hi] * 3)
    last = tile([lo, hi] * 3 + [lo, hi_last])
    g3 = tile([lo_first, hi, lo, hi_last] * 2)
    m = np.concatenate([plain, first, last, g3], axis=1).astype(np.float32)
    return m.astype(ml_dtypes.bfloat16)


def prepare_inputs(x, positions, attn_norm_g, w_in, b_gate, mla_q_norm_g, mla_kv_norm_g,
                   w_uq, w_ukv, w_o_mla, w_o_dil, w_out, final_norm_g):
    f32 = np.float32
    x = np.asarray(x, f32)
    positions = np.asarray(positions, np.int32)
    w_in0 = np.asarray(w_in, f32)[0]
    w_in_ext = np.concatenate([w_in0, _rot_cols(w_in0[:, C_KR:C_KR + 32], 32)], axis=1)
    assert w_in_ext.shape[1] == WIN_EXT
    wuq0 = np.asarray(w_uq, f32)[0].reshape(256, 8, 96)
    rot = np.zeros_like(wuq0)
    rot[:, :, 64:96] = _rot_cols(wuq0[:, :, 64:96].reshape(256, 8 * 32), 32).reshape(256, 8, 32)
    wuq_ext = np.concatenate([wuq0.reshape(256, 768), rot.reshape(256, 768)], axis=1)
    wukv0 = np.asarray(w_ukv, f32)[0].reshape(128, 8, 128)
    wukv_ext = np.concatenate([wukv0[:, :, :64].reshape(128, 512), wukv0[:, :, 64:].reshape(128, 512)], axis=1)
    p = np.arange(128)
    vec = np.zeros((128, 40), f32)
    vec[:, 0:8] = np.asarray(attn_norm_g, f32)[0].reshape(8, 128).T
    vec[:, 8:24] = np.asarray(b_gate, f32)[0].reshape(16, 128).T
    vec[:, 24:26] = np.asarray(mla_q_norm_g, f32)[0].reshape(2, 128).T
    vec[:, 26] = np.asarray(mla_kv_norm_g, f32)[0]
    invD = (10000.0 ** (-(2.0 * (p % 32)) / 64.0)).astype(f32)
    invM = (10000.0 ** (-(2.0 * (p % 16)) / 32.0)).astype(f32)
    vec[:, 27] = (invD.astype(np.float64) / (2 * math.pi)).astype(f32)
    vec[:, 28] = (invM.astype(np.float64) / (2 * math.pi)).astype(f32)
    vec[:, 29] = TWO_PI_S
    vec[:, 30] = np.where((p % 64) < 32, -TWO_PI_S, TWO_PI_S)
    vec[:, 31] = np.where((p % 32) < 16, -TWO_PI_S, TWO_PI_S)
    vec[:, 33] = EPS
    pm = np.clip(p - 64, 0, 31)
    invC = np.where(p < 64, invD, np.where(p < 96, (10000.0 ** (-(2.0 * (pm % 16)) / 32.0)), 0.0))
    vec[:, 34] = (invC.astype(np.float64) / (2 * math.pi)).astype(f32)
    vec[:, 35] = np.where(p < 64, np.where(p < 32, -TWO_PI_S, TWO_PI_S), np.where((p < 96) & (pm < 16), -TWO_PI_S, TWO_PI_S))
    vec[:, 32] = TWO_PI_S / 4.0
    fg = np.ascontiguousarray(np.broadcast_to(np.asarray(final_norm_g, f32)[None, :], (128, D)))
    ident = np.eye(128, dtype=f32)
    dd = np.arange(128)
    srcp = np.where((dd % 64) < 32, dd + 32, dd - 32)
    permd = np.zeros((128, 128), f32)
    permd[srcp, dd] = 1.0
    permd = permd.astype(ml_dtypes.bfloat16)
    shared = {"w_in": np.ascontiguousarray(w_in_ext), "w_uq": np.ascontiguousarray(wuq_ext),
              "w_ukv": np.ascontiguousarray(wukv_ext), "w_o_mla": np.asarray(w_o_mla, f32)[0],
              "w_o_dil": np.asarray(w_o_dil, f32)[0], "w_out": np.asarray(w_out, f32)[0],
              "vecs": vec, "fg": fg, "ident": ident, "permd": permd}
    in_maps = []
    for c in range(8):
        b, half = divmod(c, 2)
        shift = (half * NOWN - OWN0) % S
        m = dict(shared)
        m["x"] = np.ascontiguousarray(np.roll(x[b], -shift, axis=0))
        m["pos"] = np.ascontiguousarray(np.roll(positions[b], -shift)[None, :])
        m["masks"] = _masks(half)
        in_maps.append(m)
    return in_maps


_NC_CACHE = {}


def kernel(x, positions, attn_norm_g, w_in, b_gate, mla_q_norm_g, mla_kv_norm_g,
           w_uq, w_ukv, w_o_mla, w_o_dil, w_out, final_norm_g):
    in_maps = prepare_inputs(x, positions, attn_norm_g, w_in, b_gate, mla_q_norm_g, mla_kv_norm_g,
                             w_uq, w_ukv, w_o_mla, w_o_dil, w_out, final_norm_g)
    nc = build()
    res = run_bass_kernel_spmd(nc, in_maps, core_ids=list(range(8)))
    out = np.zeros((4, S, D), np.float32)
    for c in range(8):
        b, half = divmod(c, 2)
        out[b, half * NOWN:(half + 1) * NOWN] = np.asarray(res.results[c]["out"], np.float32)
    return out
```

```python
import math
from contextlib import ExitStack

import numpy as np
import ml_dtypes
import concourse.bass as bass
import concourse.mybir as mybir
from concourse.bass_utils import run_bass_kernel_spmd

F32 = mybir.dt.float32
BF16 = mybir.dt.bfloat16
I32 = mybir.dt.int32
AF = mybir.ActivationFunctionType
ALU = mybir.AluOpType

S = 8192
D = 1024
OWN0 = 1024
NOWN = 4096
NLOC = 6144
EPS = 1e-6
TWO_PI_S = 6.28318
DIL = (1, 4, 16)
NDSEM = 48

C_CQ, C_CKV, C_KR, C_ZM, C_QD, C_KD, C_VD, C_ZD, C_GM, C_GD = 0, 256, 384, 416, 928, 2464, 4000, 5536, 6048, 7072
C_KR_ROT = 8096
WIN_EXT = 8128


def ssl(start, n, step):
    return slice(start, start + (n - 1) * step + 1, step)


class Reg:
    __slots__ = ("w", "r", "free_w")

    def __init__(self, free_w=False):
        self.w = {}
        self.r = {}
        self.free_w = free_w


class K:
    ENG = ("pe", "act", "dve", "pool", "sp")

    def __init__(self, nc, es):
        self.nc = nc
        self.streams = {e: [] for e in self.ENG}
        self.esem = {e: es.enter_context(nc.semaphore("sem_" + e)) for e in self.ENG}
        self.ecnt = {e: 0 for e in self.ENG}
        self.waited = {e: {} for e in self.ENG}
        self.dsem = [es.enter_context(nc.semaphore("dsem%d" % i)) for i in range(NDSEM)]
        self.dcnt = [0] * NDSEM
        self.dnext = {"sp": 0, "pool": NDSEM // 2}
        self.semobj = {}
        for e in self.ENG:
            self.semobj[id(self.esem[e])] = self.esem[e]
        for s in self.dsem:
            self.semobj[id(s)] = s

    def _deps(self, eng, reads, writes, extra=()):
        need = {}
        for r in reads:
            for k, v in r.w.items():
                need[k] = max(need.get(k, 0), v)
        for r in writes:
            if r.free_w:
                continue
            for k, v in r.w.items():
                need[k] = max(need.get(k, 0), v)
            for k, v in r.r.items():
                need[k] = max(need.get(k, 0), v)
        for k, v in extra:
            need[k] = max(need.get(k, 0), v)
        out = []
        wd = self.waited[eng]
        own = id(self.esem[eng])
        for k, v in need.items():
            if eng == "pe" and k == own:
                continue
            if wd.get(k, 0) >= v:
                continue
            wd[k] = v
            out.append((self.semobj[k], v))
        return out

    def _mark(self, key, val, reads, writes):
        for r in reads:
            r.r[key] = max(r.r.get(key, 0), val)
        for r in writes:
            r.w[key] = max(r.w.get(key, 0), val)

    def op(self, eng, fn, reads=(), writes=()):
        waits = self._deps(eng, reads, writes)
        sem = self.esem[eng]
        self.ecnt[eng] += 1
        val = self.ecnt[eng]

        def emit(e, fn=fn, waits=waits, sem=sem):
            for s, v in waits:
                e.wait_ge(s, v)
            fn(e).then_inc(sem, 1)

        self.streams[eng].append(emit)
        self._mark(id(sem), val, reads, writes)

    def dma(self, q, out, in_, reads=(), writes=()):
        i = self.dnext[q]
        half = NDSEM // 2
        base = 0 if q == "sp" else half
        self.dnext[q] = base + (i - base + 1) % half
        sem = self.dsem[i]
        prev = self.dcnt[i]
        self.dcnt[i] += 16
        val = self.dcnt[i]
        extra = [(id(sem), prev)] if prev else []
        waits = self._deps(q, reads, writes, extra)

        def emit(e, waits=waits, sem=sem, out=out, in_=in_):
            for s, v in waits:
                e.wait_ge(s, v)
            e.dma_start(out=out, in_=in_).then_inc(sem, 16)

        self.streams[q].append(emit)
        self._mark(id(sem), val, reads, writes)

    def barrier(self):
        tgt = [(id(self.esem[e]), self.ecnt[e]) for e in self.ENG if self.ecnt[e]]
        tgt += [(id(self.dsem[i]), self.dcnt[i]) for i in range(NDSEM) if self.dcnt[i]]
        for eng in self.ENG:
            waits = self._deps(eng, (), (), tgt)

            def emit(e, waits=waits):
                for s, v in waits:
                    e.wait_ge(s, v)

            self.streams[eng].append(emit)

    def flush(self):
        self.barrier()
        streams = self.streams
        self.streams = {e: [] for e in self.ENG}
        with self.nc.Block() as blk:
            def run(name):
                def body(e):
                    for f in streams[name]:
                        f(e)
                return body
            blk.tensor(run("pe"))
            blk.scalar(run("act"))
            blk.vector(run("dve"))
            blk.gpsimd(run("pool"))
            blk.sync(run("sp"))

    def copy(self, eng, out, in_, R=(), W=()):
        self.op(eng, lambda e: e.tensor_copy(out=out, in_=in_), R, W)

    def memset(self, eng, ap, val, W=()):
        self.op(eng, lambda e: e.memset(ap, val), (), W)

    def ts(self, eng, out, in0, s1, s2, op0, op1, R=(), W=()):
        if op1 is None:
            self.op(eng, lambda e: e.tensor_scalar(out=out, in0=in0, scalar1=s1, scalar2=None, op0=op0), R, W)
        else:
            self.op(eng, lambda e: e.tensor_scalar(out=out, in0=in0, scalar1=s1, scalar2=s2, op0=op0, op1=op1), R, W)

    def tt(self, eng, out, in0, in1, op, R=(), W=()):
        self.op(eng, lambda e: e.tensor_tensor(out=out, in0=in0, in1=in1, op=op), R, W)

    def stt(self, eng, out, in0, scalar, in1, op0, op1, R=(), W=()):
        self.op(eng, lambda e: e.scalar_tensor_tensor(out=out, in0=in0, scalar=scalar, in1=in1, op0=op0, op1=op1), R, W)

    def recip(self, out, in_, R=(), W=()):
        self.op("dve", lambda e: e.reciprocal(out=out, in_=in_), R, W)

    def act(self, out, in_, func, R=(), W=(), bias=None, scale=None, accum=None):
        kw = {}
        if bias is not None:
            kw["bias"] = bias
        if scale is not None:
            kw["scale"] = scale
        if accum is not None:
            kw["accum_out"] = accum
        self.op("act", lambda e: e.activation(out=out, in_=in_, func=func, **kw), R, W)

    def mm(self, ps, pairs, R=(), W=()):
        def fn(e):
            n = len(pairs)
            ins = None
            for i, (l, r) in enumerate(pairs):
                ins = e.matmul(ps, lhsT=l, rhs=r, start=(i == 0), stop=(i == n - 1))
            return ins
        self.op("pe", fn, R, W)

    def mms(self, groups, R=(), W=()):
        def fn(e):
            ins = None
            for ps, pairs in groups:
                n = len(pairs)
                for i, (l, r) in enumerate(pairs):
                    ins = e.matmul(ps, lhsT=l, rhs=r, start=(i == 0), stop=(i == n - 1))
            return ins
        self.op("pe", fn, R, W)

    def trs(self, items, ident, R=(), W=()):
        def fn(e):
            ins = None
            for o, i in items:
                ins = e.transpose(o, i, ident)
            return ins
        self.op("pe", fn, R, W)


def build(stop_after=None, debug=False):
    nc = bass.Bass("TRN2", target_bir_lowering=False)
    kind_dbg = "ExternalOutput" if debug else "Internal"

    def din(name, shape, dt):
        return nc.dram_tensor(name, shape, dt, kind="ExternalInput").ap()

    def dscr(name, shape, dt=BF16):
        return nc.dram_tensor(name, shape, dt, kind=kind_dbg).ap()

    x_d = din("x", [S, D], F32)
    pos_d = din("pos", [1, S], I32)
    win_d = din("w_in", [D, WIN_EXT], F32)
    wuq_d = din("w_uq", [256, 2 * 768], F32)
    wukv_d = din("w_ukv", [128, 1024], F32)
    wom_d = din("w_o_mla", [512, D], F32)
    wod_d = din("w_o_dil", [512, D], F32)
    wout_d = din("w_out", [D, D], F32)
    vec_d = din("vecs", [128, 40], F32)
    fg_d = din("fg", [128, D], F32)
    ident_d = din("ident", [128, 128], F32)
    mask_d = din("masks", [128, 4 * 1024], BF16)
    perm_d = din("permd", [128, 128], BF16)
    out_d = nc.dram_tensor("out", [NOWN, D], F32, kind="ExternalOutput").ap()

    TAB = dscr("TAB", [4, 128, S], F32)
    KVN_d = dscr("KVN", [128, S])
    KPE_d = dscr("KPE", [32, S])
    CQN_d = dscr("CQN", [256, NOWN])
    SZM_d = dscr("SZM", [512, NOWN])
    SZD_d = dscr("SZD", [512, NOWN])
    QD_d = dscr("QD", [1536, NOWN])
    KD_d = dscr("KD", [1536, NLOC])
    VD_d = dscr("VD", [24, 128, 48 * 64])
    GM_d = dscr("GM", [1024, NOWN])
    GD_d = dscr("GD", [1024, NOWN])
    KM_d = dscr("KM", [512, S])
    VM_d = dscr("VM", [S, 512])
    QM_d = dscr("QM", [768, NOWN])
    GMT_d = dscr("GMT", [512, NOWN])
    GDT_d = dscr("GDT", [512, NOWN])
    RR_d = nc.dram_tensor("RRS", [2, 512], F32).ap()
    RD1_d = nc.dram_tensor("RD1", [1, NOWN], F32).ap()
    RD2_d = nc.dram_tensor("RD2", [1, NOWN], F32).ap()
    R_d = {n: Reg(free_w=True) for n in ("TAB", "KVN", "KPE", "CQN", "SZM", "SZD", "QD", "KD", "VD", "GM", "GD", "KM", "VM", "QM", "GMT", "GDT", "out")}

    phases = ["T", "H", "A", "M0", "MLA", "DIL", "F"]
    nph = len(phases) if stop_after is None else phases.index(stop_after) + 1
    run = set(phases[:nph])

    with ExitStack() as es:
        k = K(nc, es)

        def sb(stack, name, shape, dt):
            return stack.enter_context(nc.sbuf_tensor("s_" + name, shape, dt))

        vec = sb(es, "vec", [128, 40], F32)
        ident = sb(es, "ident", [128, 128], F32)
        ones = sb(es, "ones", [128, 128], F32)
        R_c = Reg()
        k.dma("sp", vec[:], vec_d[:, :], (), [R_c])
        k.dma("sp", ident[:], ident_d[:, :], (), [R_c])
        k.memset("dve", ones[:], 1.0, [R_c])
        P2 = [es.enter_context(nc.psum_tensor("p2_%d" % i, [128, 1024], F32)) for i in range(3)]
        P1 = [es.enter_context(nc.psum_tensor("p1_%d" % i, [128, 512], F32)) for i in range(2)]
        RP2 = [[Reg(), Reg()] for _ in range(3)]
        RP1 = [Reg(), Reg()]

        if "T" in run:
            with ExitStack() as ph:
                posi = sb(ph, "posi", [128, 2, 2048], I32)
                posf = sb(ph, "posf", [128, 2048], F32)
                tt_ = sb(ph, "tt", [128, 2048], F32)
                ti_ = sb(ph, "ti", [128, 2048], I32)
                tf_ = sb(ph, "tf", [128, 2048], F32)
                fr_ = sb(ph, "fr", [128, 2048], F32)
                so_ = sb(ph, "so", [128, 2, 2048], F32)
                Rposi = [Reg(), Reg()]
                Rposf, Rtt, Rti, Rtf, Rfr = Reg(), Reg(), Reg(), Reg(), Reg()
                Rso = [Reg(), Reg()]
                n = 0
                fa_ = sb(ph, "fa", [128, 2048], F32)
                Rfa = Reg()
                for c in range(4):
                    sl = slice(c * 2048, (c + 1) * 2048)
                    k.dma("sp", posi[:, c % 2, :], pos_d[0:1, sl].partition_broadcast(128), (), [Rposi[c % 2]])
                    k.copy("dve", posf[:], posi[:, c % 2, :], [Rposi[c % 2]], [Rposf])
                    k.ts("dve", tt_[:], posf[:], vec[:, 34:35], None, ALU.mult, None, [Rposf, R_c], [Rtt])
                    k.copy("dve", ti_[:], tt_[:], [Rtt], [Rti])
                    k.copy("dve", tf_[:], ti_[:], [Rti], [Rtf])
                    k.tt("dve", fr_[:], tt_[:], tf_[:], ALU.subtract, [Rtt, Rtf], [Rfr])
                    k.stt("dve", fa_[:], fr_[:], -1.0, fr_[:], ALU.mult, ALU.max, [Rfr], [Rfa])
                    for kind_ in range(2):
                        s = n % 2
                        n += 1
                        if kind_ == 0:
                            k.act(so_[:, s, :], fa_[:], AF.Sin, [Rfa], [Rso[s]], scale=-TWO_PI_S, bias=vec[:, 32:33])
                        else:
                            k.act(so_[:, s, :], fr_[:], AF.Sin, [Rfr, R_c], [Rso[s]], scale=vec[:, 35:36])
                        k.dma("pool", TAB[kind_, 0:64, sl], so_[0:64, s, :], [Rso[s]], [R_d["TAB"]])
                        k.dma("pool", TAB[kind_, 64:128, sl], so_[0:64, s, :], [Rso[s]], [R_d["TAB"]])
                        k.dma("pool", TAB[2 + kind_, 0:32, sl], so_[64:96, s, :], [Rso[s]], [R_d["TAB"]])
                k.flush()

        if "H" in run:
            with ExitStack() as phH:
                hT = sb(phH, "hT", [128, 8, NLOC], BF16)
                RhT = [[Reg() for _ in range(4)] for _ in range(12)]
                with ExitStack() as ph:
                    xt = sb(ph, "xt", [128, 4, D], F32)
                    junk = sb(ph, "junk", [128, D], BF16)
                    xn = sb(ph, "xn", [128, 2, D], F32)
                    st = sb(ph, "st", [128, 3, 4], F32)
                    hTo = sb(ph, "hTo", [128, 2, 8, 512], BF16)
                    gfull = sb(ph, "gfull", [128, 8, 128], F32)
                    wkvs = sb(ph, "wkvs", [128, 8, 192], F32)
                    wkv = sb(ph, "wkv", [128, 8, 192], BF16)
                    sq = sb(ph, "sq", [128, 512], F32)
                    ms = sb(ph, "ms", [128, 512], F32)
                    rs = sb(ph, "rs", [128, 512], F32)
                    tabm = sb(ph, "tabm", [32, 2, 2, 512], F32)
                    t1 = sb(ph, "t1", [32, 512], F32)
                    t2 = sb(ph, "t2", [32, 512], F32)
                    t2p = sb(ph, "t2p", [32, 512], F32)
                    Rt2p = Reg()
                    kvst = sb(ph, "kvst", [128, 2, 512], BF16)
                    kpst = sb(ph, "kpst", [32, 2, 512], BF16)
                    Rxt = [Reg() for _ in range(4)]
                    Rjunk, Rsq, Rms, Rrs, Rt1, Rt2, Rw = Reg(), Reg(), Reg(), Reg(), Reg(), Reg(), Reg()
                    Rxn = [Reg(), Reg()]
                    Rst = [Reg(), Reg(), Reg()]
                    RhTo = [[Reg() for _ in range(4)] for _ in range(2)]
                    Rgf = Reg()
                    for kk in range(8):
                        k.ts("pool", gfull[:, kk, :], ones[:], vec[:, kk:kk + 1], None, ALU.mult, None, [R_c], [Rgf])
                    Rtabm = [Reg(), Reg()]
                    Rkvst = [Reg(), Reg()]
                    Rkpst = [Reg(), Reg()]
                    wv_ = win_d.rearrange("(k p) n -> p k n", p=128)
                    k.dma("sp", wkvs[:, :, 0:160], wv_[:, :, C_CKV:C_CKV + 160], (), [Rw])
                    k.dma("sp", wkvs[:, :, 160:192], wv_[:, :, C_KR_ROT:C_KR_ROT + 32], (), [Rw])
                    k.copy("dve", wkv[:], wkvs[:], [Rw], [Rw])
                    def stage_a(i):
                        s4, s3 = i % 4, i % 3
                        k.dma("sp", xt[:, s4, :], x_d[i * 128:(i + 1) * 128, :], (), [Rxt[s4]])
                        k.act(junk[:], xt[:, s4, :], AF.Square, [Rxt[s4]], [Rjunk, Rst[s3]], accum=st[:, s3, 0:1])
                        k.act(st[:, s3, 1:2], st[:, s3, 0:1], AF.Ln, [Rst[s3]], [Rst[s3]], scale=1.0 / D, bias=vec[:, 33:34])

                    def stage_b(i):
                        s3 = i % 3
                        k.act(st[:, s3, 3:4], st[:, s3, 1:2], AF.Exp, [Rst[s3]], [Rst[s3]], scale=-0.5)

                    stage_a(0)
                    stage_a(1)
                    stage_b(0)
                    for i in range(64):
                        c, j = divmod(i, 4)
                        s4, s3, s2 = i % 4, i % 3, i % 2
                        if i + 2 < 64:
                            stage_a(i + 2)
                        k.act(xn[:, s2, :], xt[:, s4, :], AF.Copy, [Rxt[s4], Rst[s3]], [Rxn[s2]], scale=st[:, s3, 3:4])
                        pt = P2[s2]
                        k.trs([(pt[:, kk * 128:(kk + 1) * 128], xn[:, s2, kk * 128:(kk + 1) * 128]) for kk in range(8)],
                              ident[:], [Rxn[s2], R_c], RP2[s2])
                        if i + 1 < 64:
                            stage_b(i + 1)
                        if c < 12:
                            dst3 = hT[:, :, i * 128:(i + 1) * 128]
                            Rdst = RhT[c][j]
                        else:
                            dst3 = hTo[:, c % 2, :, j * 128:(j + 1) * 128]
                            Rdst = RhTo[c % 2][j]
                        k.tt("dve", dst3, pt[:, :].rearrange("p (a b) -> p a b", a=8), gfull[:], ALU.mult, RP2[s2] + [Rgf], [Rdst])
                        if j != 3:
                            continue
                        sl = slice(c * 512, (c + 1) * 512)
                        s = c % 2
                        if c < 12:
                            src = lambda kk: hT[:, kk, sl]
                        else:
                            src = lambda kk, s=s: hTo[:, s, kk, :]
                        Rsrc = RhT[c] if c < 12 else RhTo[s]
                        pc, pk, pss = P2[2], P1[0], P1[1]
                        k.mm(pc[:, 0:512], [(wkv[:, kk, 0:128], src(kk)) for kk in range(8)], [Rw] + Rsrc, [RP2[2][0]])
                        k.mm(pk[0:32, :], [(wkv[:, kk, 128:160], src(kk)) for kk in range(8)], [Rw] + Rsrc, [RP1[0]])
                        k.act(sq[:], pc[:, 0:512], AF.Square, [RP2[2][0]], [Rsq])
                        k.mm(pss[:], [(ones[:], sq[:])], [Rsq, R_c], [RP1[1]])
                        k.act(ms[:], pss[:], AF.Ln, [RP1[1], R_c], [Rms], scale=1.0 / 128, bias=vec[:, 33:34])
                        k.act(rs[:], ms[:], AF.Exp, [Rms], [Rrs], scale=-0.5)
                        k.stt("dve", kvst[:, s, :], pc[:, 0:512], vec[:, 26:27], rs[:], ALU.mult, ALU.mult,
                              [RP2[2][0], Rrs, R_c], [Rkvst[s]])
                        k.dma("pool", KVN_d[:, sl], kvst[:, s, :], [Rkvst[s]], [R_d["KVN"]])
                        k.dma("sp", tabm[:, s, 0, :], TAB[2, 0:32, sl], [R_d["TAB"]], [Rtabm[s]])
                        k.dma("sp", tabm[:, s, 1, :], TAB[3, 0:32, sl], [R_d["TAB"]], [Rtabm[s]])
                        k.tt("dve", t1[:], pk[0:32, :], tabm[:, s, 0, :], ALU.mult, [RP1[0], Rtabm[s]], [Rt1])
                        k.tt("dve", t2[:], pk[0:32, :], tabm[:, s, 1, :], ALU.mult, [RP1[0], Rtabm[s]], [Rt2])
                        k.dma("pool", t2p[0:16, :], t2[16:32, :], [Rt2], [Rt2p])
                        k.dma("pool", t2p[16:32, :], t2[0:16, :], [Rt2], [Rt2p])
                        k.tt("dve", kpst[:, s, :], t1[:], t2p[:], ALU.subtract, [Rt1, Rt2p], [Rkpst[s]])
                        k.dma("pool", KPE_d[:, sl], kpst[:, s, :], [Rkpst[s]], [R_d["KPE"]])
                    k.flush()

                if "A" in run:
                    with ExitStack() as ph:
                        wst = sb(ph, "wst", [128, 3, 8, 128], F32)
                        wt = sb(ph, "wt", [128, 2, 8, 8, 128], BF16)
                        tabd = sb(ph, "tabd", [128, 2, 2, 512], F32)
                        sqa = sb(ph, "sqa", [128, 2, 2, 512], F32)
                        msa = sb(ph, "msa", [128, 2, 512], F32)
                        rsa = sb(ph, "rsa", [128, 2, 512], F32)
                        ta = sb(ph, "ta", [128, 2, 2, 512], F32)
                        ost = sb(ph, "ost", [128, 4, 512], BF16)
                        Rwst = [Reg() for _ in range(3)]
                        Rwt = [Reg(), Reg()]
                        Rtabd = [Reg(), Reg()]
                        Rsqa, Rmsa, Rrsa = [[Reg(), Reg()], [Reg(), Reg()]], [Reg(), Reg()], [Reg(), Reg()]
                        Rta = [[Reg(), Reg()], [Reg(), Reg()]]
                        Rost = [Reg() for _ in range(4)]
                        wv_ = win_d.rearrange("(k p) n -> p k n", p=128)
                        own_chunks = list(range(2, 10))
                        groups = [("cq", [C_CQ, C_CQ + 128], [], own_chunks, CQN_d, OWN0, None)]
                        groups.append(("silu", [C_ZM + 128 * t for t in range(4)], [], own_chunks, SZM_d, OWN0, None))
                        for g in range(3):
                            groups.append(("rope", [C_QD + 512 * g + 128 * t for t in range(4)], [], own_chunks, QD_d, OWN0, 4 * g))
                        for g in range(3):
                            chs = list(range(1, 11)) if g < 2 else list(range(0, 12))
                            groups.append(("rope", [C_KD + 512 * g + 128 * t for t in range(4)], [], chs, KD_d, 0, 4 * g))
                        groups.append(("silu", [C_ZD + 128 * t for t in range(4)], [], own_chunks, SZD_d, OWN0, None))
                        groups.append(("sig", [C_GM + 128 * t for t in range(8)], [], own_chunks, GM_d, OWN0, 8))
                        groups.append(("sig", [C_GD + 128 * t for t in range(8)], [], own_chunks, GD_d, OWN0, 16))
                        no = 0
                        npz = 0
                        for g in range(3):
                            groups.append(("vd", [C_VD + 512 * g + 128 * t for t in range(4)], [], None, VD_d, 0, g))
                        nwc = [0]
                        nvs = [0]
                        cnt = [0]
                        qb = sb(ph, "qb", [128, 3, 512], BF16)
                        Rqb = [Reg() for _ in range(3)]
                        permd = sb(ph, "permd", [128, 128], BF16)
                        Rpermd = Reg()
                        k.dma("sp", permd[:], perm_d[:, :], (), [Rpermd])
                        VTB = 6
                        vst = sb(ph, "vst", [128, 2, 8, VTB, 64], BF16)
                        Rvst = [[Reg() for _ in range(VTB)] for _ in range(2)]

                        def load_group(gi):
                            if gi >= len(groups):
                                return
                            ws = gi % 2
                            for ti, c0 in enumerate(groups[gi][1] + groups[gi][2]):
                                s3 = nwc[0] % 3
                                nwc[0] += 1
                                k.dma("sp", wst[:, s3], wv_[:, :, c0:c0 + 128], (), [Rwst[s3]])
                                if ti % 2 == 0:
                                    k.act(wt[:, ws, ti], wst[:, s3], AF.Copy, [Rwst[s3]], [Rwt[ws]])
                                else:
                                    k.copy("dve", wt[:, ws, ti], wst[:, s3], [Rwst[s3]], [Rwt[ws]])

                        load_group(0)
                        for gi, (kind, cols, rcols, chs, dst, tok0, extra) in enumerate(groups):
                            ws = gi % 2
                            load_group(gi + 1)
                            nt = len(cols)
                            if kind == "vd":
                                g = extra
                                r_ = DIL[g]
                                ntr_ = NOWN // r_ // 128 + 1
                                li0_ = OWN0 // r_ - 64
                                ntile_ = r_ * ntr_
                                allh = [x for cc in RhT for x in cc]
                                for t0 in range(0, ntile_, VTB):
                                    nb = min(VTB, ntile_ - t0)
                                    sv = nvs[0] % 2
                                    nvs[0] += 1
                                    for tb in range(nb):
                                        res_, m_ = divmod(t0 + tb, ntr_)
                                        tok = ssl((li0_ + 128 * m_) * r_ + res_, 128, r_)
                                        pi = npz % 3
                                        npz += 1
                                        pa = P2[pi]
                                        k.mm(pa[:, 0:512].rearrange("p (a b) -> p a b", a=4),
                                             [(hT[:, kk, tok], wt[:, ws, 0:4, kk, :]) for kk in range(8)],
                                             [Rwt[ws]] + allh, [RP2[pi][0]])
                                        src3 = pa[:, 0:512].rearrange("p (h d) -> p h d", h=8)
                                        if tb % 2 == 0:
                                            k.act(vst[:, sv, :, tb, :], src3, AF.Copy, [RP2[pi][0]], [Rvst[sv][tb]])
                                        else:
                                            k.copy("dve", vst[:, sv, :, tb, :], src3, [RP2[pi][0]], [Rvst[sv][tb]])
                                    for h_ in range(8):
                                        dstv = VD_d[g * 8 + h_].rearrange("p (t d) -> p t d", d=64)[:, t0:t0 + nb, :]
                                        k.dma("pool" if h_ % 2 else "sp", dstv, vst[:, sv, h_, 0:nb, :], Rvst[sv][0:nb], [R_d["VD"]])
                                continue
                            if kind == "cq":
                                def cq_main(j, ws=ws, chs=chs):
                                    c = chs[j]
                                    sl = slice(c * 512, (c + 1) * 512)
                                    pi = j % 3
                                    pa = P2[pi]
                                    b_ = j % 2
                                    k.mms([(pa[:, 0:512], [(wt[:, ws, 0, kk, :], hT[:, kk, sl]) for kk in range(8)]),
                                           (pa[:, 512:1024], [(wt[:, ws, 1, kk, :], hT[:, kk, sl]) for kk in range(8)])],
                                          [Rwt[ws]] + RhT[c], RP2[pi])
                                    k.act(sqa[:, b_, 0, :], pa[:, 0:512], AF.Square, [RP2[pi][0]], [Rsqa[b_][0]])
                                    k.act(sqa[:, b_, 1, :], pa[:, 512:1024], AF.Square, [RP2[pi][1]], [Rsqa[b_][1]])

                                def cq_fin(j, chs=chs, dst=dst, tok0=tok0):
                                    c = chs[j]
                                    dsl = slice(c * 512 - tok0, (c + 1) * 512 - tok0)
                                    pi = j % 3
                                    pa = P2[pi]
                                    b_ = j % 2
                                    k.mm(P1[b_][:], [(ones[:], sqa[:, b_, 0, :]), (ones[:], sqa[:, b_, 1, :])], Rsqa[b_] + [R_c], [RP1[b_]])
                                    k.ts("dve", msa[:, b_, :], P1[b_][:], 1.0 / 256, EPS, ALU.mult, ALU.add, [RP1[b_]], [Rmsa[b_]])
                                    k.act(msa[:, b_, :], msa[:, b_, :], AF.Sqrt, [Rmsa[b_]], [Rmsa[b_]])
                                    k.recip(rsa[:, b_, :], msa[:, b_, :], [Rmsa[b_]], [Rrsa[b_]])
                                    for t in range(2):
                                        so = cnt[0] % 4
                                        cnt[0] += 1
                                        k.stt("dve", ost[:, so, :], pa[:, t * 512:(t + 1) * 512], vec[:, 24 + t:25 + t], rsa[:, b_, :],
                                              ALU.mult, ALU.mult, [RP2[pi][t], Rrsa[b_], R_c], [Rost[so]])
                                        k.dma("sp", dst[t * 128:(t + 1) * 128, dsl], ost[:, so, :], [Rost[so]], [R_d["CQN"]])

                                cq_main(0)
                                for j in range(len(chs)):
                                    if j + 1 < len(chs):
                                        cq_main(j + 1)
                                    cq_fin(j)
                                continue
                            if kind == "rope":
                                items = [(c, t) for c in chs for t in range(nt)]
                                info = {}
                                dname = "QD" if dst is QD_d else "KD"

                                def rope_main(j, items=items, info=info, ws=ws, nt=nt, chs_all=chs):
                                    c, t = items[j]
                                    sl = slice(c * 512, (c + 1) * 512)
                                    s = c % 2
                                    nxt = []
                                    if j == 0:
                                        nxt = [c]
                                    elif t == 1:
                                        nxt = [cc for cc in chs_all if cc > c][:1]
                                    if nxt:
                                        for cc in nxt:
                                            sl2 = slice(cc * 512, (cc + 1) * 512)
                                            k.dma("sp", tabd[:, cc % 2, 0, :], TAB[0, :, sl2], [R_d["TAB"]], [Rtabd[cc % 2]])
                                            k.dma("sp", tabd[:, cc % 2, 1, :], TAB[1, :, sl2], [R_d["TAB"]], [Rtabd[cc % 2]])
                                    pi = cnt[0] % 3
                                    sq_ = cnt[0] % 3
                                    so = cnt[0] % 4
                                    cnt[0] += 1
                                    info[j] = (pi, sq_, so)
                                    pa = P2[pi]
                                    k.mm(pa[:, 0:512], [(wt[:, ws, t, kk, :], hT[:, kk, sl]) for kk in range(8)],
                                         [Rwt[ws]] + RhT[c], [RP2[pi][0]])
                                    k.act(qb[:, sq_, :], pa[:, 0:512], AF.Copy, [RP2[pi][0]], [Rqb[sq_]])

                                def rope_fin(j, items=items, info=info, dst=dst, tok0=tok0, extra=extra, dname=dname):
                                    c, t = items[j]
                                    pi, sq_, so = info[j]
                                    s = c % 2
                                    pa = P2[pi]
                                    dsl = slice(c * 512 - tok0, (c + 1) * 512 - tok0)
                                    row0 = 512 * (extra // 4) + 128 * t
                                    k.mm(pa[:, 512:1024], [(permd[:], qb[:, sq_, :])], [Rqb[sq_], Rpermd], [RP2[pi][1]])
                                    b_ = j % 2
                                    k.tt("dve", ta[:, b_, 0, :], pa[:, 0:512], tabd[:, s, 0, :], ALU.mult, [RP2[pi][0], Rtabd[s]], [Rta[b_][0]])
                                    k.tt("dve", ta[:, b_, 1, :], pa[:, 512:1024], tabd[:, s, 1, :], ALU.mult, [RP2[pi][1], Rtabd[s]], [Rta[b_][1]])
                                    k.tt("pool", ost[:, so, :], ta[:, b_, 0, :], ta[:, b_, 1, :], ALU.add, Rta[b_], [Rost[so]])
                                    k.dma("sp", dst[row0:row0 + 128, dsl], ost[:, so, :], [Rost[so]], [R_d[dname]])

                                rope_main(0)
                                for j in range(len(items)):
                                    if j + 1 < len(items):
                                        rope_main(j + 1)
                                    rope_fin(j)
                                no += len(items)
                                npz += len(items)
                                continue
                            for c in chs:
                                sl = slice(c * 512, (c + 1) * 512)
                                dsl = slice(c * 512 - tok0, (c + 1) * 512 - tok0)
                                rhs = [hT[:, kk, sl] for kk in range(8)]
                                if kind == "rope":
                                    s = c % 2
                                    k.dma("sp", tabd[:, s, 0, :], TAB[0, :, sl], [R_d["TAB"]], [Rtabd[s]])
                                    k.dma("sp", tabd[:, s, 1, :], TAB[1, :, sl], [R_d["TAB"]], [Rtabd[s]])
                                if kind == "cq":
                                    raise AssertionError("cq handled by the pipelined branch")
                                for t in range(nt):
                                    so = no % 4
                                    no += 1
                                    pi = npz % 3
                                    npz += 1
                                    pa = P2[pi]
                                    row0 = (cols[t] - {"silu": cols[0], "sig": cols[0], "rope": cols[0]}[kind])
                                    if kind == "rope":
                                        row0 += 512 * (extra // 4)
                                    drow = slice(row0, row0 + 128)
                                    Rdst = [R_d[{id(SZM_d): "SZM", id(SZD_d): "SZD", id(QD_d): "QD", id(KD_d): "KD",
                                                 id(GM_d): "GM", id(GD_d): "GD"}[id(dst)]]]
                                    if kind == "rope":
                                        k.mms([(pa[:, 0:512], [(wt[:, ws, t, kk, :], rhs[kk]) for kk in range(8)]),
                                               (pa[:, 512:1024], [(wt[:, ws, nt + t, kk, :], rhs[kk]) for kk in range(8)])],
                                              [Rwt[ws]] + RhT[c], RP2[pi])
                                        s = c % 2
                                        raise AssertionError("rope handled by the pipelined branch")
                                    else:
                                        k.mm(pa[:, 0:512], [(wt[:, ws, t, kk, :], rhs[kk]) for kk in range(8)], [Rwt[ws]] + RhT[c], [RP2[pi][0]])
                                        if kind == "silu":
                                            k.act(ost[:, so, :], pa[:, 0:512], AF.Silu, [RP2[pi][0]], [Rost[so]])
                                        else:
                                            k.act(ost[:, so, :], pa[:, 0:512], AF.Sigmoid, [RP2[pi][0], R_c], [Rost[so]],
                                                  bias=vec[:, extra + t:extra + t + 1])
                                    k.dma("pool", dst[drow, dsl], ost[:, so, :], [Rost[so]], Rdst)
                        k.flush()

        if "M0" in run:
            with ExitStack() as ph:
                kvn = sb(ph, "kvn", [128, S], BF16)
                cqn = sb(ph, "cqn", [128, 2, NOWN], BF16)
                wks = sb(ph, "wks", [128, 1024], F32)
                wkb = sb(ph, "wkb", [128, 1024], BF16)
                wqs = sb(ph, "wqs", [128, 2, 1536], F32)
                wqb = sb(ph, "wqb", [128, 2, 1536], BF16)
                tabq = sb(ph, "tabq", [96, 2, 2, 512], F32)
                t1 = sb(ph, "t1q", [96, 2, 2, 512], F32)
                ost = sb(ph, "ostm", [128, 4, 512], BF16)
                Rkvn, Rcqn, Rwk, Rwq = Reg(), Reg(), Reg(), Reg()
                Rtabq = [Reg(), Reg()]
                Rt1 = [[Reg(), Reg()], [Reg(), Reg()]]
                Rost = [Reg() for _ in range(4)]
                k.dma("sp", kvn[:], KVN_d[:, :], [R_d["KVN"]], [Rkvn])
                k.dma("sp", cqn[:, 0, :], CQN_d[0:128, :], [R_d["CQN"]], [Rcqn])
                k.dma("sp", cqn[:, 1, :], CQN_d[128:256, :], [R_d["CQN"]], [Rcqn])
                k.dma("sp", wks[:], wukv_d[:, :], (), [Rwk])
                k.act(wkb[:], wks[:], AF.Copy, [Rwk], [Rwk])
                k.dma("sp", wqs[:], wuq_d.rearrange("(k p) n -> p k n", p=128), (), [Rwq])
                k.copy("dve", wqb[:], wqs[:], [Rwq], [Rwq])
                for s in range(2):
                    k.memset("pool", tabq[0:64, s, 0, :], 1.0, [Rtabq[s]])
                    k.memset("pool", tabq[0:64, s, 1, :], 0.0, [Rtabq[s]])
                no = 0
                npz = 0
                for hp in range(4):
                    for c in range(16):
                        sl = slice(c * 512, (c + 1) * 512)
                        so = no % 4
                        no += 1
                        pi = npz % 3
                        npz += 1
                        k.mm(P2[pi][:, 0:512], [(wkb[:, hp * 128:(hp + 1) * 128], kvn[:, sl])], [Rwk, Rkvn], [RP2[pi][0]])
                        if c % 2 == 0:
                            k.act(ost[:, so, :], P2[pi][:, 0:512], AF.Copy, [RP2[pi][0]], [Rost[so]])
                        else:
                            k.copy("dve", ost[:, so, :], P2[pi][:, 0:512], [RP2[pi][0]], [Rost[so]])
                        k.dma("sp", KM_d[hp * 128:(hp + 1) * 128, sl], ost[:, so, :], [Rost[so]], [R_d["KM"]])
                for t in range(64):
                    so = no % 4
                    no += 1
                    pi = npz % 3
                    npz += 1
                    k.mm(P2[pi][:, 0:512], [(kvn[:, t * 128:(t + 1) * 128], wkb[:, 512:1024])], [Rwk, Rkvn], [RP2[pi][0]])
                    if t % 2 == 0:
                        k.act(ost[:, so, :], P2[pi][:, 0:512], AF.Copy, [RP2[pi][0]], [Rost[so]])
                    else:
                        k.copy("dve", ost[:, so, :], P2[pi][:, 0:512], [RP2[pi][0]], [Rost[so]])
                    k.dma("sp", VM_d[t * 128:(t + 1) * 128, :], ost[:, so, :], [Rost[so]], [R_d["VM"]])
                for c in range(8):
                    sl = slice(c * 512, (c + 1) * 512)
                    gsl = slice(OWN0 + c * 512, OWN0 + (c + 1) * 512)
                    s = c % 2
                    k.dma("sp", tabq[64:96, s, 0, :], TAB[2, 0:32, gsl], [R_d["TAB"]], [Rtabq[s]])
                    k.dma("sp", tabq[64:96, s, 1, :], TAB[3, 0:32, gsl], [R_d["TAB"]], [Rtabq[s]])
                    for h in range(8):
                        so = no % 4
                        no += 1
                        pi = npz % 3
                        npz += 1
                        pa = P2[pi]
                        k.mms([(pa[0:96, 0:512], [(wqb[:, kk, h * 96:(h + 1) * 96], cqn[:, kk, sl]) for kk in range(2)]),
                               (pa[0:96, 512:1024], [(wqb[:, kk, 768 + h * 96:768 + (h + 1) * 96], cqn[:, kk, sl]) for kk in range(2)])],
                              [Rwq, Rcqn], RP2[pi])
                        b_ = h % 2
                        k.tt("dve", t1[:, b_, 0, :], pa[0:96, 0:512], tabq[:, s, 0, :], ALU.mult, [RP2[pi][0], Rtabq[s]], [Rt1[b_][0]])
                        k.tt("dve", t1[:, b_, 1, :], pa[0:96, 512:1024], tabq[:, s, 1, :], ALU.mult, [RP2[pi][1], Rtabq[s]], [Rt1[b_][1]])
                        k.tt("pool", ost[0:96, so, :], t1[:, b_, 0, :], t1[:, b_, 1, :], ALU.add, Rt1[b_], [Rost[so]])
                        k.dma("sp", QM_d[h * 96:(h + 1) * 96, sl], ost[0:96, so, :], [Rost[so]], [R_d["QM"]])
                k.flush()

        if "MLA" in run:
            with ExitStack() as ph:
                Kt = sb(ph, "Kt", [96, 2, S], BF16)
                Va = sb(ph, "Va", [128, 2, 64, 65], BF16)
                qt = sb(ph, "qt", [96, 2, NOWN], BF16)
                szt = sb(ph, "szt", [64, 2, NOWN], BF16)
                Pt = sb(ph, "Pt", [128, 3, 1024], BF16)
                rrow = sb(ph, "rrow", [65, 2, 512], F32)
                bcs = sb(ph, "bcs", [64, 2, 512], F32)
                tn = sb(ph, "tn", [64, 512], F32)
                ost = sb(ph, "osta", [64, 2, 512], BF16)
                RK = [Reg(), Reg()]
                RV = [Reg(), Reg()]
                Rq = [Reg(), Reg()]
                Rsz = [Reg(), Reg()]
                RP = [Reg() for _ in range(3)]
                Rrrow, Rbcs, Rtn = [Reg(), Reg()], [Reg(), Reg()], Reg()
                Rrrd = [Reg(), Reg()]
                Rost = [Reg(), Reg()]
                for s in range(2):
                    k.memset("pool", Va[:, s, :, 64:65], 1.0, [RV[s]])
                VMv = VM_d.rearrange("(t p) c -> p t c", p=128)
                sc = 96.0 ** -0.5

                def mla_loads(h):
                    s = h % 2
                    k.dma("sp", Kt[0:64, s, :], KM_d[h * 64:(h + 1) * 64, :], [R_d["KM"]], [RK[s]])
                    k.dma("sp", Kt[64:96, s, :], KPE_d[:, :], [R_d["KPE"]], [RK[s]])
                    k.dma("sp", qt[:, s, :], QM_d[h * 96:(h + 1) * 96, :], [R_d["QM"]], [Rq[s]])
                    k.dma("sp", Va[:, s, :, 0:64], VMv[:, :, h * 64:(h + 1) * 64], [R_d["VM"]], [RV[s]])
                    k.dma("sp", szt[:, s, :], SZM_d[h * 64:(h + 1) * 64, :], [R_d["SZM"]], [Rsz[s]])

                steps = [(h, qc, kp) for h in range(8) for qc in range(8) for kp in range(32)]

                def mla_qk(i):
                    h, qc, kp = steps[i]
                    s, ss = h % 2, i % 3
                    qsl = slice(qc * 512, (qc + 1) * 512)
                    ps = P2[ss]
                    k.mms([(ps[:, 0:512], [(Kt[:, s, (2 * kp) * 128:(2 * kp + 1) * 128], qt[:, s, qsl])]),
                           (ps[:, 512:1024], [(Kt[:, s, (2 * kp + 1) * 128:(2 * kp + 2) * 128], qt[:, s, qsl])])],
                          [RK[s], Rq[s]], RP2[ss])

                def mla_exp_pv(i):
                    h, qc, kp = steps[i]
                    s, ss, sp_ = h % 2, i % 3, i % 3
                    acc = P1[qc % 2]
                    k.act(Pt[:, sp_, :], P2[ss][:, :], AF.Exp, RP2[ss], [RP[sp_]], scale=sc)

                    def pv(e):
                        e.matmul(acc[0:65, :], lhsT=Va[:, s, 2 * kp, :], rhs=Pt[:, sp_, 0:512], start=(kp == 0), stop=False)
                        return e.matmul(acc[0:65, :], lhsT=Va[:, s, 2 * kp + 1, :], rhs=Pt[:, sp_, 512:1024], start=False, stop=(kp == 31))
                    k.op("pe", pv, [RV[s], RP[sp_]], [RP1[qc % 2]])

                def mla_norm(h, qc):
                    s = h % 2
                    qsl = slice(qc * 512, (qc + 1) * 512)
                    acc = P1[qc % 2]
                    Racc = RP1[qc % 2]
                    so = (h * 8 + qc) % 2
                    k.recip(rrow[64:65, so, :], acc[64:65, :], [Racc], [Rrrow[so]])
                    k.dma("pool", RR_d[so:so + 1, :], rrow[64:65, so, :], [Rrrow[so]], [Rrrd[so]])
                    k.dma("pool", bcs[:, so, :], RR_d[so:so + 1, :].partition_broadcast(64), [Rrrd[so]], [Rbcs[so]])
                    k.tt("dve", tn[:], acc[0:64, :], bcs[:, so, :], ALU.mult, [Racc, Rbcs[so]], [Rtn])
                    k.tt("pool", ost[:, so, :], tn[:], szt[:, s, qsl], ALU.mult, [Rtn, Rsz[s]], [Rost[so]])
                    k.dma("pool", GMT_d[h * 64:(h + 1) * 64, qsl], ost[:, so, :], [Rost[so]], [R_d["GMT"]])

                mla_loads(0)
                pending = None
                for i, (h, qc, kp) in enumerate(steps):
                    if i == 0:
                        mla_qk(0)
                        mla_qk(1)
                    if i + 2 < len(steps):
                        mla_qk(i + 2)
                    mla_exp_pv(i)
                    if kp == 31:
                        pending = (h, qc)
                    elif kp == 1 and pending is not None:
                        mla_norm(*pending)
                        pending = None
                    if qc == 0 and kp == 2 and h + 1 < 8:
                        mla_loads(h + 1)
                mla_norm(*pending)
                k.flush()

        if "DIL" in run:
            with ExitStack() as ph:
                qd = sb(ph, "qd", [64, 2, NOWN], BF16)
                kd = sb(ph, "kd", [64, 2, NLOC], BF16)
                Vd = sb(ph, "Vd", [128, 2, 48, 65], BF16)
                Vdd = sb(ph, "Vdd", [128, 2, 48, 64], BF16)
                RVdd = [Reg(), Reg()]
                mask = sb(ph, "mask", [128, 4, 1024], BF16)
                Pd = sb(ph, "Pd", [128, 2, 1024], BF16)
                Pm = sb(ph, "Pm", [128, 2, 1024], BF16)
                nd = sb(ph, "nd", [65, 2, NOWN], F32)
                szd = sb(ph, "szd", [64, 2, NOWN], BF16)
                dsp = sb(ph, "dsp", [64, 64], F32)
                bcf = sb(ph, "bcf", [64, NOWN], F32)
                tn = sb(ph, "tnd", [64, 2, 512], F32)
                ost = sb(ph, "ostd", [64, 2, 512], BF16)
                Rqd, Rkd, RVd = [Reg(), Reg()], [Reg(), Reg()], [Reg(), Reg()]
                Rmask = Reg()
                RPd, RPm = [Reg(), Reg()], [Reg(), Reg()]
                Rnd = [Reg(), Reg()]
                Rszd = [Reg(), Reg()]
                Rdsp, Rbcf, Rtn = Reg(), Reg(), [Reg(), Reg()]
                Rrd1, Rrd2 = Reg(), Reg()
                Rost = [Reg(), Reg()]
                RPmh = [[Reg(), Reg()], [Reg(), Reg()]]
                k.dma("sp", mask[:], mask_d.rearrange("p (m c) -> p m c", m=4), (), [Rmask])
                for s in range(2):
                    k.memset("pool", Vd[:, s, :, 64:65], 1.0, [RVd[s]])
                units = [(hg, g) for hg in range(8) for g in range(3)]
                dsteps = [(u, ci) for u in range(len(units)) for ci in range(8)]

                def geom(g):
                    r = DIL[g]
                    nbr = NOWN // r // 128
                    return r, nbr, nbr + 1, OWN0 // r - 64

                def dil_loads(u):
                    hg, g = units[u]
                    s = u % 2
                    r, nbr, ntr, li0 = geom(g)
                    hd = g * 8 + hg
                    if g == 0:
                        k.dma("sp", szd[:, hg % 2, :], SZD_d[hg * 64:(hg + 1) * 64, :], [R_d["SZD"]], [Rszd[hg % 2]])
                    k.dma("sp", qd[:, s, :], QD_d[hd * 64:(hd + 1) * 64, :], [R_d["QD"]], [Rqd[s]])
                    if g < 2:
                        k.dma("sp", kd[:, s, 512:5632], KD_d[hd * 64:(hd + 1) * 64, 512:5632], [R_d["KD"]], [Rkd[s]])
                    else:
                        k.dma("sp", kd[:, s, :], KD_d[hd * 64:(hd + 1) * 64, :], [R_d["KD"]], [Rkd[s]])
                    nt_ = r * ntr
                    k.dma("sp", Vdd[:, s, 0:nt_, :], VD_d[hd].rearrange("p (t d) -> p t d", d=64)[:, 0:nt_, :], [R_d["VD"]], [RVdd[s]])

                def dil_pad(u):
                    hg, g = units[u]
                    s = u % 2
                    r, nbr, ntr, li0 = geom(g)
                    nt_ = r * ntr
                    k.act(Vd[:, s, 0:nt_, 0:64], Vdd[:, s, 0:nt_, :], AF.Copy, [RVdd[s]], [RVd[s]])

                def blocks_of(g, ci):
                    r, nbr, ntr, li0 = geom(g)
                    return [divmod(ci * 4 + bi, nbr) for bi in range(4)]

                def dil_qk(i):
                    u, ci = dsteps[i]
                    hg, g = units[u]
                    s, ss = u % 2, i % 3
                    r, nbr, ntr, li0 = geom(g)
                    ps = P2[ss]
                    grp = []
                    blks = blocks_of(g, ci)
                    merged = set()
                    for bi in (0, 2):
                        (r0, n0), (r1, n1) = blks[bi], blks[bi + 1]
                        if r0 == r1 and n1 == n0 + 1:
                            kl0 = (li0 + 128 * (n0 + 1)) * r + r0
                            ks = kd[:, s, ssl(kl0, 128, r)]
                            qs2 = qd[:, s, ssl((128 * n0) * r + r0, 256, r)]
                            col = (bi * 2 + 1) * 128
                            grp.append((ps[:, col:col + 256], [(ks, qs2)]))
                            merged.add((bi, 1))
                            merged.add((bi + 1, 0))
                    for bi, (res, n_) in enumerate(blks):
                        qo = (128 * n_) * r + res
                        qs = qd[:, s, ssl(qo, 128, r)]
                        for side in range(2):
                            if (bi, side) in merged:
                                continue
                            kl0 = (li0 + 128 * (n_ + side)) * r + res
                            ks = kd[:, s, ssl(kl0, 128, r)]
                            col = (bi * 2 + side) * 128
                            grp.append((ps[:, col:col + 128], [(ks, qs)]))
                    k.mms(grp, [Rqd[s], Rkd[s]], RP2[ss])

                def dil_e(i):
                    u, ci = dsteps[i]
                    hg, g = units[u]
                    ss = i % 2
                    s3_ = i % 3
                    k.act(Pd[:, ss, :], P2[s3_][:, :], AF.Exp, RP2[s3_], [RPd[ss]], scale=0.125)
                    if g == 0:
                        mk = 1 if ci == 0 else (2 if ci == 7 else 0)
                    elif g == 1:
                        mk = 1 if ci % 2 == 0 else 2
                    else:
                        mk = 3
                    k.tt("dve", Pm[:, ss, 0:512], Pd[:, ss, 0:512], mask[:, mk, 0:512], ALU.mult, [RPd[ss], Rmask], [RPmh[ss][0]])
                    k.tt("dve", Pm[:, ss, 512:1024], Pd[:, ss, 512:1024], mask[:, mk, 512:1024], ALU.mult, [RPd[ss], Rmask], [RPmh[ss][1]])

                def dil_p(i):
                    u, ci = dsteps[i]
                    hg, g = units[u]
                    s, ss, sn = u % 2, i % 2, hg % 2
                    r, nbr, ntr, li0 = geom(g)
                    blocks = blocks_of(g, ci)
                    acc = P1[ss]
                    grp = []
                    for bi, (res, n_) in enumerate(blocks):
                        pairs = []
                        for side in range(2):
                            col = (bi * 2 + side) * 128
                            pairs.append((Vd[:, s, res * ntr + n_ + side, :], Pm[:, ss, col:col + 128]))
                        grp.append((acc[0:65, bi * 128:(bi + 1) * 128], pairs))
                    k.mms(grp, [RVd[s]] + RPmh[ss], [RP1[ss]])
                    runs = []
                    if g < 2:
                        res, n0 = blocks[0]
                        o0 = (128 * n0) * r + res
                        runs.append((ssl(o0, 512, r), slice(0, 512)))
                    else:
                        for j in range(2):
                            res, n0 = blocks[2 * j]
                            runs.append((ssl(res, 256, r), slice(j * 256, (j + 1) * 256)))
                    for osl, asl in runs:
                        if g == 0:
                            k.copy("dve", nd[:, sn, osl], acc[0:65, asl], [RP1[ss]], [Rnd[sn]])
                        else:
                            k.tt("dve", nd[:, sn, osl], nd[:, sn, osl], acc[0:65, asl], ALU.add, [RP1[ss], Rnd[sn]], [Rnd[sn]])

                def dil_final_a(hg):
                    sn = hg % 2
                    k.dma("sp", RD1_d[0:1, :], nd[64:65, sn, :], [Rnd[sn]], [Rrd1])
                    k.dma("sp", dsp[:], RD1_d.rearrange("o (p f) -> (o p) f", p=64), [Rrd1], [Rdsp])
                    k.recip(dsp[:], dsp[:], [Rdsp], [Rdsp])
                    k.dma("sp", RD2_d.rearrange("o (p f) -> (o p) f", p=64), dsp[:], [Rdsp], [Rrd2])
                    k.dma("sp", bcf[:], RD2_d[0:1, :].partition_broadcast(64), [Rrd2], [Rbcf])

                def dil_final_b(hg, qc):
                    sn = hg % 2
                    qsl = slice(qc * 512, (qc + 1) * 512)
                    so = qc % 2
                    k.tt("dve", tn[:, so, :], nd[0:64, sn, qsl], bcf[:, qsl], ALU.mult, [Rnd[sn], Rbcf], [Rtn[so]])
                    k.tt("pool", ost[:, so, :], tn[:, so, :], szd[:, sn, qsl], ALU.mult, [Rtn[so], Rszd[sn]], [Rost[so]])
                    k.dma("pool", GDT_d[hg * 64:(hg + 1) * 64, qsl], ost[:, so, :], [Rost[so]], [R_d["GDT"]])

                dil_loads(0)
                dil_pad(0)
                pending = None
                fin_slots = {(0, 6): 0, (0, 7): 1, (1, 1): 2, (1, 2): 3, (1, 3): 4, (1, 5): 5, (1, 6): 6, (1, 7): 7}
                for i, (u, ci) in enumerate(dsteps):
                    hg, g = units[u]
                    if ci == 0 and u + 1 < len(units):
                        dil_loads(u + 1)
                    if ci == 4 and u + 1 < len(units):
                        dil_pad(u + 1)
                    if i == 0:
                        dil_qk(0)
                        dil_qk(1)
                        dil_e(0)
                    if i + 2 < len(dsteps):
                        dil_qk(i + 2)
                    if i + 1 < len(dsteps):
                        dil_e(i + 1)
                    dil_p(i)
                    if g == 2 and ci == 7:
                        pending = hg
                    elif g == 0 and ci == 1 and pending is not None:
                        dil_final_a(pending)
                    elif pending is not None and (g, ci) in fin_slots:
                        dil_final_b(pending, fin_slots[(g, ci)])
                        if fin_slots[(g, ci)] == 7:
                            pending = None
                dil_final_a(pending)
                for qc in range(8):
                    dil_final_b(pending, qc)
                k.flush()

        if "F" in run:
            with ExitStack() as ph:
                wstg = sb(ph, "wstg", [128, 8, D], F32)
                wom = sb(ph, "wom", [128, 4, D], BF16)
                wod = sb(ph, "wod", [128, 4, D], BF16)
                wout = sb(ph, "wout", [128, 8, D], BF16)
                fg = sb(ph, "fg", [128, D], F32)
                gmt = sb(ph, "gmt", [128, 2, 4, 512], BF16)
                gdt = sb(ph, "gdt", [128, 2, 4, 512], BF16)
                gm = sb(ph, "gm", [128, 2, 8, 512], BF16)
                gd = sb(ph, "gd", [128, 2, 8, 512], BF16)
                m1 = sb(ph, "m1", [128, 2, 512], F32)
                m2 = sb(ph, "m2", [128, 2, 512], F32)
                mg = sb(ph, "mg", [128, 2, 8, 512], BF16)
                xt = sb(ph, "xtf", [128, 2, D], F32)
                rr = sb(ph, "rr", [128, 2, D], F32)
                junk = sb(ph, "junkf", [128, D], BF16)
                st = sb(ph, "stf", [128, 2, 4], F32)
                ot = sb(ph, "ot", [128, 2, D], F32)
                Rwstg, Rwom, Rwod, Rwout, Rfg = Reg(), Reg(), Reg(), Reg(), Reg()
                Rwout2 = Reg()
                Rgmt, Rgdt, Rgm, Rgd = [Reg(), Reg()], [Reg(), Reg()], [Reg(), Reg()], [Reg(), Reg()]
                Rmgt = [[Reg() for _ in range(8)] for _ in range(2)]
                Rm1, Rm2, Rjunk = [Reg(), Reg()], [Reg(), Reg()], Reg()
                Rxt, Rrr, Rst, Rot = [Reg(), Reg()], [Reg(), Reg()], [Reg(), Reg()], [Reg(), Reg()]
                wstg2 = sb(ph, "wstg2", [128, 8, D], F32)
                Rwa, Rwb, Rw2 = Reg(), Reg(), Reg()
                k.dma("sp", wstg[:, 0:4, :], wom_d.rearrange("(h p) n -> p h n", p=128), (), [Rwa])
                k.dma("sp", wstg[:, 4:8, :], wod_d.rearrange("(h p) n -> p h n", p=128), (), [Rwb])
                k.dma("sp", wstg2[:], wout_d.rearrange("(k p) n -> p k n", p=128), (), [Rw2])
                k.act(wom[:], wstg[:, 0:4, :], AF.Copy, [Rwa], [Rwom])
                k.copy("dve", wod[:], wstg[:, 4:8, :], [Rwb], [Rwod])
                k.act(wout[:, 0:4, :], wstg2[:, 0:4, :], AF.Copy, [Rw2], [Rwout])
                k.copy("dve", wout[:, 4:8, :], wstg2[:, 4:8, :], [Rw2], [Rwout2])
                k.dma("sp", fg[:], fg_d[:, :], (), [Rfg])
                npz = 0
                ntile = 0
                def f_loads(qc):
                    s = qc % 2
                    qsl = slice(qc * 512, (qc + 1) * 512)
                    k.dma("sp", gmt[:, s], GMT_d[:, qsl].rearrange("(h p) t -> p h t", p=128), [R_d["GMT"]], [Rgmt[s]])
                    k.dma("sp", gdt[:, s], GDT_d[:, qsl].rearrange("(h p) t -> p h t", p=128), [R_d["GDT"]], [Rgdt[s]])
                    k.dma("sp", gm[:, s], GM_d[:, qsl].rearrange("(h p) t -> p h t", p=128), [R_d["GM"]], [Rgm[s]])
                    k.dma("sp", gd[:, s], GD_d[:, qsl].rearrange("(h p) t -> p h t", p=128), [R_d["GD"]], [Rgd[s]])

                def f_first(qc):
                    s = qc % 2
                    for dt_ in range(8):
                        pi = fcnt[0] % 2
                        fcnt[0] += 1
                        pa = P2[pi]
                        dsl = slice(dt_ * 128, (dt_ + 1) * 128)
                        k.mms([(pa[:, 0:512], [(wom[:, h, dsl], gmt[:, s, h, :]) for h in range(4)]),
                               (pa[:, 512:1024], [(wod[:, h, dsl], gdt[:, s, h, :]) for h in range(4)])],
                              [Rwom, Rwod, Rgmt[s], Rgdt[s]], RP2[pi])
                        b_ = dt_ % 2
                        k.tt("dve", m1[:, b_, :], pa[:, 0:512], gm[:, s, dt_, :], ALU.mult, [RP2[pi][0], Rgm[s]], [Rm1[b_]])
                        k.tt("dve", m2[:, b_, :], pa[:, 512:1024], gd[:, s, dt_, :], ALU.mult, [RP2[pi][1], Rgd[s]], [Rm2[b_]])
                        k.tt("dve", mg[:, s, dt_, :], m1[:, b_, :], m2[:, b_, :], ALU.add, [Rm1[b_], Rm2[b_]], [Rmgt[s][dt_]])

                def f_out(qc):
                    s = qc % 2
                    for tt4 in range(4):
                        i = qc * 4 + tt4
                        s2 = i % 2
                        if s2 == 0:
                            halves = [(P2[2][:, 0:512], RP2[2][0]), (P2[2][:, 512:1024], RP2[2][1])]
                        else:
                            halves = [(P1[0][:], RP1[0]), (P1[1][:], RP1[1])]
                        tsl = slice(tt4 * 128, (tt4 + 1) * 128)
                        k.mms([(halves[0][0], [(mg[:, s, kk, tsl], wout[:, kk, 0:512]) for kk in range(8)]),
                               (halves[1][0], [(mg[:, s, kk, tsl], wout[:, kk, 512:1024]) for kk in range(8)])],
                              Rmgt[s] + [Rwout, Rwout2], [halves[0][1], halves[1][1]])
                        k.dma("sp", xt[:, s2, :], x_d[OWN0 + i * 128:OWN0 + (i + 1) * 128, :], (), [Rxt[s2]])
                        for hh in range(2):
                            k.tt("dve", rr[:, s2, hh * 512:(hh + 1) * 512], halves[hh][0], xt[:, s2, hh * 512:(hh + 1) * 512], ALU.add,
                                 [halves[hh][1], Rxt[s2]], [Rrr[s2]])
                        k.act(junk[:], rr[:, s2, :], AF.Square, [Rrr[s2]], [Rjunk, Rst[s2]], accum=st[:, s2, 0:1])
                        k.act(st[:, s2, 1:2], st[:, s2, 0:1], AF.Ln, [Rst[s2], R_c], [Rst[s2]], scale=1.0 / D, bias=vec[:, 33:34])
                        k.act(st[:, s2, 3:4], st[:, s2, 1:2], AF.Exp, [Rst[s2]], [Rst[s2]], scale=-0.5)
                        k.act(rr[:, s2, :], rr[:, s2, :], AF.Copy, [Rrr[s2], Rst[s2]], [Rrr[s2]], scale=st[:, s2, 3:4])
                        k.tt("pool", ot[:, s2, :], rr[:, s2, :], fg[:], ALU.mult, [Rrr[s2], Rfg], [Rot[s2]])
                        k.dma("pool", out_d[i * 128:(i + 1) * 128, :], ot[:, s2, :], [Rot[s2]], [R_d["out"]])

                fcnt = [0]
                f_loads(0)
                f_loads(1)
                f_first(0)
                for qc in range(8):
                    if qc + 1 < 8:
                        f_first(qc + 1)
                    f_out(qc)
                    if qc + 2 < 8:
                        f_loads(qc + 2)
                k.flush()
    return nc


def _rot_cols(w, head_dim):
    n = w.shape[1]
    half = head_dim // 2
    idx = np.arange(n)
    d = idx % head_dim
    src = np.where(d < half, idx + half, idx - half)
    return w[:, src]


def _masks(half):
    kk = np.arange(128)[:, None]
    qq = np.arange(128)[None, :]
    lo = (kk >= qq)
    hi = (kk <= qq)
    lo_first = lo & (kk >= 64) if half == 0 else lo
    hi_last = hi & (kk < 64) if half == 1 else hi

    def tile(pattern):
        return np.concatenate(pattern, axis=1)
    plain = tile([lo, hi] * 4)
    first = tile([lo_first, hi] + [lo, hi] * 3)
    last = tile([lo, hi] * 3 + [lo, hi_last])
    g3 = tile([lo_first, hi, lo, hi_last] * 2)
    m = np.concatenate([plain, first, last, g3], axis=1).astype(np.float32)
    return m.astype(ml_dtypes.bfloat16)


def prepare_inputs(x, positions, attn_norm_g, w_in, b_gate, mla_q_norm_g, mla_kv_norm_g,
                   w_uq, w_ukv, w_o_mla, w_o_dil, w_out, final_norm_g):
    f32 = np.float32
    x = np.asarray(x, f32)
    positions = np.asarray(positions, np.int32)
    w_in0 = np.asarray(w_in, f32)[0]
    w_in_ext = np.concatenate([w_in0, _rot_cols(w_in0[:, C_KR:C_KR + 32], 32)], axis=1)
    assert w_in_ext.shape[1] == WIN_EXT
    wuq0 = np.asarray(w_uq, f32)[0].reshape(256, 8, 96)
    rot = np.zeros_like(wuq0)
    rot[:, :, 64:96] = _rot_cols(wuq0[:, :, 64:96].reshape(256, 8 * 32), 32).reshape(256, 8, 32)
    wuq_ext = np.concatenate([wuq0.reshape(256, 768), rot.reshape(256, 768)], axis=1)
    wukv0 = np.asarray(w_ukv, f32)[0].reshape(128, 8, 128)
    wukv_ext = np.concatenate([wukv0[:, :, :64].reshape(128, 512), wukv0[:, :, 64:].reshape(128, 512)], axis=1)
    p = np.arange(128)
    vec = np.zeros((128, 40), f32)
    vec[:, 0:8] = np.asarray(attn_norm_g, f32)[0].reshape(8, 128).T
    vec[:, 8:24] = np.asarray(b_gate, f32)[0].reshape(16, 128).T
    vec[:, 24:26] = np.asarray(mla_q_norm_g, f32)[0].reshape(2, 128).T
    vec[:, 26] = np.asarray(mla_kv_norm_g, f32)[0]
    invD = (10000.0 ** (-(2.0 * (p % 32)) / 64.0)).astype(f32)
    invM = (10000.0 ** (-(2.0 * (p % 16)) / 32.0)).astype(f32)
    vec[:, 27] = (invD.astype(np.float64) / (2 * math.pi)).astype(f32)
    vec[:, 28] = (invM.astype(np.float64) / (2 * math.pi)).astype(f32)
    vec[:, 29] = TWO_PI_S
    vec[:, 30] = np.where((p % 64) < 32, -TWO_PI_S, TWO_PI_S)
    vec[:, 31] = np.where((p % 32) < 16, -TWO_PI_S, TWO_PI_S)
    vec[:, 33] = EPS
    pm = np.clip(p - 64, 0, 31)
    invC = np.where(p < 64, invD, np.where(p < 96, (10000.0 ** (-(2.0 * (pm % 16)) / 32.0)), 0.0))
    vec[:, 34] = (invC.astype(np.float64) / (2 * math.pi)).astype(f32)
    vec[:, 35] = np.where(p < 64, np.where(p < 32, -TWO_PI_S, TWO_PI_S), np.where((p < 96) & (pm < 16), -TWO_PI_S, TWO_PI_S))
    vec[:, 32] = TWO_PI_S / 4.0
    fg = np.ascontiguousarray(np.broadcast_to(np.asarray(final_norm_g, f32)[None, :], (128, D)))
    ident = np.eye(128, dtype=f32)
    dd = np.arange(128)
    srcp = np.where((dd % 64) < 32, dd + 32, dd - 32)
    permd = np.zeros((128, 128), f32)
    permd[srcp, dd] = 1.0
    permd = permd.astype(ml_dtypes.bfloat16)
    shared = {"w_in": np.ascontiguousarray(w_in_ext), "w_uq": np.ascontiguousarray(wuq_ext),
              "w_ukv": np.ascontiguousarray(wukv_ext), "w_o_mla": np.asarray(w_o_mla, f32)[0],
              "w_o_dil": np.asarray(w_o_dil, f32)[0], "w_out": np.asarray(w_out, f32)[0],
              "vecs": vec, "fg": fg, "ident": ident, "permd": permd}
    in_maps = []
    for c in range(8):
        b, half = divmod(c, 2)
        shift = (half * NOWN - OWN0) % S
        m = dict(shared)
        m["x"] = np.ascontiguousarray(np.roll(x[b], -shift, axis=0))
        m["pos"] = np.ascontiguousarray(np.roll(positions[b], -shift)[None, :])
        m["masks"] = _masks(half)
        in_maps.append(m)
    return in_maps


_NC_CACHE = {}


def kernel(x, positions, attn_norm_g, w_in, b_gate, mla_q_norm_g, mla_kv_norm_g,
           w_uq, w_ukv, w_o_mla, w_o_dil, w_out, final_norm_g):
    in_maps = prepare_inputs(x, positions, attn_norm_g, w_in, b_gate, mla_q_norm_g, mla_kv_norm_g,
                             w_uq, w_ukv, w_o_mla, w_o_dil, w_out, final_norm_g)
    nc = build()
    res = run_bass_kernel_spmd(nc, in_maps, core_ids=list(range(8)))
    out = np.zeros((4, S, D), np.float32)
    for c in range(8):
        b, half = divmod(c, 2)
        out[b, half * NOWN:(half + 1) * NOWN] = np.asarray(res.results[c]["out"], np.float32)
    return out
```
